# Optimizing a Trainium2 kernel written in Bass

```python
import math
import jax, jax.numpy as jnp
from jax import lax
import numpy as np

D_MODEL = 2048
BATCH = 4
SEQ = 4096
DEPTH = 2
DEC_BATCH = 32
DEC_SEQ = 32
PAST_LEN = 2048

CHUNK = 64
Q_BLOCK = 128
ROPE_THETA = 10000.0
EPS = 1e-6
A_HEADS = 4
A_HD = 128
A_VD = 2 * A_HD
B_HEADS = 8
B_KV = 2
B_HD = 128
I_HEADS = 8
I_HD = 64
TOPK_MAX = 256
C_HEADS = 8
C_HD = 128
C_CONV = 4
C_QKV = 3 * C_HEADS * C_HD
BRANCH_W = 1024
N_BRANCH = 3
D_FF = 5632
FFN_CONV = 3
IN_SPLITS = (A_HEADS * 2 * A_HD, A_HEADS * 2 * A_HD, A_HEADS * A_VD,
             B_HEADS * B_HD, B_KV * B_HD, B_KV * B_HD, I_HEADS * I_HD, I_HD, I_HEADS,
             C_QKV, C_HEADS, C_HEADS, C_HEADS * C_HD,
             N_BRANCH * D_MODEL)
D_IN = sum(IN_SPLITS)

kernel_name = 'hybrid_streaming_encoder_step'


def _rmsnorm(x, g):
    xf = x.astype(jnp.float32)
    y = xf * lax.rsqrt(jnp.mean(xf * xf, axis=-1, keepdims=True) + EPS)
    return (y * g.astype(jnp.float32)).astype(x.dtype)


def _l2norm(x):
    xf = x.astype(jnp.float32)
    return xf * lax.rsqrt(jnp.sum(xf * xf, axis=-1, keepdims=True) + EPS)


def _modulate(h, shift, scale):
    return h * (1.0 + scale[:, None, :]) + shift[:, None, :]


def _rope(x, pos):
    d = x.shape[-1]
    half = d // 2
    inv = jnp.power(ROPE_THETA, -jnp.arange(half, dtype=jnp.float32) / half)
    ang = pos.astype(jnp.float32)[:, None] * inv[None, :]
    shp = (pos.shape[0],) + (1,) * (x.ndim - 3) + (half,)
    cos = jnp.cos(ang).reshape(shp)
    sin = jnp.sin(ang).reshape(shp)
    xf = x.astype(jnp.float32)
    x1, x2 = xf[..., :half], xf[..., half:]
    return jnp.concatenate([x1 * cos - x2 * sin, x2 * cos + x1 * sin], axis=-1).astype(x.dtype)


def _chunk_visible(q_pos, k_pos):
    return (k_pos[None, :] // CHUNK) <= (q_pos[:, None] // CHUNK)


def _causal_dwconv(x, buf, w):
    W = w.shape[0]
    T = x.shape[1]
    xp = jnp.concatenate([buf.astype(x.dtype), x], axis=1)
    y = xp[:, 0:T] * w[0]
    for j in range(1, W):
        y = y + xp[:, j:j + T] * w[j]
    return y, xp[:, T:]


def _sweep(fn, q_arrays, q_pos):
    T = q_pos.shape[0]
    qb = min(Q_BLOCK, T)
    nb = T // qb
    blocks = tuple(jnp.moveaxis(a.reshape((a.shape[0], nb, qb) + a.shape[2:]), 1, 0) for a in q_arrays)
    out = lax.map(lambda args: fn(*args), blocks + (q_pos.reshape(nb, qb),))
    out = jnp.moveaxis(out, 0, 1)
    return out.reshape((out.shape[0], T) + out.shape[3:])


def _gdn_chunk(S, inp):
    q, k, v, g, beta = inp
    C = q.shape[2]
    G = jnp.cumsum(g, axis=-1)
    i = jnp.arange(C)
    incl = i[:, None] >= i[None, :]
    strict = i[:, None] > i[None, :]
    gam = jnp.exp(jnp.where(incl, G[..., :, None] - G[..., None, :], -jnp.inf))
    a = jnp.where(strict, beta[..., :, None] * jnp.einsum('bhik,bhjk->bhij', k, k) * gam, 0.0)
    m = a + jnp.eye(C, dtype=a.dtype)
    rhs = jnp.concatenate([beta[..., None] * v, (beta * jnp.exp(G))[..., None] * k], axis=-1)
    sol = lax.linalg.triangular_solve(m, rhs, left_side=True, lower=True, unit_diagonal=True)
    dv = v.shape[-1]
    u, w = sol[..., :dv], sol[..., dv:]
    v_new = u - jnp.einsum('bhck,bhkv->bhcv', w, S)
    qk = jnp.einsum('bhik,bhjk->bhij', q, k) * gam
    o = (jnp.einsum('bhck,bhkv->bhcv', q * jnp.exp(G)[..., None], S)
         + jnp.einsum('bhij,bhjv->bhiv', qk, v_new))
    gl = G[..., -1]
    S_new = (jnp.exp(gl)[..., None, None] * S
             + jnp.einsum('bhck,bhcv->bhkv', k * jnp.exp(gl[..., None] - G)[..., None], v_new))
    return S_new, o


def _layer(x, c, past, lp, layer_idx):
    kA_p, vA_p, kB_p, vB_p, kI_p, cbuf, S0, fbuf = past
    B, T, D = x.shape
    P = kA_p.shape[1]
    L = P + T
    q_pos = P + jnp.arange(T, dtype=jnp.int32)
    k_pos = jnp.arange(L, dtype=jnp.int32)
    f32 = jnp.float32

    mod = jnp.einsum('bd,de->be', jax.nn.silu(c), lp['w_ada']) + lp['b_ada']
    sh1, sc1, g1, sh2, sc2, g2 = jnp.split(mod, 6, axis=-1)

    h = _modulate(_rmsnorm(x, lp['norm_mix']), sh1, sc1)
    proj = jnp.einsum('btd,de->bte', h, lp['w_in'])
    split_pts = np.cumsum(IN_SPLITS)[:-1].tolist()
    (aq, ak, av, bq, bk, bv, bqi, bki, bwi, cqkv, ca, cb, cz, gt) = jnp.split(proj, split_pts, axis=-1)

    qa = _rope(_rmsnorm(aq.reshape(B, T, A_HEADS, 2, A_HD), lp['a_q_norm']), q_pos)
    ka = _rope(_rmsnorm(ak.reshape(B, T, A_HEADS, 2, A_HD), lp['a_k_norm']), q_pos)
    va = av.reshape(B, T, A_HEADS, A_VD)
    ka_all = jnp.concatenate([kA_p, ka], axis=1)
    va_all = jnp.concatenate([vA_p, va], axis=1)
    lam_init = 0.8 - 0.6 * math.exp(-0.3 * layer_idx)
    lam_p = lp['a_lambda'].astype(f32)
    lam = jnp.exp(jnp.sum(lam_p[0] * lam_p[1])) - jnp.exp(jnp.sum(lam_p[2] * lam_p[3])) + lam_init

    def diff_block(q_blk, qp_blk):
        s = jnp.einsum('bqhmd,bkhmd->bhmqk', q_blk, ka_all, preferred_element_type=f32) * (A_HD ** -0.5)
        vis = _chunk_visible(qp_blk, k_pos)
        p = jax.nn.softmax(jnp.where(vis, s, -jnp.inf), axis=-1)
        attn = p[:, :, 0] - lam * p[:, :, 1]
        return jnp.einsum('bhqk,bkhe->bqhe', attn, va_all).astype(x.dtype)

    oa = _sweep(diff_block, (qa,), q_pos)
    oa = _rmsnorm(oa, lp['a_subln']) * (1.0 - lam_init)

    qb = _rope(_rmsnorm(bq.reshape(B, T, B_HEADS, B_HD), lp['b_q_norm']), q_pos)
    kb = _rope(_rmsnorm(bk.reshape(B, T, B_KV, B_HD), lp['b_k_norm']), q_pos)
    vb = bv.reshape(B, T, B_KV, B_HD)
    qi = _rope(bqi.reshape(B, T, I_HEADS, I_HD), q_pos)
    ki = _rope(bki, q_pos)
    wi = bwi * (I_HEADS ** -0.5)
    kb_all = jnp.concatenate([kB_p, kb], axis=1)
    vb_all = jnp.concatenate([vB_p, vb], axis=1)
    ki_all = jnp.concatenate([kI_p, ki], axis=1)
    topk = min(TOPK_MAX, L // 4)
    grp = B_HEADS // B_KV

    def dsa_block(q_blk, qi_blk, w_blk, qp_blk):
        nb_, nq = q_blk.shape[0], qp_blk.shape[0]
        isc = jnp.einsum('bqhd,bkd->bqhk', qi_blk, ki_all, preferred_element_type=f32) * (I_HD ** -0.5)
        score = jnp.einsum('bqhk,bqh->bqk', jax.nn.relu(isc), w_blk.astype(f32))
        vis = _chunk_visible(qp_blk, k_pos)
        score = jnp.where(vis[None], score, -jnp.inf)
        _, sel = lax.top_k(score, topk)
        valid = (sel // CHUNK) <= (qp_blk // CHUNK)[None, :, None]
        k_sel = jax.vmap(lambda kk, ii: kk[ii])(kb_all, sel)
        v_sel = jax.vmap(lambda vv, ii: vv[ii])(vb_all, sel)
        qg = q_blk.reshape(nb_, nq, B_KV, grp, B_HD)
        s = jnp.einsum('bqngd,bqknd->bqngk', qg, k_sel, preferred_element_type=f32) * (B_HD ** -0.5)
        p = jax.nn.softmax(jnp.where(valid[:, :, None, None, :], s, -jnp.inf), axis=-1)
        o = jnp.einsum('bqngk,bqknd->bqngd', p, v_sel)
        return o.reshape(nb_, nq, B_HEADS, B_HD).astype(x.dtype)

    ob = _sweep(dsa_block, (qb, qi, wi), q_pos)

    cqkv_c, cbuf_new = _causal_dwconv(cqkv, cbuf, lp['c_conv'])
    cq, ck, cv = jnp.split(jax.nn.silu(cqkv_c), 3, axis=-1)
    cq = _l2norm(cq.reshape(B, T, C_HEADS, C_HD)) * (C_HD ** -0.5)
    ck = _l2norm(ck.reshape(B, T, C_HEADS, C_HD))
    cv = cv.reshape(B, T, C_HEADS, C_HD)
    g = -jnp.exp(lp['c_a_log'].astype(f32)) * jax.nn.softplus(ca.astype(f32) + lp['c_dt_bias'].astype(f32))
    beta = jax.nn.sigmoid(cb.astype(f32))
    cc = min(CHUNK, T)
    n = T // cc

    def to_blocks(a):
        a = a.astype(f32).reshape((B, n, cc) + a.shape[2:])
        return jnp.moveaxis(jnp.moveaxis(a, 1, 0), 2, 3)

    S_fin, oc = lax.scan(_gdn_chunk, S0.astype(f32),
                         (to_blocks(cq), to_blocks(ck), to_blocks(cv), to_blocks(g), to_blocks(beta)))
    oc = jnp.moveaxis(jnp.moveaxis(oc, 3, 2), 0, 1).reshape(B, T, C_HEADS, C_HD).astype(x.dtype)
    oc = _rmsnorm(oc, lp['c_out_norm']) * jax.nn.silu(cz.reshape(B, T, C_HEADS, C_HD))

    branches = jnp.stack([oa.reshape(B, T, BRANCH_W), ob.reshape(B, T, BRANCH_W),
                          oc.reshape(B, T, BRANCH_W)], axis=2)
    bproj = jnp.einsum('btnw,nwd->btnd', branches, lp['w_branch'])
    gates = jax.nn.sigmoid(gt.reshape(B, T, N_BRANCH, D))
    mixed = jnp.sum(gates * bproj, axis=2)
    y = jnp.einsum('btd,de->bte', mixed, lp['w_out'])
    x = x + g1[:, None, :] * y

    h2 = _modulate(_rmsnorm(x, lp['norm_ffn']), sh2, sc2)
    u = jnp.einsum('btd,df->btf', h2, lp['w_up'])
    u, fbuf_new = _causal_dwconv(u, fbuf, lp['ffn_conv'])
    ug, uu = jnp.split(u, 2, axis=-1)
    fo = jnp.einsum('btf,fd->btd', jax.nn.silu(ug) * uu, lp['w_down'])
    x = x + g2[:, None, :] * fo

    new_state = (ka, va, kb, vb, ki, cbuf_new, S_fin.astype(S0.dtype), fbuf_new)
    return x, new_state


def setup_inputs(seed: int = 0) -> dict:
    key = jax.random.key(seed)
    keys = jax.random.split(key, 40)
    counter = [0]

    def nxt():
        counter[0] += 1
        return keys[counter[0] - 1]

    def nrm(shape, scale=1.0):
        return jax.random.normal(nxt(), shape, jnp.float32) * scale

    def gain(shape):
        return 1.0 + 0.02 * nrm(shape)

    d = D_MODEL
    x_prompt = nrm((BATCH, SEQ, d))
    x_sample = nrm((DEC_BATCH, DEC_SEQ, d))
    c_prompt = nrm((BATCH, d))
    c_sample = nrm((DEC_BATCH, d))
    cache_diff_k = nrm((DEPTH, DEC_BATCH, PAST_LEN, A_HEADS, 2, A_HD))
    cache_diff_v = nrm((DEPTH, DEC_BATCH, PAST_LEN, A_HEADS, A_VD))
    cache_dsa_k = nrm((DEPTH, DEC_BATCH, PAST_LEN, B_KV, B_HD))
    cache_dsa_v = nrm((DEPTH, DEC_BATCH, PAST_LEN, B_KV, B_HD))
    cache_dsa_kidx = nrm((DEPTH, DEC_BATCH, PAST_LEN, I_HD))
    state_gdn_conv = nrm((DEPTH, DEC_BATCH, C_CONV - 1, C_QKV))
    state_gdn = nrm((DEPTH, DEC_BATCH, C_HEADS, C_HD, C_HD), 0.1)
    state_ffn_conv = nrm((DEPTH, DEC_BATCH, FFN_CONV - 1, 2 * D_FF))
    w_ada = nrm((DEPTH, d, 6 * d), 0.5 * d ** -0.5)
    b_ada = nrm((DEPTH, 6 * d), 0.01)
    norm_mix = gain((DEPTH, d))
    w_in = nrm((DEPTH, d, D_IN), d ** -0.5)
    a_q_norm = gain((DEPTH, A_HD))
    a_k_norm = gain((DEPTH, A_HD))
    a_lambda = nrm((DEPTH, 4, A_HD), 0.1)
    a_subln = gain((DEPTH, A_VD))
    b_q_norm = gain((DEPTH, B_HD))
    b_k_norm = gain((DEPTH, B_HD))
    c_conv = nrm((DEPTH, C_CONV, C_QKV), C_CONV ** -0.5)
    c_a_log = jnp.log(jax.random.uniform(nxt(), (DEPTH, C_HEADS), jnp.float32, 1.0, 16.0))
    dt = jnp.exp(jax.random.uniform(nxt(), (DEPTH, C_HEADS), jnp.float32, math.log(1e-3), math.log(1e-1)))
    c_dt_bias = dt + jnp.log(-jnp.expm1(-dt))
    c_out_norm = gain((DEPTH, C_HD))
    w_branch = nrm((DEPTH, N_BRANCH, BRANCH_W, d), BRANCH_W ** -0.5)
    w_out = nrm((DEPTH, d, d), d ** -0.5)
    norm_ffn = gain((DEPTH, d))
    w_up = nrm((DEPTH, d, 2 * D_FF), d ** -0.5)
    ffn_conv = nrm((DEPTH, FFN_CONV, 2 * D_FF), FFN_CONV ** -0.5)
    w_down = nrm((DEPTH, D_FF, d), D_FF ** -0.5)
    return {'x_prompt': x_prompt, 'x_sample': x_sample, 'c_prompt': c_prompt, 'c_sample': c_sample,
            'cache_diff_k': cache_diff_k, 'cache_diff_v': cache_diff_v, 'cache_dsa_k': cache_dsa_k,
            'cache_dsa_v': cache_dsa_v, 'cache_dsa_kidx': cache_dsa_kidx, 'state_gdn_conv': state_gdn_conv,
            'state_gdn': state_gdn, 'state_ffn_conv': state_ffn_conv,
            'w_ada': w_ada, 'b_ada': b_ada, 'norm_mix': norm_mix, 'w_in': w_in,
            'a_q_norm': a_q_norm, 'a_k_norm': a_k_norm, 'a_lambda': a_lambda, 'a_subln': a_subln,
            'b_q_norm': b_q_norm, 'b_k_norm': b_k_norm, 'c_conv': c_conv, 'c_a_log': c_a_log,
            'c_dt_bias': c_dt_bias, 'c_out_norm': c_out_norm, 'w_branch': w_branch, 'w_out': w_out,
            'norm_ffn': norm_ffn, 'w_up': w_up, 'ffn_conv': ffn_conv, 'w_down': w_down}


def reference(x_prompt, x_sample, c_prompt, c_sample, cache_diff_k, cache_diff_v, cache_dsa_k,
              cache_dsa_v, cache_dsa_kidx, state_gdn_conv, state_gdn, state_ffn_conv,
              w_ada, b_ada, norm_mix, w_in, a_q_norm, a_k_norm, a_lambda, a_subln,
              b_q_norm, b_k_norm, c_conv, c_a_log, c_dt_bias, c_out_norm, w_branch, w_out,
              norm_ffn, w_up, ffn_conv, w_down):
    dt_ = x_prompt.dtype
    bp = x_prompt.shape[0]
    prompt_past = (jnp.zeros((bp, 0, A_HEADS, 2, A_HD), dt_), jnp.zeros((bp, 0, A_HEADS, A_VD), dt_),
                   jnp.zeros((bp, 0, B_KV, B_HD), dt_), jnp.zeros((bp, 0, B_KV, B_HD), dt_),
                   jnp.zeros((bp, 0, I_HD), dt_), jnp.zeros((bp, C_CONV - 1, C_QKV), dt_),
                   jnp.zeros((bp, C_HEADS, C_HD, C_HD), dt_), jnp.zeros((bp, FFN_CONV - 1, 2 * D_FF), dt_))
    xp, xs = x_prompt, x_sample
    prompt_states, sample_states = [], []
    for l in range(DEPTH):
        lp = dict(w_ada=w_ada[l], b_ada=b_ada[l], norm_mix=norm_mix[l], w_in=w_in[l],
                  a_q_norm=a_q_norm[l], a_k_norm=a_k_norm[l], a_lambda=a_lambda[l], a_subln=a_subln[l],
                  b_q_norm=b_q_norm[l], b_k_norm=b_k_norm[l], c_conv=c_conv[l], c_a_log=c_a_log[l],
                  c_dt_bias=c_dt_bias[l], c_out_norm=c_out_norm[l], w_branch=w_branch[l], w_out=w_out[l],
                  norm_ffn=norm_ffn[l], w_up=w_up[l], ffn_conv=ffn_conv[l], w_down=w_down[l])
        xp, st_p = _layer(xp, c_prompt, prompt_past, lp, l)
        sample_past = (cache_diff_k[l], cache_diff_v[l], cache_dsa_k[l], cache_dsa_v[l], cache_dsa_kidx[l],
                       state_gdn_conv[l], state_gdn[l], state_ffn_conv[l])
        xs, st_s = _layer(xs, c_sample, sample_past, lp, l)
        prompt_states.append(st_p)
        sample_states.append(st_s)
    (p_diff_k, p_diff_v, p_dsa_k, p_dsa_v, p_dsa_kidx, p_gdn_conv, p_gdn_state,
     p_ffn_conv) = [jnp.stack(s, axis=0) for s in zip(*prompt_states)]
    (s_diff_k, s_diff_v, s_dsa_k, s_dsa_v, s_dsa_kidx, s_gdn_conv, s_gdn_state,
     s_ffn_conv) = [jnp.stack(s, axis=0) for s in zip(*sample_states)]
    return (xp, xs, p_diff_k, p_diff_v, p_dsa_k, p_dsa_v, p_dsa_kidx, p_gdn_conv, p_gdn_state, p_ffn_conv,
            s_diff_k, s_diff_v, s_dsa_k, s_dsa_v, s_dsa_kidx, s_gdn_conv, s_gdn_state, s_ffn_conv)
```

```python
import math
from contextlib import ExitStack

import numpy as np
import concourse.bass as bass
import concourse.mybir as mybir
from concourse.bass_utils import run_bass_kernel_spmd

F32 = mybir.dt.float32
BF16 = mybir.dt.bfloat16
AF = mybir.ActivationFunctionType
ALU = mybir.AluOpType
AX = mybir.AxisListType

EPS = 1e-6
NEG = -1.0e30
NDMA = 8
MASK_ENG = "pool"
SCR_Q = "act"
USE_SCR = True

CFG_FULL = dict(D=2048, SEQ=4096, DFF=5632, PAST=2048, DSEQ=32, NSS=4, DEPTH=2)


class Reg:
    __slots__ = ("w", "r", "f", "name")

    def __init__(self, name=""):
        self.w = {}
        self.r = {}
        self.f = {}
        self.name = name


class T:
    __slots__ = ("h", "g")

    def __init__(self, h, g=None):
        self.h = h
        self.g = g if g is not None else Reg()

    def __getitem__(self, k):
        return self.h[k]


class KB:
    def __init__(self, nc, es):
        self.nc = nc
        self.es = es
        self.eng = dict(pe=nc.tensor, dve=nc.vector, act=nc.scalar, pool=nc.gpsimd, sp=nc.sync)
        self.sems = []
        self.esem = {}
        self.ecnt = {}
        for e in ("pe", "dve", "act", "pool"):
            self.esem[e] = self.new_sem("e_" + e)
            self.ecnt[e] = 0
        self.known = {e: {} for e in self.eng}
        self.dq = {}
        for q in ("sp", "pool", "act"):
            self.dq[q] = dict(sems=[self.new_sem("d_%s%d" % (q, i)) for i in range(NDMA)], n=0)
        self.uid = 0
        self.nins = 0

    def new_sem(self, name):
        h = self.es.enter_context(self.nc.semaphore(name))
        self.sems.append(h)
        return len(self.sems) - 1

    def name(self, p):
        self.uid += 1
        return "%s_%d" % (p, self.uid)

    def sb(self, st, name, shape, dt):
        return T(st.enter_context(self.nc.sbuf_tensor(self.name(name), list(shape), dt)))

    def ps(self, st, name, shape, dt=F32):
        return T(st.enter_context(self.nc.psum_tensor(self.name(name), list(shape), dt)))

    def dram(self, name, shape, dt, kind="Internal"):
        return T(self.nc.dram_tensor(name, list(shape), dt, kind=kind).ap())

    def _wait(self, e, evs):
        kn = self.known[e]
        pes = self.esem["pe"]
        for s, v in evs.items():
            if kn.get(s, 0) >= v:
                continue
            if e == "pe" and s == pes:
                continue
            self.eng[e].wait_ge(self.sems[s], v)
            kn[s] = v
            self.nins += 1

    @staticmethod
    def _deps(reads, writes, partial):
        evs = {}
        for r in reads:
            for s, v in r.g.w.items():
                if evs.get(s, 0) < v:
                    evs[s] = v
        for w in writes:
            for s, v in w.g.r.items():
                if evs.get(s, 0) < v:
                    evs[s] = v
            for s, v in (w.g.f if partial else w.g.w).items():
                if evs.get(s, 0) < v:
                    evs[s] = v
        return evs

    @staticmethod
    def _mark(reads, writes, partial, s, c):
        for r in reads:
            if r.g.r.get(s, 0) < c:
                r.g.r[s] = c
        for w in writes:
            if partial:
                if w.g.w.get(s, 0) < c:
                    w.g.w[s] = c
            else:
                w.g.w = {s: c}
                w.g.f = {s: c}
                w.g.r = {}

    def op(self, e, fn, R=(), W=(), partial=False):
        self._wait(e, self._deps(R, W, partial))
        ins = fn(self.eng[e])
        self.ecnt[e] += 1
        c = self.ecnt[e]
        s = self.esem[e]
        ins.then_inc(self.sems[s], 1)
        self.nins += 1
        self._mark(R, W, partial, s, c)

    def dma(self, q, out, in_, R=(), W=(), partial=False, slow=False):
        evs = self._deps(R, W, partial)
        dq = self.dq[q]
        n = dq["n"]
        K = len(dq["sems"])
        s = dq["sems"][n % K]
        v = 16 * (n // K + 1)
        if n >= K and evs.get(s, 0) < v - 16:
            evs[s] = v - 16
        self._wait(q, evs)
        if slow:
            ins = self.eng[q].dma_start(out=out, in_=in_, allow_slow_non_contiguous=True)
        else:
            ins = self.eng[q].dma_start(out=out, in_=in_)
        ins.then_inc(self.sems[s], 16)
        dq["n"] = n + 1
        self.nins += 1
        self._mark(R, W, partial, s, v)

    def barrier(self):
        evs = {}
        for e in ("pe", "dve", "act", "pool"):
            if self.ecnt[e]:
                evs[self.esem[e]] = self.ecnt[e]
        for q, dq in self.dq.items():
            n = dq["n"]
            K = len(dq["sems"])
            for i in range(min(n, K)):
                last = ((n - 1 - i) // K) * K + i
                evs[dq["sems"][i]] = 16 * (last // K + 1)
        for e in ("pe", "dve", "act", "pool", "sp"):
            ev = dict(evs)
            if e in self.esem:
                ev.pop(self.esem[e], None)
            self._wait(e, ev)
        for e in ("dve", "act", "pool"):
            if self.ecnt[e]:
                self._wait(e, {self.esem[e]: self.ecnt[e]})
        if self.ecnt["pe"]:
            self.eng["pe"].wait_ge(self.sems[self.esem["pe"]], self.ecnt["pe"])
            self.known["pe"][self.esem["pe"]] = self.ecnt["pe"]

    def mm(self, out, lhsT, rhs, start, stop, R, W):
        self.op("pe", lambda e: e.matmul(out, lhsT, rhs, start=start, stop=stop), R, W, partial=True)

    def tr(self, out, in_, ident, R, W):
        self.op("pe", lambda e: e.transpose(out, in_, ident), R, W, partial=True)

    def tt(self, eng, out, in0, in1, op, R, W, partial=False):
        self.op(eng, lambda e: e.tensor_tensor(out=out, in0=in0, in1=in1, op=op), R, W, partial)

    def ts(self, eng, out, in0, s1, op0, R, W, s2=None, op1=None, partial=False, accum=None):
        def f(e):
            kw = {}
            if op1 is not None:
                kw["op1"] = op1
            if accum is not None:
                kw["accum_out"] = accum
            return e.tensor_scalar(out=out, in0=in0, scalar1=s1, scalar2=s2, op0=op0, **kw)
        self.op(eng, f, R, W, partial)

    def stt(self, eng, out, in0, scalar, in1, op0, op1, R, W, partial=False):
        self.op(eng, lambda e: e.scalar_tensor_tensor(out=out, in0=in0, scalar=scalar, in1=in1,
                                                      op0=op0, op1=op1), R, W, partial)

    def act(self, out, in_, func, R, W, scale=None, bias=None, accum=None, partial=False):
        def f(e):
            kw = {}
            if scale is not None:
                kw["scale"] = scale
            if bias is not None:
                kw["bias"] = bias
            if accum is not None:
                kw["accum_out"] = accum
            return e.activation(out=out, in_=in_, func=func, **kw)
        self.op("act", f, R, W, partial)

    def cp(self, eng, out, in_, R, W, partial=False):
        if eng == "act":
            self.op("act", lambda e: e.copy(out=out, in_=in_), R, W, partial)
        else:
            self.op(eng, lambda e: e.tensor_copy(out=out, in_=in_), R, W, partial)

    def rsqrt(self, out, in_, scale, eps, R, W, partial=False):
        self.act(out, in_, AF.Sqrt, R, W, scale=scale, bias=eps, partial=partial)
        self.op("dve", lambda e: e.reciprocal(out=out, in_=out), W, W, partial=partial)

    def memset(self, eng, ap, val, W, partial=False):
        self.op(eng, lambda e: e.memset(ap, val), (), W, partial)


class WStream:
    def __init__(self, kb, bufs, loads, q="pool", sc=None, mode="cast"):
        self.kb, self.bufs, self.loads, self.q = kb, bufs, loads, q
        self.sc, self.mode = sc, mode
        self.issued = 0

    def get(self, i, keep=0):
        nb = len(self.bufs)
        while self.issued < min(len(self.loads), i - keep + nb):
            k = self.issued
            b = self.bufs[k % nb]
            out_ap, in_ap = self.loads[k](b)
            if self.mode == "scr":
                sc_ap, _ = self.loads[k](self.sc[k])
                self.kb.dma(SCR_Q, out_ap, sc_ap, R=(self.sc[k],), W=(b,))
            else:
                self.kb.dma(self.q, out_ap, in_ap, R=(), W=(b,))
                if self.mode == "first":
                    sc_ap, _ = self.loads[k](self.sc[k])
                    self.kb.dma("sp", sc_ap, out_ap, R=(b,), W=(self.sc[k],))
            self.issued += 1
        return self.bufs[i % nb]


def col_layout(D):
    o = {}
    c = 0
    for nme, n in (("aq", 1024), ("ak", 1024), ("av", 1024), ("bq", 1024), ("bk", 256), ("bv", 256),
                   ("bqi", 512), ("bki", 64), ("bwi", 8), ("cqkv", 3072), ("ca", 8), ("cb", 8),
                   ("cz", 1024), ("gt", 3 * D)):
        o[nme] = c
        c += n
    o["DIN"] = c
    return o


def build(cfg):
    D, SEQ, DFF, PAST, DSEQ, NSS, DEPTH = (cfg[k] for k in ("D", "SEQ", "DFF", "PAST", "DSEQ", "NSS", "DEPTH"))
    assert DSEQ * NSS == 128 and SEQ % 512 == 0 and PAST % 128 == 0 and DFF % 512 == 0 and D % 512 == 0
    KC = D // 128
    NTP = SEQ // 128
    NT = NTP + 1
    TT_ = SEQ + 128
    NG = SEQ // 512 + 1
    FC = DFF // 128
    CO = col_layout(D)
    DIN = CO["DIN"]
    NSEQ = 1 + NSS
    NCH = SEQ // 64 + NSS
    NPT = PAST // 128
    TOPK_P = min(256, SEQ // 4)
    TOPK_S = min(256, (PAST + DSEQ) // 4)
    assert TOPK_P % 8 == 0 and TOPK_S % 8 == 0

    nc = bass.Bass("TRN2", target_bir_lowering=False)
    es = ExitStack()
    kb = KB(nc, es)

    def din(name, shape, dt=F32):
        return T(nc.dram_tensor(name, list(shape), dt, kind="ExternalInput").ap())

    def dout(name, shape, dt=F32):
        return T(nc.dram_tensor(name, list(shape), dt, kind="ExternalOutput").ap())

    x_all = din("x_all", [TT_, D])
    c_all = din("c_all", [NSEQ, D])
    cache_ka = din("cache_ka", [DEPTH, NSS, PAST, 1024])
    cache_va = din("cache_va", [DEPTH, NSS, PAST, 1024])
    cache_kb = din("cache_kb", [DEPTH, NSS, PAST, 256])
    cache_vb = din("cache_vb", [DEPTH, NSS, PAST, 256])
    cache_ki = din("cache_ki", [DEPTH, NSS, PAST, 64])
    st_cconv = din("st_cconv", [DEPTH, NSS, 3, 3072])
    st_gdn = din("st_gdn", [DEPTH, NSS, 8, 128, 128])
    st_fconv = din("st_fconv", [DEPTH, NSS, 2, 2 * DFF])
    w_ada = din("w_ada", [DEPTH, D, 6 * D])
    b_ada = din("b_ada", [DEPTH, 6 * D])
    norm_mix = din("norm_mix", [DEPTH, D])
    w_in = din("w_in", [DEPTH, D, DIN])
    a_q_norm = din("a_q_norm", [DEPTH, 128])
    a_k_norm = din("a_k_norm", [DEPTH, 128])
    a_lambda = din("a_lambda", [DEPTH, 4, 128])
    a_subln = din("a_subln", [DEPTH, 256])
    b_q_norm = din("b_q_norm", [DEPTH, 128])
    b_k_norm = din("b_k_norm", [DEPTH, 128])
    c_conv = din("c_conv", [DEPTH, 4, 3072])
    c_a_log = din("c_a_log", [DEPTH, 8])
    c_dt_bias = din("c_dt_bias", [DEPTH, 8])
    c_out_norm = din("c_out_norm", [DEPTH, 128])
    w_branch = din("w_branch", [DEPTH, 3, 1024, D])
    w_out = din("w_out", [DEPTH, D, D])
    norm_ffn = din("norm_ffn", [DEPTH, D])
    w_up = din("w_up", [DEPTH, D, 2 * DFF])
    ffn_conv = din("ffn_conv", [DEPTH, 3, 2 * DFF])
    w_down = din("w_down", [DEPTH, DFF, D])
    rope64 = din("rope64", [TT_, 128])
    rope32 = din("rope32", [TT_, 64])
    cmat = din("cmat", [128, 640])
    y_all = dout("y_all", [TT_, D])
    ka_out = dout("ka_out", [DEPTH, TT_, 1024])
    va_out = dout("va_out", [DEPTH, TT_, 1024])
    kb_out = dout("kb_out", [DEPTH, TT_, 256])
    vb_out = dout("vb_out", [DEPTH, TT_, 256])
    ki_out = dout("ki_out", [DEPTH, TT_, 64])
    cconv_out = dout("cconv_out", [DEPTH, NSEQ, 3, 3072])
    gdn_out = dout("gdn_out", [DEPTH, NSEQ, 8, 128, 128])
    fconv_out = dout("fconv_out", [DEPTH, NSEQ, 2, 2 * DFF])
    XT = kb.dram("XT", [KC, 128, TT_], F32)
    QA = kb.dram("QA", [NT, 128, 8, 128], BF16)
    KA = kb.dram("KA", [NT, 128, 8, 128], BF16)
    KAc = kb.dram("KAc", [NSS, max(NPT, 1), 128, 8, 128], BF16)
    QB = kb.dram("QB", [NT, 128, 8, 128], BF16)
    KBs = kb.dram("KBs", [2, 128, TT_], BF16)
    KBc = kb.dram("KBc", [NSS, 2, 128, max(PAST, 128)], BF16)
    QI = kb.dram("QI", [NT, 128, 4, 128], BF16)
    KI2 = kb.dram("KI2", [128, TT_], BF16)
    KIc = kb.dram("KIc", [NSS, 128, max(PAST, 128)], BF16)
    CQKV = kb.dram("CQKV", [24, 128, TT_], BF16)
    CZs = kb.dram("CZs", [NCH, 64, 1024], BF16)
    GT = kb.dram("GT", [3 * KC, 128, TT_], BF16)
    BR = kb.dram("BR", [24, 128, TT_], BF16, kind="ExternalOutput" if cfg.get("debug") else "Internal")

    NB1 = 11 + 3 + 6 + 3 * D // 512
    NB3 = 3 * (D // 512) + D // 512 + 2 * (DFF // 512) + (D // 512) * (FC // (11 if FC % 11 == 0 else 4))
    WS1 = [nc.dram_tensor("WS1_%d" % l, [NB1, 128, KC, 512], BF16, kind="Internal").ap() for l in range(DEPTH)]
    WS3 = [nc.dram_tensor("WS3_%d" % l, [NB3, 128, max(KC, 11), 512], BF16, kind="Internal").ap() for l in range(DEPTH)]

    top = ExitStack()
    es.enter_context(top)

    cm = kb.sb(top, "cm", [128, 640], F32)
    kb.dma("sp", cm[:], cmat[:, :], W=(cm,))
    identf = cm[:, 0:128]
    onesf = cm[:, 128:256]
    UTi = cm[:, 256:384]
    UTs = cm[:, 384:512]
    LTs = cm[:, 512:640]
    cmb = kb.sb(top, "cmb", [128, 256], BF16)
    kb.cp("dve", cmb[:], cm[:, 0:256], R=(cm,), W=(cmb,))
    identb = cmb[:, 0:128]
    onesb = cmb[:, 128:256]
    zb = kb.sb(top, "zb", [128, 128], BF16)
    kb.memset("pool", zb[:], 0.0, W=(zb,))

    seqs = [(0, SEQ, 0, None)] + [(SEQ + DSEQ * s, DSEQ, PAST, s) for s in range(NSS)]
    groups = [(512 * g, 512, [(0, 512, 0)]) for g in range(SEQ // 512)]
    groups.append((SEQ, 128, [(DSEQ * s, DSEQ, 1 + s) for s in range(NSS)]))

    MOD = [kb.sb(top, "mod", [128, 6 * KC, NSEQ], F32) for _ in range(DEPTH)]
    A1 = [kb.sb(top, "a1", [128, KC, NSEQ], F32) for _ in range(DEPTH)]
    A2 = [kb.sb(top, "a2", [128, KC, NSEQ], F32) for _ in range(DEPTH)]
    GBR = kb.sb(top, "gbr", [64, NCH, 16], F32)
    SGN = kb.sb(top, "sgn", [128, NT, 8], F32)

    def pbc(ap1d, n):
        return ap1d.partition_broadcast(128)

    def phase0():
        st = ExitStack()
        xin = [kb.sb(st, "xin", [128, D], F32) for _ in range(2)]
        xts = [kb.sb(st, "xts", [128, KC, 128], F32) for _ in range(2)]
        pt = [kb.ps(st, "p0t", [128, 4, 128], F32) for _ in range(2)]
        n = 0
        for t in range(NT):
            xi = xin[t % 2]
            xo = xts[t % 2]
            kb.dma("sp", xi[:], x_all[t * 128:(t + 1) * 128, :], W=(xi,))
            for k4 in range(KC // 4):
                p = pt[n % 2]
                n += 1
                for j in range(4):
                    kc = k4 * 4 + j
                    kb.tr(p[:, j, :], xi[:, kc * 128:(kc + 1) * 128], identf, R=(xi, cm), W=(p,))
                kb.cp("act" if k4 % 2 else "dve", xo[:, k4 * 4:k4 * 4 + 4, :], p[:], R=(p,), W=(xo,), partial=True)
            kb.dma("sp", XT[:, :, t * 128:(t + 1) * 128].rearrange("k p t -> p k t"), xo[:], R=(xo,), W=(XT,),
                   partial=True)
        cT = kb.sb(st, "cT", [128, KC, NSEQ], F32)
        for kc in range(KC):
            kb.dma("sp", cT[:, kc, :], c_all[:, kc * 128:(kc + 1) * 128].rearrange("b p -> p b"), W=(cT,), partial=True,
                   slow=True)
        sg = kb.sb(st, "csg", [128, KC, NSEQ], F32)
        kb.act(sg[:], cT[:], AF.Sigmoid, R=(cT,), W=(sg,))
        cs = kb.sb(st, "cs", [128, KC, NSEQ], BF16)
        kb.tt("dve", cs[:], cT[:], sg[:], ALU.mult, R=(cT, sg), W=(cs,))
        wb = [kb.sb(st, "wada", [128, KC, 512], BF16) for _ in range(3)]
        pm = [kb.ps(st, "pmod", [128, 8], F32) for _ in range(2)]
        bT = kb.sb(st, "bT", [128, 6 * KC], F32)
        nm = kb.sb(st, "nm", [128, KC], F32)
        n = 0
        for l in range(DEPTH):
            kb.dma("sp", bT[:], b_ada[l, :].rearrange("(k p) -> p k", p=128), W=(bT,), slow=True)
            nblk = 6 * D // 512
            loads = [(lambda b, i=i: (b[:], w_ada[l, :, i * 512:(i + 1) * 512].rearrange("(k p) c -> p k c", p=128)))
                     for i in range(nblk)]
            ws = WStream(kb, wb, loads)
            for i in range(nblk):
                w = ws.get(i)
                for j in range(4):
                    ec = i * 4 + j
                    p = pm[n % 2]
                    n += 1
                    for kc in range(KC):
                        kb.mm(p[:, 0:NSEQ], w[:, kc, j * 128:(j + 1) * 128], cs[:, kc, :], kc == 0, kc == KC - 1,
                              R=(w, cs), W=(p,))
                    kb.ts("dve", MOD[l][:, ec, :], p[:, 0:NSEQ], bT[:, ec:ec + 1], ALU.add, R=(p, bT), W=(MOD[l],),
                          partial=True)
            for (nrm, A, off) in ((norm_mix, A1[l], KC), (norm_ffn, A2[l], 4 * KC)):
                kb.dma("sp", nm[:], nrm[l, :].rearrange("(k p) -> p k", p=128), W=(nm,), slow=True)
                kb.ts("dve", A[:], MOD[l][:, off:off + KC, :], 1.0, ALU.add, R=(MOD[l],), W=(A,))
                kb.tt("dve", A[:], A[:], nm[:].unsqueeze(2).to_broadcast([128, KC, NSEQ]), ALU.mult, R=(A, nm), W=(A,))
        kb.barrier()
        st.close()

    def phase1(l):
        st = ExitStack()
        xT = kb.sb(st, "xT", [128, KC, 512], F32)
        hT = kb.sb(st, "hT", [128, KC, 512], BF16)
        sqb = [kb.sb(st, "sqb", [128, 512], F32) for _ in range(2)]
        rstd = kb.sb(st, "rstd", [128, 512], F32)
        tmpx = [kb.sb(st, "tmpx", [128, 512], F32) for _ in range(2)]
        wb = [kb.sb(st, "w1", [128, KC, 512], BF16) for _ in range(3)]
        gains = {}
        for nme, src in (("aq", a_q_norm), ("ak", a_k_norm), ("bq", b_q_norm), ("bk", b_k_norm)):
            g = kb.sb(st, "g" + nme, [128, 128], F32)
            kb.dma("sp", g[:], src[l, :].partition_broadcast(128), W=(g,))
            gains[nme] = g
        NEA = kb.sb(st, "nea", [128, 8], F32)
        DTB = kb.sb(st, "dtb", [128, 8], F32)
        kb.dma("sp", NEA[:], c_a_log[l, :].partition_broadcast(128), W=(NEA,))
        kb.dma("sp", DTB[:], c_dt_bias[l, :].partition_broadcast(128), W=(DTB,))
        kb.act(NEA[:], NEA[:], AF.Exp, R=(NEA,), W=(NEA,))
        kb.ts("dve", NEA[:], NEA[:], -1.0, ALU.mult, R=(NEA,), W=(NEA,))
        CW = kb.sb(st, "cw", [128, 24, 4], F32)
        for tap in range(4):
            kb.dma("sp", CW[:, :, tap], c_conv[l, tap, :].rearrange("(k p) -> p k", p=128), W=(CW,), partial=True,
                   slow=True)
        HISTC = kb.sb(st, "histc", [128, 24, 3], F32)
        kb.memset("pool", HISTC[:], 0.0, W=(HISTC,))
        SH = kb.sb(st, "shist", [128, NSS, 24, 3], F32)
        for s in range(NSS):
            for j in range(24):
                kb.dma("sp", SH[:, s, j, :], st_cconv[l, s, :, j * 128:(j + 1) * 128].rearrange("t p -> p t"), W=(SH,),
                       partial=True, slow=True)
        R64g = kb.sb(st, "r64", [128, 4, 128], F32)
        R32g = kb.sb(st, "r32", [128, 4, 64], F32)
        tA = [kb.sb(st, "tA", [128, 512], F32) for _ in range(2)]
        tB = [kb.sb(st, "tB", [128, 512], F32) for _ in range(2)]
        tC = [kb.sb(st, "tC", [128, 512], F32) for _ in range(2)]
        tO = [kb.sb(st, "tO", [128, 512], F32) for _ in range(3)]
        ob = [kb.sb(st, "ob", [128, 512], BF16) for _ in range(2)]
        ssm = [kb.sb(st, "ssm", [128, 16], F32) for _ in range(2)]
        QAT = [kb.sb(st, "qat", [128, 8, 128], BF16) for _ in range(4)]
        KAT = [kb.sb(st, "kat", [128, 8, 128], BF16) for _ in range(4)]
        QBT = [kb.sb(st, "qbt", [128, 8, 128], BF16) for _ in range(4)]
        KBT = [kb.sb(st, "kbt", [128, 2, 128], BF16) for _ in range(4)]
        QIT = [kb.sb(st, "qit", [128, 4, 128], BF16) for _ in range(4)]
        KIT = [kb.sb(st, "kit", [128, 128], BF16) for _ in range(4)]
        AW = [kb.sb(st, "aw", [128, 8], F32) for _ in range(4)]
        ext = [kb.sb(st, "ext", [128, 4 * 35 if False else 515], F32) for _ in range(2)]
        gst = [kb.sb(st, "gst", [128, 4, 512], BF16) for _ in range(2)]
        czs = [kb.sb(st, "czs", [64, 512], BF16) for _ in range(3)]
        gtmp = [kb.sb(st, "gtmp", [64, 16], F32) for _ in range(2)]
        pA = [kb.ps(st, "pA", [128, 512], F32) for _ in range(2)]
        pT = [kb.ps(st, "pT", [128, 8, 128], BF16) for _ in range(2)]
        pS = kb.ps(st, "pS", [128, 512], F32)
        pF = [kb.ps(st, "pF", [128, 512], F32) for _ in range(2)]
        pL = kb.ps(st, "pL", [128, 512], F32)
        cnt = dict(a=0, t=0, f=0, w=0, o=0)

        def nxt(k, lst):
            cnt[k] += 1
            return lst[cnt[k] % len(lst)]

        def normrope(src, srcT, nh, gain, rope, half, out, outT, wi, ti):
            hd = 2 * half
            a, b, c_, sm = tA[wi], tB[wi], tC[wi], ssm[wi]
            av = a[:, 0:nh * hd].rearrange("p (h d) -> p h d", d=hd)
            bv = b[:, 0:nh * hd].rearrange("p (h d) -> p h d", d=hd)
            cv = c_[:, 0:nh * hd].rearrange("p (h d) -> p h d", d=hd)
            if gain is not None:
                kb.act(av, src, AF.Square, R=(srcT,), W=(a,))
                kb.op("dve", lambda e: e.tensor_reduce(out=sm[:, 0:nh], in_=av, axis=AX.X, op=ALU.add), R=(a,), W=(sm,))
                kb.rsqrt(sm[:, 0:nh], sm[:, 0:nh], 1.0 / hd, EPS, R=(sm,), W=(sm,))
                kb.tt("dve", bv, src, sm[:, 0:nh].unsqueeze(2).to_broadcast([128, nh, hd]), ALU.mult, R=(srcT, sm), W=(b,))
                kb.tt("dve", bv, bv, gain[:, 0:hd].unsqueeze(1).to_broadcast([128, nh, hd]), ALU.mult, R=(b, gain), W=(b,))
            else:
                kb.cp("act", bv, src, R=(srcT,), W=(b,))
            cosb = rope[:, ti, 0:half].unsqueeze(1).to_broadcast([128, nh, half])
            sinb = rope[:, ti, half:hd].unsqueeze(1).to_broadcast([128, nh, half])
            x1, x2 = bv[:, :, 0:half], bv[:, :, half:hd]
            kb.tt("dve", out[:, :, 0:half], x1, cosb, ALU.mult, R=(b, rope), W=(outT,))
            kb.tt("dve", cv[:, :, 0:half], x2, sinb, ALU.mult, R=(b, rope), W=(c_,))
            kb.tt("dve", out[:, :, 0:half], out[:, :, 0:half], cv[:, :, 0:half], ALU.subtract, R=(outT, c_), W=(outT,))
            kb.tt("dve", out[:, :, half:hd], x2, cosb, ALU.mult, R=(b, rope), W=(outT,), partial=True)
            kb.tt("dve", cv[:, :, half:hd], x1, sinb, ALU.mult, R=(b, rope), W=(c_,))
            kb.tt("dve", out[:, :, half:hd], out[:, :, half:hd], cv[:, :, half:hd], ALU.add, R=(outT, c_), W=(outT,))

        def transposes(srcb, nblk, dst, dst0):
            p = nxt("t", pT)
            for i in range(nblk):
                kb.tr(p[:, i, :], srcb[:, i * 128:(i + 1) * 128], identb, R=(srcb, cmb), W=(p,))
            kb.cp("act", dst[:, dst0:dst0 + nblk, :], p[:, 0:nblk, :], R=(p,), W=(dst,), partial=True)

        for gi, (tok0, ntok, segs) in enumerate(groups):
            ntile = ntok // 128
            kb.dma("sp", R64g[:, 0:ntile, :], rope64[tok0:tok0 + ntok, :].rearrange("(t p) c -> p t c", p=128), W=(R64g,))
            kb.dma("sp", R32g[:, 0:ntile, :], rope32[tok0:tok0 + ntok, :].rearrange("(t p) c -> p t c", p=128), W=(R32g,))
            kb.dma("sp", xT[:, :, 0:ntok], XT[:, :, tok0:tok0 + ntok].rearrange("k p t -> p k t"), R=(XT,), W=(xT,))
            for kc in range(KC):
                sq = sqb[kc % 2]
                kb.act(sq[:, 0:ntok], xT[:, kc, 0:ntok], AF.Square, R=(xT,), W=(sq,))
                kb.mm(pS[:, 0:ntok], onesf, sq[:, 0:ntok], kc == 0, kc == KC - 1, R=(cm, sq), W=(pS,))
            kb.rsqrt(rstd[:, 0:ntok], pS[:, 0:ntok], 1.0 / D, EPS, R=(pS,), W=(rstd,))
            for kc in range(KC):
                tx = tmpx[kc % 2]
                kb.tt("dve", tx[:, 0:ntok], xT[:, kc, 0:ntok], rstd[:, 0:ntok], ALU.mult, R=(xT, rstd), W=(tx,))
                for (c0, n, sq_) in segs:
                    kb.ts("pool", hT[:, kc, c0:c0 + n], tx[:, c0:c0 + n], A1[l][:, kc, sq_:sq_ + 1], ALU.mult,
                          R=(tx, A1[l], MOD[l]), W=(hT,), s2=MOD[l][:, kc, sq_:sq_ + 1], op1=ALU.add, partial=True)

            tm_blocks = []
            for nme, nb in (("aq", 2), ("ak", 2), ("av", 2), ("bq", 2)):
                for j in range(nb):
                    tm_blocks.append((nme, j, CO[nme] + 512 * j, 512))
            tm_blocks.append(("bkv", 0, CO["bk"], 512))
            tm_blocks.append(("bix", 0, CO["bki"], 72))
            tm_blocks.append(("bqi", 0, CO["bqi"], 512))
            ch_blocks = [("cab", 0, CO["ca"], 16), ("cz", 0, CO["cz"], 512), ("cz", 1, CO["cz"] + 512, 512)]
            fm_blocks = [("cqkv", j, CO["cqkv"] + 512 * j, 512) for j in range(6)]
            fm_blocks += [("gt", j, CO["gt"] + 512 * j, 512) for j in range(3 * D // 512)]
            blocks = tm_blocks + ch_blocks + fm_blocks
            loads = [(lambda b, c0=c0, ncol=ncol: (b[:, :, 0:ncol],
                                                    w_in[l, :, c0:c0 + ncol].rearrange("(k p) c -> p k c", p=128)))
                     for (_, _, c0, ncol) in blocks]
            assert len(blocks) == NB1
            if gi == 0:
                sc1 = [T(WS1[l][k]) for k in range(NB1)]
            ws = WStream(kb, wb, loads, sc=sc1, mode=("first" if gi == 0 else "scr") if USE_SCR else "cast")

            for bi, (kind, j, c0, ncol) in enumerate(blocks):
                w = ws.get(bi)
                if bi < len(tm_blocks):
                    for ti in range(ntile):
                        tglob = tok0 // 128 + ti
                        r0 = tok0 + ti * 128
                        r64, r32 = R64g, R32g
                        p = nxt("a", pA)
                        for kc in range(KC):
                            kb.mm(p[:, 0:ncol], hT[:, kc, ti * 128:(ti + 1) * 128], w[:, kc, 0:ncol], kc == 0, kc == KC - 1,
                                  R=(hT, w), W=(p,))
                        wi = cnt["w"] = (cnt["w"] + 1) % 2
                        o = nxt("o", tO)
                        obf = ob[wi]
                        p3 = p[:, :].rearrange("p (h d) -> p h d", d=128)
                        o3 = o[:, :].rearrange("p (h d) -> p h d", d=128)
                        if kind in ("aq", "ak", "bq"):
                            normrope(p3, p, 4, gains[kind], r64, 64, o3, o, wi, ti)
                            kb.cp("act", obf[:], o[:], R=(o,), W=(obf,))
                            dstl = dict(aq=QAT, ak=KAT, bq=QBT)[kind]
                            dst = dstl[ti]
                            transposes(obf, 4, dst, 4 * j)
                            if kind == "ak":
                                kb.dma("sp", ka_out[l, r0:r0 + 128, 512 * j:512 * j + 512], o[:], R=(o,), W=(ka_out,), partial=True)
                            if j == 1:
                                dd = dict(aq=QA, ak=KA, bq=QB)[kind]
                                kb.dma("sp", dd[tglob], dst[:], R=(dst,), W=(dd,), partial=True)
                        elif kind == "av":
                            kb.cp("act", o[:], p[:], R=(p,), W=(o,))
                            kb.dma("sp", va_out[l, r0:r0 + 128, 512 * j:512 * j + 512], o[:], R=(o,), W=(va_out,), partial=True)
                        elif kind == "bkv":
                            normrope(p3[:, 0:2, :], p, 2, gains["bk"], r64, 64, o3[:, 0:2, :], o, wi, ti)
                            kb.cp("act", o[:, 256:512], p[:, 256:512], R=(p,), W=(o,), partial=True)
                            kb.cp("act", obf[:, 0:256], o[:, 0:256], R=(o,), W=(obf,))
                            dst = KBT[ti]
                            transposes(obf, 2, dst, 0)
                            kb.dma("sp", kb_out[l, r0:r0 + 128, :], o[:, 0:256], R=(o,), W=(kb_out,), partial=True)
                            kb.dma("sp", vb_out[l, r0:r0 + 128, :], o[:, 256:512], R=(o,), W=(vb_out,), partial=True)
                            kb.dma("sp", KBs[:, :, r0:r0 + 128].rearrange("n p t -> p n t"), dst[:], R=(dst,), W=(KBs,), partial=True)
                        elif kind == "bix":
                            pk = p[:, 0:64].rearrange("p (h d) -> p h d", d=64)
                            ok = o[:, 0:64].rearrange("p (h d) -> p h d", d=64)
                            normrope(pk, p, 1, None, r32, 32, ok, o, wi, ti)
                            kb.dma("sp", ki_out[l, r0:r0 + 128, :], o[:, 0:64], R=(o,), W=(ki_out,), partial=True)
                            kb.cp("act", obf[:, 0:64], o[:, 0:64], R=(o,), W=(obf,))
                            kb.cp("act", obf[:, 64:128], o[:, 0:64], R=(o,), W=(obf,), partial=True)
                            dst = KIT[ti]
                            pp = nxt("t", pT)
                            kb.tr(pp[:, 0, :], obf[:, 0:128], identb, R=(obf, cmb), W=(pp,))
                            kb.cp("act", dst[:], pp[:, 0, :], R=(pp,), W=(dst,))
                            kb.dma("sp", KI2[:, r0:r0 + 128], dst[:], R=(dst,), W=(KI2,), partial=True)
                            aw = AW[ti]
                            kb.act(SGN[:, tglob, :], p[:, 64:72], AF.Sign, R=(p,), W=(SGN,), partial=True)
                            kb.act(aw[:], p[:, 64:72], AF.Abs, R=(p,), W=(aw,), scale=(8.0 ** -0.5) * 0.125)
                        elif kind == "bqi":
                            p8 = p[:, :].rearrange("p (h d) -> p h d", d=64)
                            o8 = o[:, :].rearrange("p (h d) -> p h d", d=64)
                            normrope(p8, p, 8, None, r32, 32, o8, o, wi, ti)
                            aw = AW[ti]
                            ob8 = obf[:, :].rearrange("p (h d) -> p h d", d=64)
                            kb.tt("dve", ob8, o8, aw[:].unsqueeze(2).to_broadcast([128, 8, 64]), ALU.mult, R=(o, aw), W=(obf,))
                            dst = QIT[ti]
                            transposes(obf, 4, dst, 0)
                            kb.dma("sp", QI[tglob], dst[:], R=(dst,), W=(QI,), partial=True)
                elif bi < len(tm_blocks) + len(ch_blocks):
                    for (sc0, sn, sq_) in segs:
                        C = 64 if sq_ == 0 else DSEQ
                        for ci in range(sn // C):
                            col = sc0 + ci * C
                            chg = (tok0 + col) // 64 if sq_ == 0 else SEQ // 64 + (sq_ - 1)
                            p = nxt("a", pA)
                            for kc in range(KC):
                                kb.mm(p[0:C, 0:ncol], hT[:, kc, col:col + C], w[:, kc, 0:ncol], kc == 0, kc == KC - 1,
                                      R=(hT, w), W=(p,))
                            if kind == "cab":
                                gt_ = nxt("f", gtmp)
                                kb.tt("dve", gt_[0:C, 0:8], p[0:C, 0:8], DTB[0:C, :], ALU.add, R=(p, DTB), W=(gt_,))
                                kb.act(gt_[0:C, 0:8], gt_[0:C, 0:8], AF.Exp, R=(gt_,), W=(gt_,))
                                kb.act(gt_[0:C, 0:8], gt_[0:C, 0:8], AF.Ln, R=(gt_,), W=(gt_,), bias=1.0)
                                kb.tt("dve", GBR[0:C, chg, 0:8], gt_[0:C, 0:8], NEA[0:C, :], ALU.mult, R=(gt_, NEA), W=(GBR,), partial=True)
                                kb.act(GBR[0:C, chg, 8:16], p[0:C, 8:16], AF.Sigmoid, R=(p,), W=(GBR,), partial=True)
                            else:
                                cz_ = nxt("f", czs)
                                kb.act(cz_[0:C, :], p[0:C, :], AF.Silu, R=(p,), W=(cz_,))
                                kb.dma("sp", CZs[chg, 0:C, 512 * j:512 * j + 512], cz_[0:C, :], R=(cz_,), W=(CZs,), partial=True)
                else:
                    g4 = gst[bi % 2]
                    for q in range(4):
                        p = nxt("f", pF)
                        for kc in range(KC):
                            kb.mm(p[:, 0:ntok], w[:, kc, q * 128:(q + 1) * 128], hT[:, kc, 0:ntok], kc == 0, kc == KC - 1,
                                  R=(w, hT), W=(p,))
                        if kind == "gt":
                            kb.act(g4[:, q, 0:ntok], p[:, 0:ntok], AF.Sigmoid, R=(p,), W=(g4,), partial=(q > 0))
                        else:
                            fj = 4 * j + q
                            e_ = ext[fj % 2]
                            y_ = tA[fj % 2]
                            s_ = tB[fj % 2]
                            for (sc0, sn, sq_) in segs:
                                if sq_ == 0:
                                    hist = HISTC[:, fj, :]
                                    hT_ = HISTC
                                else:
                                    hist = SH[:, sq_ - 1, fj, :]
                                    hT_ = SH
                                kb.cp("pool", e_[:, 0:3], hist, R=(hT_,), W=(e_,))
                                kb.cp("act", e_[:, 3:3 + sn], p[:, sc0:sc0 + sn], R=(p,), W=(e_,), partial=True)
                                kb.ts("dve", y_[:, sc0:sc0 + sn], e_[:, 0:sn], CW[:, fj, 0:1], ALU.mult, R=(e_, CW), W=(y_,),
                                      partial=(sc0 > 0))
                                for tap in range(1, 4):
                                    kb.stt("dve", y_[:, sc0:sc0 + sn], e_[:, tap:tap + sn], CW[:, fj, tap:tap + 1],
                                           y_[:, sc0:sc0 + sn], ALU.mult, ALU.add, R=(e_, CW, y_), W=(y_,), partial=True)
                                kb.cp("pool", hist, e_[:, sn:sn + 3], R=(e_,), W=(hT_,), partial=True)
                            kb.act(s_[:, 0:ntok], y_[:, 0:ntok], AF.Silu, R=(y_,), W=(s_,))
                            if fj < 16:
                                kb.tt("dve", y_[:, 0:ntok], s_[:, 0:ntok], s_[:, 0:ntok], ALU.mult, R=(s_,), W=(y_,))
                                kb.mm(pL[:, 0:ntok], onesf, y_[:, 0:ntok], True, True, R=(cm, y_), W=(pL,))
                                kb.rsqrt(y_[:, 0:ntok], pL[:, 0:ntok], 1.0, EPS, R=(pL,), W=(y_,))
                                kb.stt("dve", g4[:, q, 0:ntok], s_[:, 0:ntok], (128.0 ** -0.5) if fj < 8 else 1.0, y_[:, 0:ntok],
                                       ALU.mult, ALU.mult, R=(s_, y_), W=(g4,), partial=(q > 0))
                            else:
                                kb.cp("act", g4[:, q, 0:ntok], s_[:, 0:ntok], R=(s_,), W=(g4,), partial=(q > 0))
                    dd = GT if kind == "gt" else CQKV
                    kb.dma("sp", dd[4 * j:4 * j + 4, :, tok0:tok0 + ntok].rearrange("c p t -> p c t"), g4[:, :, 0:ntok], R=(g4,),
                           W=(dd,), partial=True)
        for sq_ in range(NSEQ):
            for fj in range(24):
                src = HISTC[:, fj, :] if sq_ == 0 else SH[:, sq_ - 1, fj, :]
                kb.dma("sp", cconv_out[l, sq_, :, fj * 128:(fj + 1) * 128].rearrange("t p -> p t"), src,
                       R=(HISTC if sq_ == 0 else SH,), W=(cconv_out,), partial=True, slow=True)
        kb.barrier()
        st.close()

    def phase2_cache(l):
        if NPT == 0:
            return
        st = ExitStack()
        cin = [kb.sb(st, "cin", [128, 1024 + 256 + 128], BF16) for _ in range(2)]
        cka = [kb.sb(st, "cka", [128, 8, 128], BF16) for _ in range(2)]
        ckb = [kb.sb(st, "ckb", [128, 2, 128], BF16) for _ in range(2)]
        cki = [kb.sb(st, "cki", [128, 128], BF16) for _ in range(2)]
        pT = [kb.ps(st, "pTc", [128, 8, 128], BF16) for _ in range(3)]
        n = 0
        for s in range(NSS):
            for pt in range(NPT):
                ci = cin[n % 2]
                a, b, c_ = cka[n % 2], ckb[n % 2], cki[n % 2]
                n += 1
                r0 = pt * 128
                kb.dma("pool", ci[:, 0:1024], cache_ka[l, s, r0:r0 + 128, :], W=(ci,))
                kb.dma("pool", ci[:, 1024:1280], cache_kb[l, s, r0:r0 + 128, :], W=(ci,), partial=True)
                kb.dma("pool", ci[:, 1280:1344], cache_ki[l, s, r0:r0 + 128, :], W=(ci,), partial=True)
                kb.dma("pool", ci[:, 1344:1408], cache_ki[l, s, r0:r0 + 128, :], W=(ci,), partial=True)
                p = pT[0]
                for i in range(8):
                    kb.tr(p[:, i, :], ci[:, i * 128:(i + 1) * 128], identb, R=(ci, cmb), W=(p,))
                kb.cp("act", a[:], p[:], R=(p,), W=(a,))
                p = pT[1]
                for i in range(2):
                    kb.tr(p[:, i, :], ci[:, 1024 + i * 128:1024 + (i + 1) * 128], identb, R=(ci, cmb), W=(p,))
                kb.cp("dve", b[:], p[:, 0:2, :], R=(p,), W=(b,))
                p = pT[2]
                kb.tr(p[:, 0, :], ci[:, 1280:1408], identb, R=(ci, cmb), W=(p,))
                kb.cp("dve", c_[:], p[:, 0, :], R=(p,), W=(c_,))
                kb.dma("sp", KAc[s, pt], a[:], R=(a,), W=(KAc,), partial=True)
                kb.dma("sp", KBc[s, :, :, r0:r0 + 128].rearrange("n p t -> p n t"), b[:], R=(b,), W=(KBc,), partial=True)
                kb.dma("sp", KIc[s, :, r0:r0 + 128], c_[:], R=(c_,), W=(KIc,), partial=True)
        kb.barrier()
        st.close()

    def phase2a(l):
        st = ExitStack()
        lam_init = 0.8 - 0.6 * math.exp(-0.3 * l)
        lmb = kb.sb(st, "lmb", [128, 4, 128], F32)
        kb.dma("sp", lmb[:].rearrange("p a d -> p (a d)"), a_lambda[l].rearrange("a d -> (a d)").partition_broadcast(128), W=(lmb,))
        lt = kb.sb(st, "lt", [128, 2, 128], F32)
        l2 = kb.sb(st, "l2", [128, 2], F32)
        nlam = kb.sb(st, "nlam", [128, 1], F32)
        kb.tt("dve", lt[:, 0, :], lmb[:, 0, :], lmb[:, 1, :], ALU.mult, R=(lmb,), W=(lt,))
        kb.tt("dve", lt[:, 1, :], lmb[:, 2, :], lmb[:, 3, :], ALU.mult, R=(lmb,), W=(lt,), partial=True)
        kb.op("dve", lambda e: e.tensor_reduce(out=l2[:], in_=lt[:], axis=AX.X, op=ALU.add), R=(lt,), W=(l2,))
        kb.act(l2[:], l2[:], AF.Exp, R=(l2,), W=(l2,))
        kb.tt("dve", nlam[:], l2[:, 1:2], l2[:, 0:1], ALU.subtract, R=(l2,), W=(nlam,))
        kb.ts("dve", nlam[:], nlam[:], -lam_init, ALU.add, R=(nlam,), W=(nlam,))
        SUB = kb.sb(st, "subln", [128, 2], F32)
        for ec in range(2):
            kb.dma("sp", SUB[:, ec:ec + 1], a_subln[l, ec * 128:(ec + 1) * 128].rearrange("(p o) -> p o", o=1), W=(SUB,),
                   partial=True)
        kb.ts("dve", SUB[:], SUB[:], 1.0 - lam_init, ALU.mult, R=(SUB,), W=(SUB,))

        qt = [kb.sb(st, "qa", [128, 8, 128], BF16) for _ in range(2)]
        kt = [kb.sb(st, "ka", [128, 4, 128], BF16) for _ in range(3)]
        vt = [kb.sb(st, "va", [128, 512], BF16) for _ in range(3)]
        PT = [kb.sb(st, "pt", [128, 4, 128], BF16) for _ in range(2)]
        rd = kb.sb(st, "rd", [128, 4, 128], F32)
        t0_ = kb.sb(st, "t0", [128, 2, 2, 128], F32)
        t1_ = kb.sb(st, "t1", [128, 2, 2, 128], F32)
        sq_ = kb.sb(st, "sq", [128, 2, 2, 128], F32)
        rs = kb.sb(st, "rs", [128, 2, 128], F32)
        oa = [kb.sb(st, "oa", [128, 4, 128], BF16) for _ in range(2)]
        pSc = [kb.ps(st, "pSc", [128, 4, 128], F32) for _ in range(2)]
        pO = kb.ps(st, "pO", [128, 4, 2, 128], F32)
        pD = kb.ps(st, "pD", [128, 4, 128], F32)
        pN = kb.ps(st, "pN", [128, 2, 128], F32)
        scale = 128.0 ** -0.5
        nk_ = 0
        no = 0
        for (tok0, Tn, Pn, sidx) in seqs:
            nqt = max(Tn // 128, 1)
            for qi in range(nqt):
                nq = min(128, Tn)
                tile_g = (tok0 // 128) if sidx is None else NTP
                qc0 = 0 if sidx is None else DSEQ * sidx
                q = qt[(qi + (0 if sidx is None else sidx)) % 2]
                kb.dma("sp", q[:], QA[tile_g if sidx is not None else qi], R=(QA,), W=(q,))
                keys = []
                if sidx is None:
                    for j in range(qi + 1):
                        keys.append(("p", j, 128, j == qi))
                else:
                    for j in range(NPT):
                        keys.append(("c", j, 128, False))
                    keys.append(("n", 0, DSEQ, False))
                for hp in range(2):
                    pO2 = pO[:, :, :, :].rearrange("p a b c -> p (a b c)")
                    kb.mm(pO2[:, 0:512], zb[:], q[:, 0:4, :], True, False, R=(zb, q), W=(pO,))
                    kb.mm(pO2[:, 512:1024], zb[:], q[:, 0:4, :], True, False, R=(zb, q), W=(pO,))
                    kb.mm(pD[:, :, :], zb[:], q[:, 0:4, :], True, False, R=(zb, q), W=(pD,))
                    def emit_qk(ki_):
                        nonlocal nk_
                        kk, j, nk, diag = keys[ki_]
                        k_ = kt[nk_ % 3]
                        v_ = vt[nk_ % 3]
                        nk_ += 1
                        if kk == "p":
                            kb.dma("sp", k_[:], KA[j, :, 4 * hp:4 * hp + 4, :], R=(KA,), W=(k_,))
                            kb.dma("pool", v_[:], va_out[l, j * 128:(j + 1) * 128, 512 * hp:512 * hp + 512], R=(va_out,), W=(v_,))
                        elif kk == "c":
                            kb.dma("sp", k_[:], KAc[sidx, j, :, 4 * hp:4 * hp + 4, :], R=(KAc,), W=(k_,))
                            kb.dma("pool", v_[:], cache_va[l, sidx, j * 128:(j + 1) * 128, 512 * hp:512 * hp + 512], W=(v_,))
                        else:
                            kb.dma("sp", k_[:, :, 0:nk], KA[NTP, :, 4 * hp:4 * hp + 4, qc0:qc0 + nk], R=(KA,), W=(k_,))
                            kb.dma("pool", v_[0:nk, :], va_out[l, tok0:tok0 + nk, 512 * hp:512 * hp + 512], R=(va_out,), W=(v_,))
                        ps = pSc[nk_ % 2]
                        p_ = PT[nk_ % 2]
                        for hmi in range(4):
                            kb.mm(ps[0:nk, hmi, 0:nq], k_[:, hmi, 0:nk], q[:, 4 * hp + hmi, qc0:qc0 + nq], True, True,
                                  R=(k_, q), W=(ps,))
                        kb.act(p_[0:nk, :, 0:nq], ps[0:nk, :, 0:nq], AF.Exp, R=(ps,), W=(p_,), scale=scale)
                        if diag:
                            kb.memset("pool", p_[64:128, :, 0:64], 0.0, W=(p_,), partial=True)
                        return v_, p_

                    nxt_vp = emit_qk(0)
                    for ki_, (kk, j, nk, diag) in enumerate(keys):
                        v_, p_ = nxt_vp
                        if ki_ + 1 < len(keys):
                            nxt_vp = emit_qk(ki_ + 1)
                        first = ki_ == 0
                        last = ki_ == len(keys) - 1
                        for hmi in range(4):
                            for ec in range(2):
                                c0 = (hmi // 2) * 256 + ec * 128
                                kb.mm(pO[:, hmi, ec, 0:nq], v_[0:nk, c0:c0 + 128], p_[0:nk, hmi, 0:nq], False, last,
                                      R=(v_, p_), W=(pO,))
                            kb.mm(pD[:, hmi, 0:nq], onesb[0:nk, :], p_[0:nk, hmi, 0:nq], False, last, R=(cmb, p_), W=(pD,))
                    kb.op("dve", lambda e: e.reciprocal(out=rd[:, :, 0:nq], in_=pD[:, :, 0:nq]), R=(pD,), W=(rd,))
                    pO5 = pO[:, :, :, :].rearrange("p (h m) e q -> p h m e q", m=2)
                    rd4 = rd[:, :, :].rearrange("p (h m) q -> p h m q", m=2)
                    for h2 in range(2):
                        kb.tt("dve", t0_[:, h2, :, 0:nq], pO5[:, h2, 0, :, 0:nq],
                              rd4[:, h2, 0, 0:nq].unsqueeze(1).to_broadcast([128, 2, nq]), ALU.mult, R=(pO, rd), W=(t0_,),
                              partial=(h2 > 0))
                        kb.stt("dve", t1_[:, h2, :, 0:nq], pO5[:, h2, 1, :, 0:nq], nlam[:, 0:1],
                               rd4[:, h2, 1, 0:nq].unsqueeze(1).to_broadcast([128, 2, nq]), ALU.mult, ALU.mult, R=(pO, rd, nlam),
                               W=(t1_,), partial=(h2 > 0))
                    for h2 in range(2):
                        kb.tt("dve", t0_[:, h2, :, 0:nq], t0_[:, h2, :, 0:nq], t1_[:, h2, :, 0:nq], ALU.add, R=(t0_, t1_), W=(t0_,),
                              partial=(h2 > 0))
                        kb.act(sq_[:, h2, :, 0:nq], t0_[:, h2, :, 0:nq], AF.Square, R=(t0_,), W=(sq_,), partial=(h2 > 0))
                    for h2 in range(2):
                        for ec in range(2):
                            kb.mm(pN[:, h2, 0:nq], onesf, sq_[:, h2, ec, 0:nq], ec == 0, ec == 1, R=(cm, sq_), W=(pN,))
                    kb.rsqrt(rs[:, :, 0:nq], pN[:, :, 0:nq], 1.0 / 256, EPS, R=(pN,), W=(rs,))
                    for h2 in range(2):
                        kb.tt("dve", t0_[:, h2, :, 0:nq], t0_[:, h2, :, 0:nq],
                              rs[:, h2, 0:nq].unsqueeze(1).to_broadcast([128, 2, nq]), ALU.mult, R=(t0_, rs), W=(t0_,),
                              partial=(h2 > 0))
                    o_ = oa[no % 2]
                    no += 1
                    o4 = o_[:, :, :].rearrange("p (h e) q -> p h e q", e=2)
                    for ec in range(2):
                        kb.ts("dve", o4[:, :, ec, 0:nq], t0_[:, :, ec, 0:nq], SUB[:, ec:ec + 1], ALU.mult, R=(t0_, SUB), W=(o_,),
                              partial=(ec > 0))
                    qtok = tok0 + qi * 128
                    kb.dma("sp", BR[4 * hp:4 * hp + 4, :, qtok:qtok + nq].rearrange("c p t -> p c t"), o_[:, :, 0:nq], R=(o_,),
                           W=(BR,), partial=True)
        kb.barrier()
        st.close()

    def phase2b(l):
        st = ExitStack()
        LMAX = max(SEQ, PAST + DSEQ)
        LT_ = (LMAX + 127) // 128
        KIT = kb.sb(st, "kit", [128, LT_ * 128], BF16)
        KBT = kb.sb(st, "kbt", [128, 2, LT_ * 128], BF16)
        VB = kb.sb(st, "vb", [128, LT_, 256], BF16)
        qi_ = [kb.sb(st, "qi", [128, 4, 128], BF16) for _ in range(2)]
        qb_ = [kb.sb(st, "qb", [128, 8, 128], BF16) for _ in range(2)]
        score = kb.sb(st, "score", [128, LT_ * 128], F32)
        work = kb.sb(st, "work", [128, LT_ * 128], F32)
        mask = kb.sb(st, "mask", [128, LT_ * 128], BF16)
        MT = kb.sb(st, "mt", [128, LT_, 128], BF16)
        rr = [kb.sb(st, "rr", [128, 512], F32) for _ in range(2)]
        m8 = kb.sb(st, "m8", [128, 8], F32)
        thr = kb.sb(st, "thr", [128, 1], F32)
        sg0 = [kb.sb(st, "sg0", [128, 8], F32) for _ in range(2)]
        E_ = [kb.sb(st, "E", [128, 4, 128], BF16) for _ in range(2)]
        P_ = [kb.sb(st, "P", [128, 4, 128], BF16) for _ in range(2)]
        rdn = kb.sb(st, "rdn", [128, 4, 128], F32)
        ob_ = [kb.sb(st, "obo", [128, 4, 128], BF16) for _ in range(2)]
        pI = [kb.ps(st, "pI", [128, 512], F32) for _ in range(2)]
        pM = kb.ps(st, "pM", [128, 8, 128], BF16)
        pSb = [kb.ps(st, "pSb", [128, 4, 128], F32) for _ in range(2)]
        pOb = kb.ps(st, "pOb", [128, 4, 128], F32)
        pDb = kb.ps(st, "pDb", [128, 4, 128], F32)
        scale = 128.0 ** -0.5
        ni = 0
        nb_ = 0
        for (tok0, Tn, Pn, sidx) in seqs:
            L = Pn + Tn
            nkt = (L + 127) // 128
            if sidx is None:
                kb.dma("sp", KIT[:, 0:L], KI2[:, 0:L], R=(KI2,), W=(KIT,))
                kb.dma("sp", KBT[:, :, 0:L], KBs[:, :, 0:L].rearrange("n p t -> p n t"), R=(KBs,), W=(KBT,))
                kb.dma("pool", VB[:, 0:nkt, :], vb_out[l, 0:L, :].rearrange("(j p) c -> p j c", p=128), R=(vb_out,), W=(VB,))
            else:
                if Pn:
                    kb.dma("sp", KIT[:, 0:Pn], KIc[sidx, :, 0:Pn], R=(KIc,), W=(KIT,))
                    kb.dma("sp", KBT[:, :, 0:Pn], KBc[sidx, :, :, 0:Pn].rearrange("n p t -> p n t"), R=(KBc,), W=(KBT,))
                    kb.dma("pool", VB[:, 0:Pn // 128, :], cache_vb[l, sidx, :, :].rearrange("(j p) c -> p j c", p=128), W=(VB,))
                kb.dma("sp", KIT[:, Pn:L], KI2[:, tok0:tok0 + Tn], R=(KI2,), W=(KIT,), partial=True)
                kb.dma("sp", KBT[:, :, Pn:L], KBs[:, :, tok0:tok0 + Tn].rearrange("n p t -> p n t"), R=(KBs,), W=(KBT,), partial=True)
                kb.dma("pool", VB[0:Tn, Pn // 128, :], vb_out[l, tok0:tok0 + Tn, :], R=(vb_out,), W=(VB,), partial=True)
            nqt = max(Tn // 128, 1)
            topk = TOPK_P if sidx is None else TOPK_S
            for qi in range(nqt):
                nq = min(128, Tn)
                tile_g = qi if sidx is None else NTP
                qc0 = 0 if sidx is None else DSEQ * sidx
                Lv = 128 * (qi + 1) if sidx is None else L
                nvt = (Lv + 127) // 128
                qI = qi_[ni % 2]
                qB = qb_[ni % 2]
                ni += 1
                kb.dma("sp", qI[:], QI[tile_g], R=(QI,), W=(qI,))
                kb.dma("sp", qB[:], QB[tile_g], R=(QB,), W=(qB,))
                sgt = sg0[ni % 2]
                kb.dma("sp", sgt[0:nq, :], SGN[qc0:qc0 + nq, tile_g, :], R=(SGN,), W=(sgt,))
                for k0 in range(0, Lv, 512):
                    kn = min(512, Lv - k0)
                    for h in range(8):
                        p = pI[(nb_) % 2]
                        r_ = rr[nb_ % 2]
                        nb_ += 1
                        pb = (h % 2) * 64
                        kb.mm(p[0:nq, 0:kn], qI[pb:pb + 64, h // 2, qc0:qc0 + nq], KIT[pb:pb + 64, k0:k0 + kn], True, True,
                              R=(qI, KIT), W=(p,))
                        kb.act(r_[0:nq, 0:kn], p[0:nq, 0:kn], AF.Relu, R=(p,), W=(r_,))
                        sg = sgt[0:nq, h:h + 1]
                        if h == 0:
                            kb.ts("dve", score[0:nq, k0:k0 + kn], r_[0:nq, 0:kn], sg, ALU.mult, R=(r_, sgt), W=(score,),
                                  partial=True)
                        else:
                            kb.stt("dve", score[0:nq, k0:k0 + kn], r_[0:nq, 0:kn], sg, score[0:nq, k0:k0 + kn], ALU.mult, ALU.add,
                                   R=(r_, sgt, score), W=(score,), partial=True)
                if sidx is None:
                    kb.memset("dve", score[0:64, Lv - 64:Lv], NEG, W=(score,), partial=True)
                nvis_min = (Lv - 64) if sidx is None else Lv
                if nvis_min > topk:
                    src = score
                    for r in range(topk // 8):
                        kb.op("dve", lambda e, src=src: e.max(out=m8[0:nq, :], in_=src[0:nq, 0:Lv]), R=(src,), W=(m8,))
                        if r < topk // 8 - 1:
                            kb.op("dve", lambda e, src=src: e.match_replace(out=work[0:nq, 0:Lv], in_to_replace=m8[0:nq, :],
                                                                            in_values=src[0:nq, 0:Lv], imm_value=NEG),
                                  R=(src, m8), W=(work,))
                            src = work
                    kb.cp("dve", thr[0:nq, :], m8[0:nq, 7:8], R=(m8,), W=(thr,))
                else:
                    kb.memset("dve", thr[0:nq, :], -1.0e29, W=(thr,))
                kb.ts("dve", mask[0:nq, 0:Lv], score[0:nq, 0:Lv], thr[0:nq, 0:1], ALU.is_ge, R=(score, thr), W=(mask,))
                for j0 in range(0, nvt, 8):
                    jn = min(8, nvt - j0)
                    for jj in range(jn):
                        j = j0 + jj
                        nk = min(128, Lv - j * 128)
                        kb.tr(pM[0:nk, jj, 0:nq], mask[0:nq, j * 128:j * 128 + nk], identb[0:nq, 0:nq], R=(mask, cmb), W=(pM,))
                    nkl = min(128, Lv - (j0 + jn - 1) * 128)
                    if nkl == 128:
                        kb.cp("act", MT[:, j0:j0 + jn, 0:nq], pM[:, 0:jn, 0:nq], R=(pM,), W=(MT,), partial=True)
                    else:
                        if jn > 1:
                            kb.cp("act", MT[:, j0:j0 + jn - 1, 0:nq], pM[:, 0:jn - 1, 0:nq], R=(pM,), W=(MT,), partial=True)
                        kb.cp("act", MT[0:nkl, j0 + jn - 1, 0:nq], pM[0:nkl, jn - 1, 0:nq], R=(pM,), W=(MT,), partial=True)
                steps = [(n, j) for n in range(2) for j in range(nvt)]
                nb0 = nb_
                nb_ += len(steps)

                def emit_qk(s_):
                    n, j = steps[s_]
                    nk = min(128, Lv - j * 128)
                    ps = pSb[(nb0 + s_) % 2]
                    e_ = E_[(nb0 + s_) % 2]
                    p_ = P_[(nb0 + s_) % 2]
                    kb.mm(ps[0:nk, :, 0:nq], KBT[:, n, j * 128:j * 128 + nk], qB[:, 4 * n:4 * n + 4, qc0:qc0 + nq], True, True,
                          R=(KBT, qB), W=(ps,))
                    kb.act(e_[0:nk, :, 0:nq], ps[0:nk, :, 0:nq], AF.Exp, R=(ps,), W=(e_,), scale=scale)
                    kb.tt(MASK_ENG, p_[0:nk, :, 0:nq], e_[0:nk, :, 0:nq],
                          MT[0:nk, j, 0:nq].unsqueeze(1).to_broadcast([nk, 4, nq]), ALU.mult, R=(e_, MT), W=(p_,))

                emit_qk(0)
                for s_, (n, j) in enumerate(steps):
                    if s_ + 1 < len(steps):
                        emit_qk(s_ + 1)
                    nk = min(128, Lv - j * 128)
                    p_ = P_[(nb0 + s_) % 2]
                    kb.mm(pOb[:, :, 0:nq], VB[0:nk, j, n * 128:(n + 1) * 128], p_[0:nk, :, 0:nq], j == 0, j == nvt - 1,
                          R=(VB, p_), W=(pOb,))
                    kb.mm(pDb[:, :, 0:nq], onesb[0:nk, :], p_[0:nk, :, 0:nq], j == 0, j == nvt - 1, R=(cmb, p_), W=(pDb,))
                    if j == nvt - 1:
                        kb.op("dve", lambda e: e.reciprocal(out=rdn[:, :, 0:nq], in_=pDb[:, :, 0:nq]), R=(pDb,), W=(rdn,))
                        o_ = ob_[n % 2]
                        kb.tt("dve", o_[:, :, 0:nq], pOb[:, :, 0:nq], rdn[:, :, 0:nq], ALU.mult, R=(pOb, rdn), W=(o_,))
                        qtok = tok0 + qi * 128
                        kb.dma("sp", BR[8 + 4 * n:8 + 4 * n + 4, :, qtok:qtok + nq].rearrange("c p t -> p c t"), o_[:, :, 0:nq], R=(o_,),
                               W=(BR,), partial=True)
        kb.barrier()
        st.close()

    def phase2c(l):
        st = ExitStack()
        COUT = kb.sb(st, "cout", [128, 128], F32)
        kb.dma("sp", COUT[:], c_out_norm[l, :].partition_broadcast(128), W=(COUT,))
        S = kb.sb(st, "S", [128, 8, 128], F32)
        Sb = kb.sb(st, "Sb", [128, 8, 128], BF16)
        qkv = [kb.sb(st, "qkv", [128, 24, 512], BF16) for _ in range(2)]
        czt = [kb.sb(st, "czt", [64, 1024], BF16) for _ in range(2)]
        octs = [kb.sb(st, "octs", [128, 8, 512], BF16) for _ in range(2)]

        def f32t(nm, shape, n=1):
            return [kb.sb(st, nm, shape, F32) for _ in range(n)]
        Ug, Ib, X, Xm, Xp = (f32t(k, [64, 8, 64])[0] for k in ("Ug", "Ib", "X", "Xm", "Xp"))
        gTi, gTs, gLs = (f32t(k, [64, 8, 64])[0] for k in ("gTi", "gTs", "gLs"))
        Gc = f32t("Gc", [64, 8])[0]
        egc = f32t("egc", [64, 8])[0]
        dk = f32t("dk", [64, 8])[0]
        bge = f32t("bge", [64, 8])[0]
        eG128 = f32t("eG128", [128, 8, 64])[0]
        egl = f32t("egl", [128, 8])[0]
        Pm = f32t("Pm", [64, 8, 64], 2)
        PTm = f32t("PTm", [64, 8, 64], 2)
        Rm = f32t("Rm", [64, 8, 64])[0]
        tmp = f32t("tmp", [64, 8, 64])[0]
        QKg = kb.sb(st, "QKg", [64, 8, 64], BF16)
        bv = f32t("bv", [64, 8, 128])[0]
        bgk = f32t("bgk", [64, 8, 128])[0]
        kd = kb.sb(st, "kd", [64, 8, 128], BF16)
        Usb = f32t("Usb", [64, 8, 128])[0]
        wTb = kb.sb(st, "wTb", [128, 8, 64], BF16)
        qgT = kb.sb(st, "qgT", [128, 8, 64], BF16)
        vnb = kb.sb(st, "vnb", [64, 8, 128], BF16)
        osb = f32t("osb", [64, 8, 128])[0]
        osq = f32t("osq", [64, 8, 128])[0]
        ors = f32t("ors", [64, 8])[0]
        onb = kb.sb(st, "onb", [64, 8, 128], BF16)
        Stmp = f32t("Stmp", [128, 8, 128])[0]
        b0 = kb.ps(st, "b0", [128, 8, 64], F32)
        b1 = kb.ps(st, "b1", [128, 8, 64], F32)
        b2 = kb.ps(st, "b2", [128, 8, 64], F32)
        b3 = kb.ps(st, "b3", [128, 8, 64], F32)
        b4 = kb.ps(st, "b4", [128, 8, 64], F32)
        b5 = kb.ps(st, "b5", [128, 8, 128], BF16)
        b6 = kb.ps(st, "b6", [128, 8, 128], BF16)
        b7 = kb.ps(st, "b7", [128, 4, 128], F32)
        b3u = b3
        nch = 0
        for si, (tok0, Tn, Pn, sidx) in enumerate(seqs):
            C = 64 if sidx is None else DSEQ
            nsteps = int(math.log2(C)) - 1
            if sidx is None:
                kb.memset("pool", S[:], 0.0, W=(S,))
                kb.memset("pool", Sb[:], 0.0, W=(Sb,))
            else:
                kb.dma("sp", S[:], st_gdn[l, sidx].rearrange("h k v -> k h v"), W=(S,))
                kb.cp("act", Sb[:], S[:], R=(S,), W=(Sb,))
            nchunk = Tn // C
            for ci in range(nchunk):
                t0 = tok0 + ci * C
                chg = (t0 // 64) if sidx is None else SEQ // 64 + sidx
                if sidx is None:
                    if ci % 8 == 0:
                        qb_ = qkv[(ci // 8) % 2]
                        kb.dma("sp", qb_[:], CQKV[:, :, t0:t0 + 512].rearrange("c p t -> p c t"), R=(CQKV,), W=(qb_,))
                    co = (ci % 8) * 64
                else:
                    qb_ = qkv[nch % 2]
                    kb.dma("sp", qb_[:, :, 0:C], CQKV[:, :, t0:t0 + C].rearrange("c p t -> p c t"), R=(CQKV,), W=(qb_,))
                    co = 0
                nch += 1
                qT = qb_[:, 0:8, co:co + C]
                kT = qb_[:, 8:16, co:co + C]
                vT = qb_[:, 16:24, co:co + C]
                cz_ = czt[nch % 2]
                kb.dma("sp", cz_[0:C, :], CZs[chg, 0:C, :], R=(CZs,), W=(cz_,))
                g = GBR[0:C, chg, 0:8]
                beta = GBR[0:C, chg, 8:16]

                def bch(ap2, n=C):
                    return ap2.unsqueeze(2).to_broadcast([C, 8, n])

                def bcm(ap2):
                    return ap2.unsqueeze(1).to_broadcast([C, 8, C])
                kb.tt("dve", Ug[0:C, :, 0:C], bcm(UTi[0:C, 0:C]), bch(g), ALU.mult, R=(cm, GBR), W=(Ug,))
                kb.tt("dve", Ib[0:C, :, 0:C], bcm(identf[0:C, 0:C]), bch(beta), ALU.mult, R=(cm, GBR), W=(Ib,))
                kb.mm(b0[:, :, 0:C], onesf[0:C, :], Ug[0:C, :, 0:C], True, True, R=(cm, Ug), W=(b0,))
                kb.mm(b1[0:C, :, 0:C], onesf[0:C, 0:C], Ib[0:C, :, 0:C], True, True, R=(cm, Ib), W=(b1,))
                kb.mm(b2[0:C, 0, 0:8], UTi[0:C, 0:C], g, True, True, R=(cm, GBR), W=(b2,))
                kb.mm(b2[:, 1, 0:8], onesf[0:C, :], g, True, True, R=(cm, GBR), W=(b2,))
                kb.cp("act", Gc[0:C, :], b2[0:C, 0, 0:8], R=(b2,), W=(Gc,))
                kb.tt("dve", X[0:C, :, 0:C], b0[0:C, :, 0:C], bch(Gc[0:C, :]), ALU.subtract, R=(b0, Gc), W=(X,))
                kb.ts("dve", Xm[0:C, :, 0:C], X[0:C, :, 0:C], 0.0, ALU.min, R=(X,), W=(Xm,))
                kb.ts("pool", Xp[0:C, :, 0:C], X[0:C, :, 0:C], 0.0, ALU.max, R=(X,), W=(Xp,))
                kb.act(Xm[0:C, :, 0:C], Xm[0:C, :, 0:C], AF.Exp, R=(Xm,), W=(Xm,))
                kb.act(Xp[0:C, :, 0:C], Xp[0:C, :, 0:C], AF.Exp, R=(Xp,), W=(Xp,), scale=-1.0)
                kb.tt("dve", gTi[0:C, :, 0:C], Xm[0:C, :, 0:C], bcm(UTi[0:C, 0:C]), ALU.mult, R=(Xm, cm), W=(gTi,))
                kb.tt("dve", gTs[0:C, :, 0:C], Xm[0:C, :, 0:C], bcm(UTs[0:C, 0:C]), ALU.mult, R=(Xm, cm), W=(gTs,))
                kb.tt("dve", gLs[0:C, :, 0:C], Xp[0:C, :, 0:C], bcm(LTs[0:C, 0:C]), ALU.mult, R=(Xp, cm), W=(gLs,))
                kb.act(eG128[:, :, 0:C], b0[:, :, 0:C], AF.Exp, R=(b0,), W=(eG128,))
                kb.act(egc[0:C, :], Gc[0:C, :], AF.Exp, R=(Gc,), W=(egc,))
                kb.act(egl[:, :], b2[:, 1, 0:8], AF.Exp, R=(b2,), W=(egl,))
                kb.tt("dve", dk[0:C, :], b2[0:C, 1, 0:8], Gc[0:C, :], ALU.subtract, R=(b2, Gc), W=(dk,))
                kb.act(dk[0:C, :], dk[0:C, :], AF.Exp, R=(dk,), W=(dk,))
                kb.tt("dve", bge[0:C, :], beta, egc[0:C, :], ALU.mult, R=(GBR, egc), W=(bge,))
                for h in range(8):
                    kb.mm(b3[0:C, h, 0:C], kT[:, h, :], kT[:, h, :], True, True, R=(qb_,), W=(b3,))
                for h in range(8):
                    kb.mm(b4[0:C, h, 0:C], kT[:, h, :], qT[:, h, :], True, True, R=(qb_,), W=(b4,))
                for h in range(8):
                    kb.tr(b5[0:C, h, :], kT[:, h, :], identb, R=(qb_, cmb), W=(b5,))
                for h in range(8):
                    kb.tr(b6[0:C, h, :], vT[:, h, :], identb, R=(qb_, cmb), W=(b6,))
                P0, P0T = Pm[0], PTm[0]
                kb.stt("dve", tmp[0:C, :, 0:C], b3[0:C, :, 0:C], -1.0, gTs[0:C, :, 0:C], ALU.mult, ALU.mult, R=(b3, gTs), W=(tmp,))
                kb.tt("dve", P0[0:C, :, 0:C], tmp[0:C, :, 0:C], b1[0:C, :, 0:C], ALU.mult, R=(tmp, b1), W=(P0,))
                kb.stt("dve", tmp[0:C, :, 0:C], b3[0:C, :, 0:C], -1.0, gLs[0:C, :, 0:C], ALU.mult, ALU.mult, R=(b3, gLs), W=(tmp,))
                kb.tt("dve", P0T[0:C, :, 0:C], tmp[0:C, :, 0:C], bch(beta), ALU.mult, R=(tmp, GBR), W=(P0T,))
                kb.tt("dve", QKg[0:C, :, 0:C], b4[0:C, :, 0:C], gTi[0:C, :, 0:C], ALU.mult, R=(b4, gTi), W=(QKg,))
                kb.tt("dve", bv[0:C, :, :], b6[0:C, :, :], bch(beta, 128), ALU.mult, R=(b6, GBR), W=(bv,))
                kb.tt("dve", bgk[0:C, :, :], b5[0:C, :, :], bch(bge[0:C, :], 128), ALU.mult, R=(b5, bge), W=(bgk,))
                kb.tt("dve", kd[0:C, :, :], b5[0:C, :, :], bch(dk[0:C, :], 128), ALU.mult, R=(b5, dk), W=(kd,))
                kb.tt("dve", qgT[:, :, 0:C], qT, eG128[:, :, 0:C], ALU.mult, R=(qb_, eG128), W=(qgT,))
                kb.tt("dve", Rm[0:C, :, 0:C], P0[0:C, :, 0:C], bcm(identf[0:C, 0:C]), ALU.add, R=(P0, cm), W=(Rm,))
                Pc, PTc = P0, P0T
                for sstep in range(nsteps):
                    Pn_, PTn_ = Pm[(sstep + 1) % 2], PTm[(sstep + 1) % 2]
                    lastst = sstep == nsteps - 1
                    for h in range(8):
                        kb.mm(b1[0:C, h, 0:C], Pc[0:C, h, 0:C], PTc[0:C, h, 0:C], True, True, R=(Pc, PTc), W=(b1,))
                    kb.cp("act", PTn_[0:C, :, 0:C], b1[0:C, :, 0:C], R=(b1,), W=(PTn_,))
                    if not lastst:
                        for h in range(8):
                            kb.mm(b0[0:C, h, 0:C], PTc[0:C, h, 0:C], Pc[0:C, h, 0:C], True, True, R=(Pc, PTc), W=(b0,))
                        kb.cp("dve", Pn_[0:C, :, 0:C], b0[0:C, :, 0:C], R=(b0,), W=(Pn_,))
                    for h in range(8):
                        kb.mm(b2[0:C, h, 0:C], PTn_[0:C, h, 0:C], Rm[0:C, h, 0:C], True, True, R=(PTn_, Rm), W=(b2,))
                    kb.tt("dve", Rm[0:C, :, 0:C], Rm[0:C, :, 0:C], b2[0:C, :, 0:C], ALU.add, R=(Rm, b2), W=(Rm,))
                    Pc, PTc = Pn_, PTn_
                for half in range(2):
                    for hh in range(4):
                        h = half * 4 + hh
                        kb.mm(b3u[0:C, 2 * hh:2 * hh + 2, :].rearrange("p a b -> p (a b)"), Rm[0:C, h, 0:C], bv[0:C, h, :], True, True,
                              R=(Rm, bv), W=(b3u,))
                    kb.cp("act", Usb[0:C, half * 4:half * 4 + 4, :].rearrange("p a b -> p (a b)"),
                          b3u[0:C, :, :].rearrange("p a b -> p (a b)"), R=(b3u,), W=(Usb,), partial=(half > 0))
                for h in range(8):
                    kb.mm(b4[:, h, 0:C], bgk[0:C, h, :], Rm[0:C, h, 0:C], True, True, R=(bgk, Rm), W=(b4,))
                kb.cp("act", wTb[:, :, 0:C], b4[:, :, 0:C], R=(b4,), W=(wTb,))
                for half in range(2):
                    hs = slice(half * 4, half * 4 + 4)
                    for hh in range(4):
                        h = half * 4 + hh
                        kb.mm(b7[0:C, hh, :], wTb[:, h, 0:C], Sb[:, h, :], True, True, R=(wTb, Sb), W=(b7,))
                    kb.tt("dve", vnb[0:C, hs, :], Usb[0:C, hs, :], b7[0:C, :, :], ALU.subtract, R=(Usb, b7), W=(vnb,), partial=(half > 0))
                    for hh in range(4):
                        h = half * 4 + hh
                        kb.mm(b7[0:C, hh, :], qgT[:, h, 0:C], Sb[:, h, :], True, False, R=(qgT, Sb), W=(b7,))
                        kb.mm(b7[0:C, hh, :], QKg[0:C, h, 0:C], vnb[0:C, h, :], False, True, R=(QKg, vnb), W=(b7,))
                    kb.cp("act", osb[0:C, hs, :], b7[0:C, :, :], R=(b7,), W=(osb,), partial=(half > 0))
                    for hh in range(4):
                        h = half * 4 + hh
                        kb.mm(b7[:, hh, :], kd[0:C, h, :], vnb[0:C, h, :], True, True, R=(kd, vnb), W=(b7,))
                    kb.tt("dve", Stmp[:, hs, :], S[:, hs, :], egl[:, hs].unsqueeze(2).to_broadcast([128, 4, 128]), ALU.mult,
                          R=(S, egl), W=(Stmp,), partial=(half > 0))
                    kb.tt("dve", S[:, hs, :], Stmp[:, hs, :], b7[:, :, :], ALU.add, R=(Stmp, b7), W=(S,), partial=True)
                    kb.cp("act", Sb[:, hs, :], S[:, hs, :], R=(S,), W=(Sb,), partial=True)
                kb.act(osq[0:C, :, :], osb[0:C, :, :], AF.Square, R=(osb,), W=(osq,))
                kb.op("dve", lambda e: e.tensor_reduce(out=ors[0:C, :], in_=osq[0:C, :, :], axis=AX.X, op=ALU.add), R=(osq,), W=(ors,))
                kb.rsqrt(ors[0:C, :], ors[0:C, :], 1.0 / 128, EPS, R=(ors,), W=(ors,))
                kb.tt("dve", osb[0:C, :, :], osb[0:C, :, :], bch(ors[0:C, :], 128), ALU.mult, R=(osb, ors), W=(osb,))
                kb.tt("dve", osb[0:C, :, :], osb[0:C, :, :], COUT[0:C, :].unsqueeze(1).to_broadcast([C, 8, 128]), ALU.mult,
                      R=(osb, COUT), W=(osb,))
                kb.tt("dve", onb[0:C, :, :], osb[0:C, :, :], cz_[0:C, :].rearrange("p (h d) -> p h d", d=128), ALU.mult,
                      R=(osb, cz_), W=(onb,))
                for h in range(8):
                    kb.tr(b5[:, h, 0:C], onb[0:C, h, :], identb[0:C, 0:C], R=(onb, cmb), W=(b5,))
                if sidx is None:
                    oc_ = octs[(ci // 8) % 2]
                    kb.cp("act", oc_[:, :, co:co + C], b5[:, :, 0:C], R=(b5,), W=(oc_,), partial=(ci % 8 > 0))
                    if ci % 8 == 7:
                        tb = t0 + C - 512
                        kb.dma("sp", BR[16:24, :, tb:tb + 512].rearrange("c p t -> p c t"), oc_[:], R=(oc_,), W=(BR,), partial=True)
                else:
                    oc_ = octs[nch % 2]
                    kb.cp("act", oc_[:, :, 0:C], b5[:, :, 0:C], R=(b5,), W=(oc_,))
                    kb.dma("sp", BR[16:24, :, t0:t0 + C].rearrange("c p t -> p c t"), oc_[:, :, 0:C], R=(oc_,), W=(BR,), partial=True)
            kb.dma("sp", gdn_out[l, si].rearrange("h k v -> k h v"), S[:], R=(S,), W=(gdn_out,), partial=True)
        kb.barrier()
        st.close()

    def phase3(l):
        st = ExitStack()
        last_layer = l == DEPTH - 1
        xT = kb.sb(st, "xT3", [128, KC, 512], F32)
        gt_ = [kb.sb(st, "gt3", [128, 4, 512], BF16) for _ in range(2)]
        mixed = kb.sb(st, "mixed", [128, KC, 512], BF16)
        h2T = mixed
        sqb = [kb.sb(st, "sqb3", [128, 512], F32) for _ in range(2)]
        rstd = kb.sb(st, "rstd3", [128, 512], F32)
        tmpx = [kb.sb(st, "tmpx3", [128, 512], F32) for _ in range(2)]
        actT = kb.sb(st, "actT", [128, max(FC, 24), 512], BF16)
        br = actT
        NW = 11 if FC % 11 == 0 else 4
        wk = [kb.sb(st, "wk", [128, max(KC, 8, NW), 512], BF16) for _ in range(3)]
        FW = kb.sb(st, "fw", [128, 2 * FC, 3], F32)
        for tap in range(3):
            kb.dma("sp", FW[:, :, tap], ffn_conv[l, tap, :].rearrange("(k p) -> p k", p=128), W=(FW,), partial=True, slow=True)
        HF = kb.sb(st, "hf", [128, 2 * FC, 2], F32)
        kb.memset("pool", HF[:], 0.0, W=(HF,))
        SHF = kb.sb(st, "shf", [128, NSS, 2 * FC, 2], F32)
        for s in range(NSS):
            for j in range(2 * FC):
                kb.dma("sp", SHF[:, s, j, :], st_fconv[l, s, :, j * 128:(j + 1) * 128].rearrange("t p -> p t"), W=(SHF,),
                       partial=True, slow=True)
        eg = [kb.sb(st, "eg", [128, 514], F32) for _ in range(2)]
        eu = [kb.sb(st, "eu", [128, 514], F32) for _ in range(2)]
        cg = [kb.sb(st, "cg", [128, 512], F32) for _ in range(2)]
        cu = [kb.sb(st, "cu", [128, 512], F32) for _ in range(2)]
        yst = [kb.sb(st, "yst", [128, D], F32) for _ in range(1)] if last_layer else []
        MACC = [cg[0], cg[1], cu[0], cu[1]]
        mtmp = [eg[0], eu[0]]
        pB = [kb.ps(st, "pB3", [128, 512], F32) for _ in range(2)]
        pY = [kb.ps(st, "pY3", [128, 512], F32) for _ in range(2)]
        pS = kb.ps(st, "pS3", [128, 512], F32)
        pU = [kb.ps(st, "pU3", [128, 512], F32) for _ in range(3)]
        cn = dict(b=0, y=0, u=0, t=0)

        def nxt(k, lst):
            cn[k] += 1
            return lst[cn[k] % len(lst)]

        for gi, (tok0, ntok, segs) in enumerate(groups):
            ntile = ntok // 128
            kb.dma("sp", xT[:, :, 0:ntok], XT[:, :, tok0:tok0 + ntok].rearrange("k p t -> p k t"), R=(XT,), W=(xT,))
            kb.dma("sp", br[:, 0:24, 0:ntok], BR[:, :, tok0:tok0 + ntok].rearrange("c p t -> p c t"), R=(BR,), W=(br,))
            nbw = D // 512
            loads = []
            for cb in range(nbw):
                for n in range(3):
                    loads.append(lambda b, n=n, cb=cb: (b[:, 0:8, :], w_branch[l, n, :, cb * 512:(cb + 1) * 512].rearrange("(k p) c -> p k c", p=128)))
            if gi == 0:
                sc3 = [T(WS3[l][k]) for k in range(NB3)]
            md = ("first" if gi == 0 else "scr") if USE_SCR else "cast"
            o3 = 0
            ws = WStream(kb, wk, loads, sc=sc3[o3:o3 + len(loads)], mode=md)
            o3 += len(loads)
            for cb in range(nbw):
                for n in range(3):
                    w = ws.get(cb * 3 + n)
                    g4 = nxt("b", gt_)
                    c0 = n * KC + cb * 4
                    kb.dma("sp", g4[:, :, 0:ntok], GT[c0:c0 + 4, :, tok0:tok0 + ntok].rearrange("c p t -> p c t"), R=(GT,), W=(g4,))
                    for q in range(4):
                        dc = cb * 4 + q
                        p = nxt("b", pB)
                        for kc in range(8):
                            kb.mm(p[:, 0:ntok], w[:, kc, q * 128:(q + 1) * 128], br[:, 8 * n + kc, 0:ntok], kc == 0, kc == 7,
                                  R=(w, br), W=(p,))
                        acc = MACC[q]
                        if n == 0:
                            kb.tt("dve", acc[:, 0:ntok], p[:, 0:ntok], g4[:, q, 0:ntok], ALU.mult, R=(p, g4), W=(acc,))
                        else:
                            mt = nxt("t", mtmp)
                            kb.tt("dve", mt[:, 0:ntok], p[:, 0:ntok], g4[:, q, 0:ntok], ALU.mult, R=(p, g4), W=(mt,))
                            if n == 1:
                                kb.tt("pool", acc[:, 0:ntok], acc[:, 0:ntok], mt[:, 0:ntok], ALU.add, R=(acc, mt), W=(acc,))
                            else:
                                kb.tt("pool", mixed[:, dc, 0:ntok], acc[:, 0:ntok], mt[:, 0:ntok], ALU.add, R=(acc, mt), W=(mixed,),
                                      partial=True)
            loads = [(lambda b, cb=cb: (b[:, 0:KC, :], w_out[l, :, cb * 512:(cb + 1) * 512].rearrange("(k p) c -> p k c", p=128)))
                     for cb in range(nbw)]
            ws = WStream(kb, wk, loads, sc=sc3[o3:o3 + len(loads)], mode=md)
            o3 += len(loads)
            for cb in range(nbw):
                w = ws.get(cb)
                for q in range(4):
                    dc = cb * 4 + q
                    p = nxt("y", pY)
                    for kc in range(KC):
                        kb.mm(p[:, 0:ntok], w[:, kc, q * 128:(q + 1) * 128], mixed[:, kc, 0:ntok], kc == 0, kc == KC - 1,
                              R=(w, mixed), W=(p,))
                    for (c0, n_, sq_) in segs:
                        kb.stt("dve", xT[:, dc, c0:c0 + n_], p[:, c0:c0 + n_], MOD[l][:, 2 * KC + dc, sq_:sq_ + 1], xT[:, dc, c0:c0 + n_],
                               ALU.mult, ALU.add, R=(p, MOD[l], xT), W=(xT,), partial=True)
            for kc in range(KC):
                sq = sqb[kc % 2]
                kb.act(sq[:, 0:ntok], xT[:, kc, 0:ntok], AF.Square, R=(xT,), W=(sq,))
                kb.mm(pS[:, 0:ntok], onesf, sq[:, 0:ntok], kc == 0, kc == KC - 1, R=(cm, sq), W=(pS,))
            kb.rsqrt(rstd[:, 0:ntok], pS[:, 0:ntok], 1.0 / D, EPS, R=(pS,), W=(rstd,))
            for kc in range(KC):
                tx = tmpx[kc % 2]
                kb.tt("dve", tx[:, 0:ntok], xT[:, kc, 0:ntok], rstd[:, 0:ntok], ALU.mult, R=(xT, rstd), W=(tx,))
                for (c0, n_, sq_) in segs:
                    kb.ts("pool", h2T[:, kc, c0:c0 + n_], tx[:, c0:c0 + n_], A2[l][:, kc, sq_:sq_ + 1], ALU.mult,
                          R=(tx, A2[l], MOD[l]), W=(h2T,), s2=MOD[l][:, 3 * KC + kc, sq_:sq_ + 1], op1=ALU.add, partial=True)
            nfb = DFF // 512
            loads = []
            for fb in range(nfb):
                for half in range(2):
                    c0 = half * DFF + fb * 512
                    loads.append(lambda b, c0=c0: (b[:, 0:KC, :], w_up[l, :, c0:c0 + 512].rearrange("(k p) c -> p k c", p=128)))
            ws = WStream(kb, wk, loads, sc=sc3[o3:o3 + len(loads)], mode=md)
            o3 += len(loads)
            for fb in range(nfb):
                wg = ws.get(2 * fb)
                wu = ws.get(2 * fb + 1, keep=1)
                for q in range(4):
                    fc = fb * 4 + q
                    res = []
                    for half, w in ((0, wg), (1, wu)):
                        ch = half * FC + fc
                        p = nxt("u", pU)
                        for kc in range(KC):
                            kb.mm(p[:, 0:ntok], w[:, kc, q * 128:(q + 1) * 128], h2T[:, kc, 0:ntok], kc == 0, kc == KC - 1,
                                  R=(w, h2T), W=(p,))
                        e_ = (eg if half == 0 else eu)[fc % 2]
                        c_ = (cg if half == 0 else cu)[fc % 2]
                        for (c0, n_, sq_) in segs:
                            if sq_ == 0:
                                hist, hT_ = HF[:, ch, :], HF
                            else:
                                hist, hT_ = SHF[:, sq_ - 1, ch, :], SHF
                            kb.cp("pool", e_[:, 0:2], hist, R=(hT_,), W=(e_,))
                            kb.cp("act", e_[:, 2:2 + n_], p[:, c0:c0 + n_], R=(p,), W=(e_,), partial=True)
                            kb.ts("dve", c_[:, c0:c0 + n_], e_[:, 0:n_], FW[:, ch, 0:1], ALU.mult, R=(e_, FW), W=(c_,), partial=(c0 > 0))
                            for tap in (1, 2):
                                kb.stt("dve", c_[:, c0:c0 + n_], e_[:, tap:tap + n_], FW[:, ch, tap:tap + 1], c_[:, c0:c0 + n_],
                                       ALU.mult, ALU.add, R=(e_, FW, c_), W=(c_,), partial=True)
                            kb.cp("pool", hist, e_[:, n_:n_ + 2], R=(e_,), W=(hT_,), partial=True)
                        res.append(c_)
                    kb.act(res[0][:, 0:ntok], res[0][:, 0:ntok], AF.Silu, R=(res[0],), W=(res[0],))
                    kb.tt("dve", actT[:, fc, 0:ntok], res[0][:, 0:ntok], res[1][:, 0:ntok], ALU.mult, R=(res[0], res[1]), W=(actT,),
                          partial=True)
            nparts = FC // NW
            loads = []
            for cb in range(nbw):
                for pp in range(nparts):
                    loads.append(lambda b, cb=cb, pp=pp: (b[:, 0:NW, :], w_down[l, pp * NW * 128:(pp + 1) * NW * 128,
                                                                     cb * 512:(cb + 1) * 512].rearrange("(k p) c -> p k c", p=128)))
            ws = WStream(kb, wk, loads, sc=sc3[o3:o3 + len(loads)], mode=md)
            o3 += len(loads)
            assert o3 == NB3
            for cb in range(nbw):
                wl = [None] * nparts
                ps4 = []
                for q in range(4):
                    ps4.append(nxt("y", pY) if q < 2 else nxt("b", pB))
                for pp in range(nparts):
                    w = ws.get(cb * nparts + pp)
                    for q in range(4):
                        for k in range(NW):
                            fc = pp * NW + k
                            kb.mm(ps4[q][:, 0:ntok], w[:, k, q * 128:(q + 1) * 128], actT[:, fc, 0:ntok], fc == 0, fc == FC - 1,
                                  R=(w, actT), W=(ps4[q],))
                for q in range(4):
                    dc = cb * 4 + q
                    for (c0, n_, sq_) in segs:
                        kb.stt("dve", xT[:, dc, c0:c0 + n_], ps4[q][:, c0:c0 + n_], MOD[l][:, 5 * KC + dc, sq_:sq_ + 1],
                               xT[:, dc, c0:c0 + n_], ALU.mult, ALU.add, R=(ps4[q], MOD[l], xT), W=(xT,), partial=True)
            if not last_layer:
                kb.dma("sp", XT[:, :, tok0:tok0 + ntok].rearrange("k p t -> p k t"), xT[:, :, 0:ntok], R=(xT,), W=(XT,), partial=True)
            else:
                for ti in range(ntile):
                    ys = nxt("t", yst)
                    for k4 in range(KC // 4):
                        p = nxt("u", pU)
                        p3 = p[:, :].rearrange("p (a b) -> p a b", b=128)
                        for j in range(4):
                            kc = k4 * 4 + j
                            kb.tr(p3[:, j, :], xT[:, kc, ti * 128:(ti + 1) * 128], identf, R=(xT, cm), W=(p,))
                        kb.cp("act" if k4 % 2 else "dve", ys[:, k4 * 512:(k4 + 1) * 512], p[:, :], R=(p,), W=(ys,), partial=(k4 > 0))
                    r0 = tok0 + ti * 128
                    kb.dma("sp", y_all[r0:r0 + 128, :], ys[:], R=(ys,), W=(y_all,), partial=True)
        for sq_ in range(NSEQ):
            for ch in range(2 * FC):
                src = HF[:, ch, :] if sq_ == 0 else SHF[:, sq_ - 1, ch, :]
                kb.dma("sp", fconv_out[l, sq_, :, ch * 128:(ch + 1) * 128].rearrange("t p -> p t"), src,
                       R=(HF if sq_ == 0 else SHF,), W=(fconv_out,), partial=True, slow=True)
        kb.barrier()
        st.close()


    stop_after = cfg.get("stop_after")
    phase0()
    for l in range(DEPTH):
        phase1(l)
        if stop_after == "p1":
            break
        phase2_cache(l)
        phase2a(l)
        if stop_after == "p2a":
            break
        phase2b(l)
        if stop_after == "p2b":
            break
        phase2c(l)
        if stop_after == "p2c":
            break
        phase3(l)
    kb.barrier()
    top.close()
    es.close()
    return nc, kb


def host_consts(cfg):
    SEQ, PAST, DSEQ, NSS = cfg["SEQ"], cfg["PAST"], cfg["DSEQ"], cfg["NSS"]
    pos = np.concatenate([np.arange(SEQ)] + [PAST + np.arange(DSEQ)] * NSS).astype(np.float32)

    def table(half):
        inv = np.power(np.float32(10000.0), -np.arange(half, dtype=np.float32) / np.float32(half)).astype(np.float32)
        ang = (pos[:, None] * inv[None, :]).astype(np.float32)
        return np.concatenate([np.cos(ang.astype(np.float64)), np.sin(ang.astype(np.float64))], axis=1).astype(np.float32)

    i = np.arange(128)
    cm = np.zeros((128, 640), np.float32)
    cm[:, 0:128] = np.eye(128)
    cm[:, 128:256] = 1.0
    cm[:, 256:384] = (i[None, :] >= i[:, None])
    cm[:, 384:512] = (i[None, :] > i[:, None])
    cm[:, 512:640] = (i[:, None] > i[None, :])
    return dict(rope64=table(64), rope32=table(32), cmat=cm)


_WNAMES = ("w_ada", "b_ada", "norm_mix", "w_in", "a_q_norm", "a_k_norm", "a_lambda", "a_subln", "b_q_norm", "b_k_norm",
           "c_conv", "c_a_log", "c_dt_bias", "c_out_norm", "w_branch", "w_out", "norm_ffn", "w_up", "ffn_conv", "w_down")


def host_inmaps(cfg, inp, ncores):
    NSS, DEPTH, PAST = cfg["NSS"], cfg["DEPTH"], cfg["PAST"]
    f = lambda a: np.ascontiguousarray(np.asarray(a, dtype=np.float32))
    cst = host_consts(cfg)
    shared = {k: f(inp[k]) for k in _WNAMES}
    shared.update(cst)
    nb = inp["x_prompt"].shape[0]
    maps = []
    for c in range(ncores):
        b = c % nb
        s0 = NSS * c
        m = dict(shared)
        m["x_all"] = f(np.concatenate([inp["x_prompt"][b], np.asarray(inp["x_sample"][s0:s0 + NSS]).reshape(-1, cfg["D"])], 0))
        m["c_all"] = f(np.concatenate([inp["c_prompt"][b:b + 1], inp["c_sample"][s0:s0 + NSS]], 0))
        m["cache_ka"] = f(np.asarray(inp["cache_diff_k"][:, s0:s0 + NSS]).reshape(DEPTH, NSS, PAST, 1024))
        m["cache_va"] = f(np.asarray(inp["cache_diff_v"][:, s0:s0 + NSS]).reshape(DEPTH, NSS, PAST, 1024))
        m["cache_kb"] = f(np.asarray(inp["cache_dsa_k"][:, s0:s0 + NSS]).reshape(DEPTH, NSS, PAST, 256))
        m["cache_vb"] = f(np.asarray(inp["cache_dsa_v"][:, s0:s0 + NSS]).reshape(DEPTH, NSS, PAST, 256))
        m["cache_ki"] = f(np.asarray(inp["cache_dsa_kidx"][:, s0:s0 + NSS]))
        m["st_cconv"] = f(np.asarray(inp["state_gdn_conv"][:, s0:s0 + NSS]))
        m["st_gdn"] = f(np.asarray(inp["state_gdn"][:, s0:s0 + NSS]))
        m["st_fconv"] = f(np.asarray(inp["state_ffn_conv"][:, s0:s0 + NSS]))
        maps.append(m)
    return maps


def host_gather(cfg, res, ncores, nb):
    SEQ, NSS, DSEQ, DEPTH, D, DFF = cfg["SEQ"], cfg["NSS"], cfg["DSEQ"], cfg["DEPTH"], cfg["D"], cfg["DFF"]
    R = [r for r in res]

    def P(name, fn):
        return np.stack([fn(R[b][name]) for b in range(nb)], axis=0)

    def S(name, fn):
        return np.concatenate([fn(R[c][name]) for c in range(ncores)], axis=0)

    y_p = P("y_all", lambda a: a[:SEQ])
    y_s = S("y_all", lambda a: a[SEQ:].reshape(NSS, DSEQ, D))
    outs = [y_p, y_s]

    def tok(name, shp_tail):
        p = np.stack([R[b][name][:, :SEQ] for b in range(nb)], axis=1).reshape((DEPTH, nb, SEQ) + shp_tail)
        s = np.concatenate([R[c][name][:, SEQ:].reshape((DEPTH, NSS, DSEQ) + shp_tail) for c in range(ncores)], axis=1)
        return p, s

    def seqo(name):
        p = np.stack([R[b][name][:, 0] for b in range(nb)], axis=1)
        s = np.concatenate([R[c][name][:, 1:] for c in range(ncores)], axis=1)
        return p, s

    ka = tok("ka_out", (4, 2, 128))
    va = tok("va_out", (4, 256))
    kb_ = tok("kb_out", (2, 128))
    vb = tok("vb_out", (2, 128))
    ki = tok("ki_out", (64,))
    cc = seqo("cconv_out")
    gs = seqo("gdn_out")
    fc = seqo("fconv_out")
    allp = [ka, va, kb_, vb, ki, cc, gs, fc]
    outs += [a[0] for a in allp] + [a[1] for a in allp]
    return tuple(np.ascontiguousarray(o, dtype=np.float32) for o in outs)


_CACHE = {}


def run_cfg(cfg, inputs, ncores=8, trace=False):
    key = tuple(sorted((k, str(v)) for k, v in cfg.items()))
    if key not in _CACHE:
        _CACHE[key] = build(cfg)[0]
    nc = _CACHE[key]
    maps = host_inmaps(cfg, inputs, ncores)
    res = run_bass_kernel_spmd(nc, maps, core_ids=list(range(ncores)), trace=trace)
    return host_gather(cfg, res.results, ncores, inputs["x_prompt"].shape[0]), res


def kernel(**inputs):
    outs, _ = run_cfg(CFG_FULL, inputs, 8)
    return outs
```

```python
import math
from contextlib import ExitStack

import numpy as np
import concourse.bass as bass
import concourse.mybir as mybir
from concourse.bass_utils import run_bass_kernel_spmd

F32 = mybir.dt.float32
BF16 = mybir.dt.bfloat16
AF = mybir.ActivationFunctionType
ALU = mybir.AluOpType
AX = mybir.AxisListType

EPS = 1e-6
NEG = -1.0e30
NDMA = 8
MASK_ENG = "pool"
SCR_Q = "act"
USE_SCR = True

CFG_FULL = dict(D=2048, SEQ=4096, DFF=5632, PAST=2048, DSEQ=32, NSS=4, DEPTH=2)


class Reg:
    __slots__ = ("w", "r", "f", "name")

    def __init__(self, name=""):
        self.w = {}
        self.r = {}
        self.f = {}
        self.name = name


class T:
    __slots__ = ("h", "g")

    def __init__(self, h, g=None):
        self.h = h
        self.g = g if g is not None else Reg()

    def __getitem__(self, k):
        return self.h[k]


class KB:
    def __init__(self, nc, es):
        self.nc = nc
        self.es = es
        self.eng = dict(pe=nc.tensor, dve=nc.vector, act=nc.scalar, pool=nc.gpsimd, sp=nc.sync)
        self.sems = []
        self.esem = {}
        self.ecnt = {}
        for e in ("pe", "dve", "act", "pool"):
            self.esem[e] = self.new_sem("e_" + e)
            self.ecnt[e] = 0
        self.known = {e: {} for e in self.eng}
        self.dq = {}
        for q in ("sp", "pool", "act"):
            self.dq[q] = dict(sems=[self.new_sem("d_%s%d" % (q, i)) for i in range(NDMA)], n=0)
        self.uid = 0
        self.nins = 0

    def new_sem(self, name):
        h = self.es.enter_context(self.nc.semaphore(name))
        self.sems.append(h)
        return len(self.sems) - 1

    def name(self, p):
        self.uid += 1
        return "%s_%d" % (p, self.uid)

    def sb(self, st, name, shape, dt):
        return T(st.enter_context(self.nc.sbuf_tensor(self.name(name), list(shape), dt)))

    def ps(self, st, name, shape, dt=F32):
        return T(st.enter_context(self.nc.psum_tensor(self.name(name), list(shape), dt)))

    def dram(self, name, shape, dt, kind="Internal"):
        return T(self.nc.dram_tensor(name, list(shape), dt, kind=kind).ap())

    def _wait(self, e, evs):
        kn = self.known[e]
        pes = self.esem["pe"]
        for s, v in evs.items():
            if kn.get(s, 0) >= v:
                continue
            if e == "pe" and s == pes:
                continue
            self.eng[e].wait_ge(self.sems[s], v)
            kn[s] = v
            self.nins += 1

    @staticmethod
    def _deps(reads, writes, partial):
        evs = {}
        for r in reads:
            for s, v in r.g.w.items():
                if evs.get(s, 0) < v:
                    evs[s] = v
        for w in writes:
            for s, v in w.g.r.items():
                if evs.get(s, 0) < v:
                    evs[s] = v
            for s, v in (w.g.f if partial else w.g.w).items():
                if evs.get(s, 0) < v:
                    evs[s] = v
        return evs

    @staticmethod
    def _mark(reads, writes, partial, s, c):
        for r in reads:
            if r.g.r.get(s, 0) < c:
                r.g.r[s] = c
        for w in writes:
            if partial:
                if w.g.w.get(s, 0) < c:
                    w.g.w[s] = c
            else:
                w.g.w = {s: c}
                w.g.f = {s: c}
                w.g.r = {}

    def op(self, e, fn, R=(), W=(), partial=False):
        self._wait(e, self._deps(R, W, partial))
        ins = fn(self.eng[e])
        self.ecnt[e] += 1
        c = self.ecnt[e]
        s = self.esem[e]
        ins.then_inc(self.sems[s], 1)
        self.nins += 1
        self._mark(R, W, partial, s, c)

    def dma(self, q, out, in_, R=(), W=(), partial=False, slow=False):
        evs = self._deps(R, W, partial)
        dq = self.dq[q]
        n = dq["n"]
        K = len(dq["sems"])
        s = dq["sems"][n % K]
        v = 16 * (n // K + 1)
        if n >= K and evs.get(s, 0) < v - 16:
            evs[s] = v - 16
        self._wait(q, evs)
        if slow:
            ins = self.eng[q].dma_start(out=out, in_=in_, allow_slow_non_contiguous=True)
        else:
            ins = self.eng[q].dma_start(out=out, in_=in_)
        ins.then_inc(self.sems[s], 16)
        dq["n"] = n + 1
        self.nins += 1
        self._mark(R, W, partial, s, v)

    def barrier(self):
        evs = {}
        for e in ("pe", "dve", "act", "pool"):
            if self.ecnt[e]:
                evs[self.esem[e]] = self.ecnt[e]
        for q, dq in self.dq.items():
            n = dq["n"]
            K = len(dq["sems"])
            for i in range(min(n, K)):
                last = ((n - 1 - i) // K) * K + i
                evs[dq["sems"][i]] = 16 * (last // K + 1)
        for e in ("pe", "dve", "act", "pool", "sp"):
            ev = dict(evs)
            if e in self.esem:
                ev.pop(self.esem[e], None)
            self._wait(e, ev)
        for e in ("dve", "act", "pool"):
            if self.ecnt[e]:
                self._wait(e, {self.esem[e]: self.ecnt[e]})
        if self.ecnt["pe"]:
            self.eng["pe"].wait_ge(self.sems[self.esem["pe"]], self.ecnt["pe"])
            self.known["pe"][self.esem["pe"]] = self.ecnt["pe"]

    def mm(self, out, lhsT, rhs, start, stop, R, W):
        self.op("pe", lambda e: e.matmul(out, lhsT, rhs, start=start, stop=stop), R, W, partial=True)

    def tr(self, out, in_, ident, R, W):
        self.op("pe", lambda e: e.transpose(out, in_, ident), R, W, partial=True)

    def tt(self, eng, out, in0, in1, op, R, W, partial=False):
        self.op(eng, lambda e: e.tensor_tensor(out=out, in0=in0, in1=in1, op=op), R, W, partial)

    def ts(self, eng, out, in0, s1, op0, R, W, s2=None, op1=None, partial=False, accum=None):
        def f(e):
            kw = {}
            if op1 is not None:
                kw["op1"] = op1
            if accum is not None:
                kw["accum_out"] = accum
            return e.tensor_scalar(out=out, in0=in0, scalar1=s1, scalar2=s2, op0=op0, **kw)
        self.op(eng, f, R, W, partial)

    def stt(self, eng, out, in0, scalar, in1, op0, op1, R, W, partial=False):
        self.op(eng, lambda e: e.scalar_tensor_tensor(out=out, in0=in0, scalar=scalar, in1=in1,
                                                      op0=op0, op1=op1), R, W, partial)

    def act(self, out, in_, func, R, W, scale=None, bias=None, accum=None, partial=False):
        def f(e):
            kw = {}
            if scale is not None:
                kw["scale"] = scale
            if bias is not None:
                kw["bias"] = bias
            if accum is not None:
                kw["accum_out"] = accum
            return e.activation(out=out, in_=in_, func=func, **kw)
        self.op("act", f, R, W, partial)

    def cp(self, eng, out, in_, R, W, partial=False):
        if eng == "act":
            self.op("act", lambda e: e.copy(out=out, in_=in_), R, W, partial)
        else:
            self.op(eng, lambda e: e.tensor_copy(out=out, in_=in_), R, W, partial)

    def rsqrt(self, out, in_, scale, eps, R, W, partial=False):
        self.act(out, in_, AF.Sqrt, R, W, scale=scale, bias=eps, partial=partial)
        self.op("dve", lambda e: e.reciprocal(out=out, in_=out), W, W, partial=partial)

    def memset(self, eng, ap, val, W, partial=False):
        self.op(eng, lambda e: e.memset(ap, val), (), W, partial)


class WStream:
    def __init__(self, kb, bufs, loads, q="pool", sc=None, mode="cast"):
        self.kb, self.bufs, self.loads, self.q = kb, bufs, loads, q
        self.sc, self.mode = sc, mode
        self.issued = 0

    def get(self, i, keep=0):
        nb = len(self.bufs)
        while self.issued < min(len(self.loads), i - keep + nb):
            k = self.issued
            b = self.bufs[k % nb]
            out_ap, in_ap = self.loads[k](b)
            if self.mode == "scr":
                sc_ap, _ = self.loads[k](self.sc[k])
                self.kb.dma(SCR_Q, out_ap, sc_ap, R=(self.sc[k],), W=(b,))
            else:
                self.kb.dma(self.q, out_ap, in_ap, R=(), W=(b,))
                if self.mode == "first":
                    sc_ap, _ = self.loads[k](self.sc[k])
                    self.kb.dma("sp", sc_ap, out_ap, R=(b,), W=(self.sc[k],))
            self.issued += 1
        return self.bufs[i % nb]


def col_layout(D):
    o = {}
    c = 0
    for nme, n in (("aq", 1024), ("ak", 1024), ("av", 1024), ("bq", 1024), ("bk", 256), ("bv", 256),
                   ("bqi", 512), ("bki", 64), ("bwi", 8), ("cqkv", 3072), ("ca", 8), ("cb", 8),
                   ("cz", 1024), ("gt", 3 * D)):
        o[nme] = c
        c += n
    o["DIN"] = c
    return o


def build(cfg):
    D, SEQ, DFF, PAST, DSEQ, NSS, DEPTH = (cfg[k] for k in ("D", "SEQ", "DFF", "PAST", "DSEQ", "NSS", "DEPTH"))
    assert DSEQ * NSS == 128 and SEQ % 512 == 0 and PAST % 128 == 0 and DFF % 512 == 0 and D % 512 == 0
    KC = D // 128
    NTP = SEQ // 128
    NT = NTP + 1
    TT_ = SEQ + 128
    NG = SEQ // 512 + 1
    FC = DFF // 128
    CO = col_layout(D)
    DIN = CO["DIN"]
    NSEQ = 1 + NSS
    NCH = SEQ // 64 + NSS
    NPT = PAST // 128
    TOPK_P = min(256, SEQ // 4)
    TOPK_S = min(256, (PAST + DSEQ) // 4)
    assert TOPK_P % 8 == 0 and TOPK_S % 8 == 0

    nc = bass.Bass("TRN2", target_bir_lowering=False)
    es = ExitStack()
    kb = KB(nc, es)

    def din(name, shape, dt=F32):
        return T(nc.dram_tensor(name, list(shape), dt, kind="ExternalInput").ap())

    def dout(name, shape, dt=F32):
        return T(nc.dram_tensor(name, list(shape), dt, kind="ExternalOutput").ap())

    x_all = din("x_all", [TT_, D])
    c_all = din("c_all", [NSEQ, D])
    cache_ka = din("cache_ka", [DEPTH, NSS, PAST, 1024])
    cache_va = din("cache_va", [DEPTH, NSS, PAST, 1024])
    cache_kb = din("cache_kb", [DEPTH, NSS, PAST, 256])
    cache_vb = din("cache_vb", [DEPTH, NSS, PAST, 256])
    cache_ki = din("cache_ki", [DEPTH, NSS, PAST, 64])
    st_cconv = din("st_cconv", [DEPTH, NSS, 3, 3072])
    st_gdn = din("st_gdn", [DEPTH, NSS, 8, 128, 128])
    st_fconv = din("st_fconv", [DEPTH, NSS, 2, 2 * DFF])
    w_ada = din("w_ada", [DEPTH, D, 6 * D])
    b_ada = din("b_ada", [DEPTH, 6 * D])
    norm_mix = din("norm_mix", [DEPTH, D])
    w_in = din("w_in", [DEPTH, D, DIN])
    a_q_norm = din("a_q_norm", [DEPTH, 128])
    a_k_norm = din("a_k_norm", [DEPTH, 128])
    a_lambda = din("a_lambda", [DEPTH, 4, 128])
    a_subln = din("a_subln", [DEPTH, 256])
    b_q_norm = din("b_q_norm", [DEPTH, 128])
    b_k_norm = din("b_k_norm", [DEPTH, 128])
    c_conv = din("c_conv", [DEPTH, 4, 3072])
    c_a_log = din("c_a_log", [DEPTH, 8])
    c_dt_bias = din("c_dt_bias", [DEPTH, 8])
    c_out_norm = din("c_out_norm", [DEPTH, 128])
    w_branch = din("w_branch", [DEPTH, 3, 1024, D])
    w_out = din("w_out", [DEPTH, D, D])
    norm_ffn = din("norm_ffn", [DEPTH, D])
    w_up = din("w_up", [DEPTH, D, 2 * DFF])
    ffn_conv = din("ffn_conv", [DEPTH, 3, 2 * DFF])
    w_down = din("w_down", [DEPTH, DFF, D])
    rope64 = din("rope64", [TT_, 128])
    rope32 = din("rope32", [TT_, 64])
    cmat = din("cmat", [128, 640])
    y_all = dout("y_all", [TT_, D])
    ka_out = dout("ka_out", [DEPTH, TT_, 1024])
    va_out = dout("va_out", [DEPTH, TT_, 1024])
    kb_out = dout("kb_out", [DEPTH, TT_, 256])
    vb_out = dout("vb_out", [DEPTH, TT_, 256])
    ki_out = dout("ki_out", [DEPTH, TT_, 64])
    cconv_out = dout("cconv_out", [DEPTH, NSEQ, 3, 3072])
    gdn_out = dout("gdn_out", [DEPTH, NSEQ, 8, 128, 128])
    fconv_out = dout("fconv_out", [DEPTH, NSEQ, 2, 2 * DFF])
    XT = kb.dram("XT", [KC, 128, TT_], F32)
    QA = kb.dram("QA", [NT, 128, 8, 128], BF16)
    KA = kb.dram("KA", [NT, 128, 8, 128], BF16)
    KAc = kb.dram("KAc", [NSS, max(NPT, 1), 128, 8, 128], BF16)
    QB = kb.dram("QB", [NT, 128, 8, 128], BF16)
    KBs = kb.dram("KBs", [2, 128, TT_], BF16)
    KBc = kb.dram("KBc", [NSS, 2, 128, max(PAST, 128)], BF16)
    QI = kb.dram("QI", [NT, 128, 4, 128], BF16)
    KI2 = kb.dram("KI2", [128, TT_], BF16)
    KIc = kb.dram("KIc", [NSS, 128, max(PAST, 128)], BF16)
    CQKV = kb.dram("CQKV", [24, 128, TT_], BF16)
    CZs = kb.dram("CZs", [NCH, 64, 1024], BF16)
    GT = kb.dram("GT", [3 * KC, 128, TT_], BF16)
    BR = kb.dram("BR", [24, 128, TT_], BF16, kind="ExternalOutput" if cfg.get("debug") else "Internal")

    NB1 = 11 + 3 + 6 + 3 * D // 512
    NB3 = 3 * (D // 512) + D // 512 + 2 * (DFF // 512) + (D // 512) * (FC // (11 if FC % 11 == 0 else 4))
    WS1 = [nc.dram_tensor("WS1_%d" % l, [NB1, 128, KC, 512], BF16, kind="Internal").ap() for l in range(DEPTH)]
    WS3 = [nc.dram_tensor("WS3_%d" % l, [NB3, 128, max(KC, 11), 512], BF16, kind="Internal").ap() for l in range(DEPTH)]

    top = ExitStack()
    es.enter_context(top)

    cm = kb.sb(top, "cm", [128, 640], F32)
    kb.dma("sp", cm[:], cmat[:, :], W=(cm,))
    identf = cm[:, 0:128]
    onesf = cm[:, 128:256]
    UTi = cm[:, 256:384]
    UTs = cm[:, 384:512]
    LTs = cm[:, 512:640]
    cmb = kb.sb(top, "cmb", [128, 256], BF16)
    kb.cp("dve", cmb[:], cm[:, 0:256], R=(cm,), W=(cmb,))
    identb = cmb[:, 0:128]
    onesb = cmb[:, 128:256]
    zb = kb.sb(top, "zb", [128, 128], BF16)
    kb.memset("pool", zb[:], 0.0, W=(zb,))

    seqs = [(0, SEQ, 0, None)] + [(SEQ + DSEQ * s, DSEQ, PAST, s) for s in range(NSS)]
    groups = [(512 * g, 512, [(0, 512, 0)]) for g in range(SEQ // 512)]
    groups.append((SEQ, 128, [(DSEQ * s, DSEQ, 1 + s) for s in range(NSS)]))

    MOD = [kb.sb(top, "mod", [128, 6 * KC, NSEQ], F32) for _ in range(DEPTH)]
    A1 = [kb.sb(top, "a1", [128, KC, NSEQ], F32) for _ in range(DEPTH)]
    A2 = [kb.sb(top, "a2", [128, KC, NSEQ], F32) for _ in range(DEPTH)]
    GBR = kb.sb(top, "gbr", [64, NCH, 16], F32)
    SGN = kb.sb(top, "sgn", [128, NT, 8], F32)

    def pbc(ap1d, n):
        return ap1d.partition_broadcast(128)

    def phase0():
        st = ExitStack()
        xin = [kb.sb(st, "xin", [128, D], F32) for _ in range(2)]
        xts = [kb.sb(st, "xts", [128, KC, 128], F32) for _ in range(2)]
        pt = [kb.ps(st, "p0t", [128, 4, 128], F32) for _ in range(2)]
        n = 0
        for t in range(NT):
            xi = xin[t % 2]
            xo = xts[t % 2]
            kb.dma("sp", xi[:], x_all[t * 128:(t + 1) * 128, :], W=(xi,))
            for k4 in range(KC // 4):
                p = pt[n % 2]
                n += 1
                for j in range(4):
                    kc = k4 * 4 + j
                    kb.tr(p[:, j, :], xi[:, kc * 128:(kc + 1) * 128], identf, R=(xi, cm), W=(p,))
                kb.cp("act" if k4 % 2 else "dve", xo[:, k4 * 4:k4 * 4 + 4, :], p[:], R=(p,), W=(xo,), partial=True)
            kb.dma("sp", XT[:, :, t * 128:(t + 1) * 128].rearrange("k p t -> p k t"), xo[:], R=(xo,), W=(XT,),
                   partial=True)
        cT = kb.sb(st, "cT", [128, KC, NSEQ], F32)
        for kc in range(KC):
            kb.dma("sp", cT[:, kc, :], c_all[:, kc * 128:(kc + 1) * 128].rearrange("b p -> p b"), W=(cT,), partial=True,
                   slow=True)
        sg = kb.sb(st, "csg", [128, KC, NSEQ], F32)
        kb.act(sg[:], cT[:], AF.Sigmoid, R=(cT,), W=(sg,))
        cs = kb.sb(st, "cs", [128, KC, NSEQ], BF16)
        kb.tt("dve", cs[:], cT[:], sg[:], ALU.mult, R=(cT, sg), W=(cs,))
        wb = [kb.sb(st, "wada", [128, KC, 512], BF16) for _ in range(3)]
        pm = [kb.ps(st, "pmod", [128, 8], F32) for _ in range(2)]
        bT = kb.sb(st, "bT", [128, 6 * KC], F32)
        nm = kb.sb(st, "nm", [128, KC], F32)
        n = 0
        for l in range(DEPTH):
            kb.dma("sp", bT[:], b_ada[l, :].rearrange("(k p) -> p k", p=128), W=(bT,), slow=True)
            nblk = 6 * D // 512
            loads = [(lambda b, i=i: (b[:], w_ada[l, :, i * 512:(i + 1) * 512].rearrange("(k p) c -> p k c", p=128)))
                     for i in range(nblk)]
            ws = WStream(kb, wb, loads)
            for i in range(nblk):
                w = ws.get(i)
                for j in range(4):
                    ec = i * 4 + j
                    p = pm[n % 2]
                    n += 1
                    for kc in range(KC):
                        kb.mm(p[:, 0:NSEQ], w[:, kc, j * 128:(j + 1) * 128], cs[:, kc, :], kc == 0, kc == KC - 1,
                              R=(w, cs), W=(p,))
                    kb.ts("dve", MOD[l][:, ec, :], p[:, 0:NSEQ], bT[:, ec:ec + 1], ALU.add, R=(p, bT), W=(MOD[l],),
                          partial=True)
            for (nrm, A, off) in ((norm_mix, A1[l], KC), (norm_ffn, A2[l], 4 * KC)):
                kb.dma("sp", nm[:], nrm[l, :].rearrange("(k p) -> p k", p=128), W=(nm,), slow=True)
                kb.ts("dve", A[:], MOD[l][:, off:off + KC, :], 1.0, ALU.add, R=(MOD[l],), W=(A,))
                kb.tt("dve", A[:], A[:], nm[:].unsqueeze(2).to_broadcast([128, KC, NSEQ]), ALU.mult, R=(A, nm), W=(A,))
        kb.barrier()
        st.close()

    def phase1(l):
        st = ExitStack()
        xT = kb.sb(st, "xT", [128, KC, 512], F32)
        hT = kb.sb(st, "hT", [128, KC, 512], BF16)
        sqb = [kb.sb(st, "sqb", [128, 512], F32) for _ in range(2)]
        rstd = kb.sb(st, "rstd", [128, 512], F32)
        tmpx = [kb.sb(st, "tmpx", [128, 512], F32) for _ in range(2)]
        wb = [kb.sb(st, "w1", [128, KC, 512], BF16) for _ in range(3)]
        gains = {}
        for nme, src in (("aq", a_q_norm), ("ak", a_k_norm), ("bq", b_q_norm), ("bk", b_k_norm)):
            g = kb.sb(st, "g" + nme, [128, 128], F32)
            kb.dma("sp", g[:], src[l, :].partition_broadcast(128), W=(g,))
            gains[nme] = g
        NEA = kb.sb(st, "nea", [128, 8], F32)
        DTB = kb.sb(st, "dtb", [128, 8], F32)
        kb.dma("sp", NEA[:], c_a_log[l, :].partition_broadcast(128), W=(NEA,))
        kb.dma("sp", DTB[:], c_dt_bias[l, :].partition_broadcast(128), W=(DTB,))
        kb.act(NEA[:], NEA[:], AF.Exp, R=(NEA,), W=(NEA,))
        kb.ts("dve", NEA[:], NEA[:], -1.0, ALU.mult, R=(NEA,), W=(NEA,))
        CW = kb.sb(st, "cw", [128, 24, 4], F32)
        for tap in range(4):
            kb.dma("sp", CW[:, :, tap], c_conv[l, tap, :].rearrange("(k p) -> p k", p=128), W=(CW,), partial=True,
                   slow=True)
        HISTC = kb.sb(st, "histc", [128, 24, 3], F32)
        kb.memset("pool", HISTC[:], 0.0, W=(HISTC,))
        SH = kb.sb(st, "shist", [128, NSS, 24, 3], F32)
        for s in range(NSS):
            for j in range(24):
                kb.dma("sp", SH[:, s, j, :], st_cconv[l, s, :, j * 128:(j + 1) * 128].rearrange("t p -> p t"), W=(SH,),
                       partial=True, slow=True)
        R64g = kb.sb(st, "r64", [128, 4, 128], F32)
        R32g = kb.sb(st, "r32", [128, 4, 64], F32)
        tA = [kb.sb(st, "tA", [128, 512], F32) for _ in range(2)]
        tB = [kb.sb(st, "tB", [128, 512], F32) for _ in range(2)]
        tC = [kb.sb(st, "tC", [128, 512], F32) for _ in range(2)]
        tO = [kb.sb(st, "tO", [128, 512], F32) for _ in range(3)]
        ob = [kb.sb(st, "ob", [128, 512], BF16) for _ in range(2)]
        ssm = [kb.sb(st, "ssm", [128, 16], F32) for _ in range(2)]
        QAT = [kb.sb(st, "qat", [128, 8, 128], BF16) for _ in range(4)]
        KAT = [kb.sb(st, "kat", [128, 8, 128], BF16) for _ in range(4)]
        QBT = [kb.sb(st, "qbt", [128, 8, 128], BF16) for _ in range(4)]
        KBT = [kb.sb(st, "kbt", [128, 2, 128], BF16) for _ in range(4)]
        QIT = [kb.sb(st, "qit", [128, 4, 128], BF16) for _ in range(4)]
        KIT = [kb.sb(st, "kit", [128, 128], BF16) for _ in range(4)]
        AW = [kb.sb(st, "aw", [128, 8], F32) for _ in range(4)]
        ext = [kb.sb(st, "ext", [128, 4 * 35 if False else 515], F32) for _ in range(2)]
        gst = [kb.sb(st, "gst", [128, 4, 512], BF16) for _ in range(2)]
        czs = [kb.sb(st, "czs", [128, 512], BF16) for _ in range(3)]
        gtmp = [kb.sb(st, "gtmp", [64, 16], F32) for _ in range(2)]
        pA = [kb.ps(st, "pA", [128, 512], F32) for _ in range(2)]
        pT = [kb.ps(st, "pT", [128, 8, 128], BF16) for _ in range(2)]
        pS = kb.ps(st, "pS", [128, 512], F32)
        pF = [kb.ps(st, "pF", [128, 512], F32) for _ in range(2)]
        pL = kb.ps(st, "pL", [128, 512], F32)
        cnt = dict(a=0, t=0, f=0, w=0, o=0)

        def nxt(k, lst):
            cnt[k] += 1
            return lst[cnt[k] % len(lst)]

        def normrope(src, srcT, nh, gain, rope, half, out, outT, wi, ti):
            hd = 2 * half
            a, b, c_, sm = tA[wi], tB[wi], tC[wi], ssm[wi]
            av = a[:, 0:nh * hd].rearrange("p (h d) -> p h d", d=hd)
            bv = b[:, 0:nh * hd].rearrange("p (h d) -> p h d", d=hd)
            cv = c_[:, 0:nh * hd].rearrange("p (h d) -> p h d", d=hd)
            if gain is not None:
                kb.act(av, src, AF.Square, R=(srcT,), W=(a,))
                kb.op("dve", lambda e: e.tensor_reduce(out=sm[:, 0:nh], in_=av, axis=AX.X, op=ALU.add), R=(a,), W=(sm,))
                kb.rsqrt(sm[:, 0:nh], sm[:, 0:nh], 1.0 / hd, EPS, R=(sm,), W=(sm,))
                kb.tt("dve", bv, src, sm[:, 0:nh].unsqueeze(2).to_broadcast([128, nh, hd]), ALU.mult, R=(srcT, sm), W=(b,))
                kb.tt("dve", bv, bv, gain[:, 0:hd].unsqueeze(1).to_broadcast([128, nh, hd]), ALU.mult, R=(b, gain), W=(b,))
            else:
                kb.cp("act", bv, src, R=(srcT,), W=(b,))
            cosb = rope[:, ti, 0:half].unsqueeze(1).to_broadcast([128, nh, half])
            sinb = rope[:, ti, half:hd].unsqueeze(1).to_broadcast([128, nh, half])
            x1, x2 = bv[:, :, 0:half], bv[:, :, half:hd]
            kb.tt("dve", out[:, :, 0:half], x1, cosb, ALU.mult, R=(b, rope), W=(outT,))
            kb.tt("dve", cv[:, :, 0:half], x2, sinb, ALU.mult, R=(b, rope), W=(c_,))
            kb.tt("dve", out[:, :, 0:half], out[:, :, 0:half], cv[:, :, 0:half], ALU.subtract, R=(outT, c_), W=(outT,))
            kb.tt("dve", out[:, :, half:hd], x2, cosb, ALU.mult, R=(b, rope), W=(outT,), partial=True)
            kb.tt("dve", cv[:, :, half:hd], x1, sinb, ALU.mult, R=(b, rope), W=(c_,))
            kb.tt("dve", out[:, :, half:hd], out[:, :, half:hd], cv[:, :, half:hd], ALU.add, R=(outT, c_), W=(outT,))

        def transposes(srcb, nblk, dst, dst0):
            p = nxt("t", pT)
            for i in range(nblk):
                kb.tr(p[:, i, :], srcb[:, i * 128:(i + 1) * 128], identb, R=(srcb, cmb), W=(p,))
            kb.cp("act", dst[:, dst0:dst0 + nblk, :], p[:, 0:nblk, :], R=(p,), W=(dst,), partial=True)

        for gi, (tok0, ntok, segs) in enumerate(groups):
            ntile = ntok // 128
            kb.dma("sp", R64g[:, 0:ntile, :], rope64[tok0:tok0 + ntok, :].rearrange("(t p) c -> p t c", p=128), W=(R64g,))
            kb.dma("sp", R32g[:, 0:ntile, :], rope32[tok0:tok0 + ntok, :].rearrange("(t p) c -> p t c", p=128), W=(R32g,))
            kb.dma("sp", xT[:, :, 0:ntok], XT[:, :, tok0:tok0 + ntok].rearrange("k p t -> p k t"), R=(XT,), W=(xT,))
            for kc in range(KC):
                sq = sqb[kc % 2]
                kb.act(sq[:, 0:ntok], xT[:, kc, 0:ntok], AF.Square, R=(xT,), W=(sq,))
                kb.mm(pS[:, 0:ntok], onesf, sq[:, 0:ntok], kc == 0, kc == KC - 1, R=(cm, sq), W=(pS,))
            kb.rsqrt(rstd[:, 0:ntok], pS[:, 0:ntok], 1.0 / D, EPS, R=(pS,), W=(rstd,))
            for kc in range(KC):
                tx = tmpx[kc % 2]
                kb.tt("dve", tx[:, 0:ntok], xT[:, kc, 0:ntok], rstd[:, 0:ntok], ALU.mult, R=(xT, rstd), W=(tx,))
                for (c0, n, sq_) in segs:
                    kb.ts("pool", hT[:, kc, c0:c0 + n], tx[:, c0:c0 + n], A1[l][:, kc, sq_:sq_ + 1], ALU.mult,
                          R=(tx, A1[l], MOD[l]), W=(hT,), s2=MOD[l][:, kc, sq_:sq_ + 1], op1=ALU.add, partial=True)

            tm_blocks = []
            for nme, nb in (("aq", 2), ("ak", 2), ("av", 2), ("bq", 2)):
                for j in range(nb):
                    tm_blocks.append((nme, j, CO[nme] + 512 * j, 512))
            tm_blocks.append(("bkv", 0, CO["bk"], 512))
            tm_blocks.append(("bix", 0, CO["bki"], 72))
            tm_blocks.append(("bqi", 0, CO["bqi"], 512))
            ch_blocks = [("cab", 0, CO["ca"], 16), ("cz", 0, CO["cz"], 512), ("cz", 1, CO["cz"] + 512, 512)]
            fm_blocks = [("cqkv", j, CO["cqkv"] + 512 * j, 512) for j in range(6)]
            fm_blocks += [("gt", j, CO["gt"] + 512 * j, 512) for j in range(3 * D // 512)]
            blocks = tm_blocks + ch_blocks + fm_blocks
            loads = [(lambda b, c0=c0, ncol=ncol: (b[:, :, 0:ncol],
                                                    w_in[l, :, c0:c0 + ncol].rearrange("(k p) c -> p k c", p=128)))
                     for (_, _, c0, ncol) in blocks]
            assert len(blocks) == NB1
            if gi == 0:
                sc1 = [T(WS1[l][k]) for k in range(NB1)]
            ws = WStream(kb, wb, loads, sc=sc1, mode=("first" if gi == 0 else "scr") if USE_SCR else "cast")

            pend = []

            def flush():
                while pend:
                    pend.pop(0)()

            for bi, (kind, j, c0, ncol) in enumerate(blocks):
                w = ws.get(bi)
                if bi < len(tm_blocks):
                    for ti in range(ntile):
                        tglob = tok0 // 128 + ti
                        r0 = tok0 + ti * 128
                        r64, r32 = R64g, R32g
                        p = nxt("a", pA)
                        for kc in range(KC):
                            kb.mm(p[:, 0:ncol], hT[:, kc, ti * 128:(ti + 1) * 128], w[:, kc, 0:ncol], kc == 0, kc == KC - 1,
                                  R=(hT, w), W=(p,))
                        flush()
                        wi = cnt["w"] = (cnt["w"] + 1) % 2
                        o = nxt("o", tO)
                        obf = ob[wi]
                        p3 = p[:, :].rearrange("p (h d) -> p h d", d=128)
                        o3 = o[:, :].rearrange("p (h d) -> p h d", d=128)
                        if kind in ("aq", "ak", "bq"):
                            normrope(p3, p, 4, gains[kind], r64, 64, o3, o, wi, ti)
                            kb.cp("act", obf[:], o[:], R=(o,), W=(obf,))
                            dstl = dict(aq=QAT, ak=KAT, bq=QBT)[kind]
                            dst = dstl[ti]
                            if kind == "ak":
                                kb.dma("sp", ka_out[l, r0:r0 + 128, 512 * j:512 * j + 512], o[:], R=(o,), W=(ka_out,), partial=True)

                            def tail(obf=obf, dst=dst, j=j, kind=kind, tglob=tglob):
                                transposes(obf, 4, dst, 4 * j)
                                if j == 1:
                                    dd = dict(aq=QA, ak=KA, bq=QB)[kind]
                                    kb.dma("sp", dd[tglob], dst[:], R=(dst,), W=(dd,), partial=True)
                            pend.append(tail)
                        elif kind == "av":
                            kb.cp("act", o[:], p[:], R=(p,), W=(o,))
                            kb.dma("sp", va_out[l, r0:r0 + 128, 512 * j:512 * j + 512], o[:], R=(o,), W=(va_out,), partial=True)
                        elif kind == "bkv":
                            normrope(p3[:, 0:2, :], p, 2, gains["bk"], r64, 64, o3[:, 0:2, :], o, wi, ti)
                            kb.cp("act", o[:, 256:512], p[:, 256:512], R=(p,), W=(o,), partial=True)
                            kb.cp("act", obf[:, 0:256], o[:, 0:256], R=(o,), W=(obf,))
                            dst = KBT[ti]
                            kb.dma("sp", kb_out[l, r0:r0 + 128, :], o[:, 0:256], R=(o,), W=(kb_out,), partial=True)
                            kb.dma("sp", vb_out[l, r0:r0 + 128, :], o[:, 256:512], R=(o,), W=(vb_out,), partial=True)

                            def tail(obf=obf, dst=dst, r0=r0):
                                transposes(obf, 2, dst, 0)
                                kb.dma("sp", KBs[:, :, r0:r0 + 128].rearrange("n p t -> p n t"), dst[:], R=(dst,), W=(KBs,), partial=True)
                            pend.append(tail)
                        elif kind == "bix":
                            pk = p[:, 0:64].rearrange("p (h d) -> p h d", d=64)
                            ok = o[:, 0:64].rearrange("p (h d) -> p h d", d=64)
                            normrope(pk, p, 1, None, r32, 32, ok, o, wi, ti)
                            kb.dma("sp", ki_out[l, r0:r0 + 128, :], o[:, 0:64], R=(o,), W=(ki_out,), partial=True)
                            kb.cp("act", obf[:, 0:64], o[:, 0:64], R=(o,), W=(obf,))
                            kb.cp("act", obf[:, 64:128], o[:, 0:64], R=(o,), W=(obf,), partial=True)
                            aw = AW[ti]
                            kb.act(SGN[:, tglob, :], p[:, 64:72], AF.Sign, R=(p,), W=(SGN,), partial=True)
                            kb.act(aw[:], p[:, 64:72], AF.Abs, R=(p,), W=(aw,), scale=(8.0 ** -0.5) * 0.125)
                            dst = KIT[ti]

                            def tail(obf=obf, dst=dst, r0=r0):
                                pp = nxt("t", pT)
                                kb.tr(pp[:, 0, :], obf[:, 0:128], identb, R=(obf, cmb), W=(pp,))
                                kb.cp("act", dst[:], pp[:, 0, :], R=(pp,), W=(dst,))
                                kb.dma("sp", KI2[:, r0:r0 + 128], dst[:], R=(dst,), W=(KI2,), partial=True)
                            pend.append(tail)
                        elif kind == "bqi":
                            p8 = p[:, :].rearrange("p (h d) -> p h d", d=64)
                            o8 = o[:, :].rearrange("p (h d) -> p h d", d=64)
                            normrope(p8, p, 8, None, r32, 32, o8, o, wi, ti)
                            aw = AW[ti]
                            ob8 = obf[:, :].rearrange("p (h d) -> p h d", d=64)
                            kb.tt("dve", ob8, o8, aw[:].unsqueeze(2).to_broadcast([128, 8, 64]), ALU.mult, R=(o, aw), W=(obf,))
                            dst = QIT[ti]

                            def tail(obf=obf, dst=dst, tglob=tglob):
                                transposes(obf, 4, dst, 0)
                                kb.dma("sp", QI[tglob], dst[:], R=(dst,), W=(QI,), partial=True)
                            pend.append(tail)
                elif bi < len(tm_blocks) + len(ch_blocks):
                    flush()
                    if kind == "cab":
                        for (sc0, sn, sq_) in segs:
                            C = 64 if sq_ == 0 else DSEQ
                            for ci in range(sn // C):
                                col = sc0 + ci * C
                                chg = (tok0 + col) // 64 if sq_ == 0 else SEQ // 64 + (sq_ - 1)
                                p = nxt("a", pA)
                                for kc in range(KC):
                                    kb.mm(p[0:C, 0:ncol], hT[:, kc, col:col + C], w[:, kc, 0:ncol], kc == 0, kc == KC - 1,
                                          R=(hT, w), W=(p,))
                                gt_ = nxt("f", gtmp)
                                kb.tt("dve", gt_[0:C, 0:8], p[0:C, 0:8], DTB[0:C, :], ALU.add, R=(p, DTB), W=(gt_,))
                                kb.act(gt_[0:C, 0:8], gt_[0:C, 0:8], AF.Exp, R=(gt_,), W=(gt_,))
                                kb.act(gt_[0:C, 0:8], gt_[0:C, 0:8], AF.Ln, R=(gt_,), W=(gt_,), bias=1.0)
                                kb.tt("dve", GBR[0:C, chg, 0:8], gt_[0:C, 0:8], NEA[0:C, :], ALU.mult, R=(gt_, NEA), W=(GBR,), partial=True)
                                kb.act(GBR[0:C, chg, 8:16], p[0:C, 8:16], AF.Sigmoid, R=(p,), W=(GBR,), partial=True)
                    else:
                        for ti in range(ntile):
                            p = nxt("a", pA)
                            for kc in range(KC):
                                kb.mm(p[:, 0:ncol], hT[:, kc, ti * 128:(ti + 1) * 128], w[:, kc, 0:ncol], kc == 0, kc == KC - 1,
                                      R=(hT, w), W=(p,))
                            cz_ = nxt("f", czs)
                            kb.act(cz_[:, :], p[:, :], AF.Silu, R=(p,), W=(cz_,))
                            for (sc0, sn, sq_) in segs:
                                C = 64 if sq_ == 0 else DSEQ
                                for ci in range(sn // C):
                                    col = sc0 + ci * C
                                    if col < ti * 128 or col >= (ti + 1) * 128:
                                        continue
                                    chg = (tok0 + col) // 64 if sq_ == 0 else SEQ // 64 + (sq_ - 1)
                                    rr0 = col - ti * 128
                                    kb.dma("sp", CZs[chg, 0:C, 512 * j:512 * j + 512], cz_[rr0:rr0 + C, :], R=(cz_,), W=(CZs,), partial=True)
                else:
                    g4 = gst[bi % 2]
                    for q in range(4):
                        p = nxt("f", pF)
                        for kc in range(KC):
                            kb.mm(p[:, 0:ntok], w[:, kc, q * 128:(q + 1) * 128], hT[:, kc, 0:ntok], kc == 0, kc == KC - 1,
                                  R=(w, hT), W=(p,))
                        flush()
                        if kind == "gt":
                            kb.act(g4[:, q, 0:ntok], p[:, 0:ntok], AF.Sigmoid, R=(p,), W=(g4,), partial=(q > 0))
                        else:
                            fj = 4 * j + q
                            e_ = ext[fj % 2]
                            y_ = tA[fj % 2]
                            s_ = tB[fj % 2]
                            for (sc0, sn, sq_) in segs:
                                if sq_ == 0:
                                    hist = HISTC[:, fj, :]
                                    hT_ = HISTC
                                else:
                                    hist = SH[:, sq_ - 1, fj, :]
                                    hT_ = SH
                                kb.cp("pool", e_[:, 0:3], hist, R=(hT_,), W=(e_,))
                                kb.cp("act", e_[:, 3:3 + sn], p[:, sc0:sc0 + sn], R=(p,), W=(e_,), partial=True)
                                kb.ts("dve", y_[:, sc0:sc0 + sn], e_[:, 0:sn], CW[:, fj, 0:1], ALU.mult, R=(e_, CW), W=(y_,),
                                      partial=(sc0 > 0))
                                for tap in range(1, 4):
                                    kb.stt("dve", y_[:, sc0:sc0 + sn], e_[:, tap:tap + sn], CW[:, fj, tap:tap + 1],
                                           y_[:, sc0:sc0 + sn], ALU.mult, ALU.add, R=(e_, CW, y_), W=(y_,), partial=True)
                                kb.cp("pool", hist, e_[:, sn:sn + 3], R=(e_,), W=(hT_,), partial=True)
                            kb.act(s_[:, 0:ntok], y_[:, 0:ntok], AF.Silu, R=(y_,), W=(s_,))
                            if fj < 16:
                                kb.tt("dve", y_[:, 0:ntok], s_[:, 0:ntok], s_[:, 0:ntok], ALU.mult, R=(s_,), W=(y_,))

                                def tail(y_=y_, s_=s_, g4=g4, q=q, fj=fj):
                                    kb.mm(pL[:, 0:ntok], onesf, y_[:, 0:ntok], True, True, R=(cm, y_), W=(pL,))
                                    kb.rsqrt(y_[:, 0:ntok], pL[:, 0:ntok], 1.0, EPS, R=(pL,), W=(y_,))
                                    kb.stt("dve", g4[:, q, 0:ntok], s_[:, 0:ntok], (128.0 ** -0.5) if fj < 8 else 1.0, y_[:, 0:ntok],
                                           ALU.mult, ALU.mult, R=(s_, y_), W=(g4,), partial=(q > 0))
                                pend.append(tail)
                            else:
                                kb.cp("act", g4[:, q, 0:ntok], s_[:, 0:ntok], R=(s_,), W=(g4,), partial=(q > 0))
                    dd = GT if kind == "gt" else CQKV

                    def tail_st(dd=dd, j=j, g4=g4):
                        kb.dma("sp", dd[4 * j:4 * j + 4, :, tok0:tok0 + ntok].rearrange("c p t -> p c t"), g4[:, :, 0:ntok], R=(g4,),
                               W=(dd,), partial=True)
                    pend.append(tail_st)
            flush()
        for sq_ in range(NSEQ):
            for fj in range(24):
                src = HISTC[:, fj, :] if sq_ == 0 else SH[:, sq_ - 1, fj, :]
                kb.dma("sp", cconv_out[l, sq_, :, fj * 128:(fj + 1) * 128].rearrange("t p -> p t"), src,
                       R=(HISTC if sq_ == 0 else SH,), W=(cconv_out,), partial=True, slow=True)
        kb.barrier()
        st.close()

    def phase2_cache(l):
        if NPT == 0:
            return
        st = ExitStack()
        cin = [kb.sb(st, "cin", [128, 1024 + 256 + 128], BF16) for _ in range(2)]
        cka = [kb.sb(st, "cka", [128, 8, 128], BF16) for _ in range(2)]
        ckb = [kb.sb(st, "ckb", [128, 2, 128], BF16) for _ in range(2)]
        cki = [kb.sb(st, "cki", [128, 128], BF16) for _ in range(2)]
        pT = [kb.ps(st, "pTc", [128, 8, 128], BF16) for _ in range(3)]
        n = 0
        for s in range(NSS):
            for pt in range(NPT):
                ci = cin[n % 2]
                a, b, c_ = cka[n % 2], ckb[n % 2], cki[n % 2]
                n += 1
                r0 = pt * 128
                kb.dma("pool", ci[:, 0:1024], cache_ka[l, s, r0:r0 + 128, :], W=(ci,))
                kb.dma("pool", ci[:, 1024:1280], cache_kb[l, s, r0:r0 + 128, :], W=(ci,), partial=True)
                kb.dma("pool", ci[:, 1280:1344], cache_ki[l, s, r0:r0 + 128, :], W=(ci,), partial=True)
                kb.dma("pool", ci[:, 1344:1408], cache_ki[l, s, r0:r0 + 128, :], W=(ci,), partial=True)
                p = pT[0]
                for i in range(8):
                    kb.tr(p[:, i, :], ci[:, i * 128:(i + 1) * 128], identb, R=(ci, cmb), W=(p,))
                kb.cp("act", a[:], p[:], R=(p,), W=(a,))
                p = pT[1]
                for i in range(2):
                    kb.tr(p[:, i, :], ci[:, 1024 + i * 128:1024 + (i + 1) * 128], identb, R=(ci, cmb), W=(p,))
                kb.cp("dve", b[:], p[:, 0:2, :], R=(p,), W=(b,))
                p = pT[2]
                kb.tr(p[:, 0, :], ci[:, 1280:1408], identb, R=(ci, cmb), W=(p,))
                kb.cp("dve", c_[:], p[:, 0, :], R=(p,), W=(c_,))
                kb.dma("sp", KAc[s, pt], a[:], R=(a,), W=(KAc,), partial=True)
                kb.dma("sp", KBc[s, :, :, r0:r0 + 128].rearrange("n p t -> p n t"), b[:], R=(b,), W=(KBc,), partial=True)
                kb.dma("sp", KIc[s, :, r0:r0 + 128], c_[:], R=(c_,), W=(KIc,), partial=True)
        kb.barrier()
        st.close()

    def phase2a(l):
        st = ExitStack()
        lam_init = 0.8 - 0.6 * math.exp(-0.3 * l)
        lmb = kb.sb(st, "lmb", [128, 4, 128], F32)
        kb.dma("sp", lmb[:].rearrange("p a d -> p (a d)"), a_lambda[l].rearrange("a d -> (a d)").partition_broadcast(128), W=(lmb,))
        lt = kb.sb(st, "lt", [128, 2, 128], F32)
        l2 = kb.sb(st, "l2", [128, 2], F32)
        nlam = kb.sb(st, "nlam", [128, 1], F32)
        kb.tt("dve", lt[:, 0, :], lmb[:, 0, :], lmb[:, 1, :], ALU.mult, R=(lmb,), W=(lt,))
        kb.tt("dve", lt[:, 1, :], lmb[:, 2, :], lmb[:, 3, :], ALU.mult, R=(lmb,), W=(lt,), partial=True)
        kb.op("dve", lambda e: e.tensor_reduce(out=l2[:], in_=lt[:], axis=AX.X, op=ALU.add), R=(lt,), W=(l2,))
        kb.act(l2[:], l2[:], AF.Exp, R=(l2,), W=(l2,))
        kb.tt("dve", nlam[:], l2[:, 1:2], l2[:, 0:1], ALU.subtract, R=(l2,), W=(nlam,))
        kb.ts("dve", nlam[:], nlam[:], -lam_init, ALU.add, R=(nlam,), W=(nlam,))
        SUB = kb.sb(st, "subln", [128, 2], F32)
        for ec in range(2):
            kb.dma("sp", SUB[:, ec:ec + 1], a_subln[l, ec * 128:(ec + 1) * 128].rearrange("(p o) -> p o", o=1), W=(SUB,),
                   partial=True)
        kb.ts("dve", SUB[:], SUB[:], 1.0 - lam_init, ALU.mult, R=(SUB,), W=(SUB,))

        qt = [kb.sb(st, "qa", [128, 8, 128], BF16) for _ in range(2)]
        kt = [kb.sb(st, "ka", [128, 4, 128], BF16) for _ in range(3)]
        vt = [kb.sb(st, "va", [128, 512], BF16) for _ in range(3)]
        PT = [kb.sb(st, "pt", [128, 4, 128], BF16) for _ in range(2)]
        rd = kb.sb(st, "rd", [128, 4, 128], F32)
        t0_ = kb.sb(st, "t0", [128, 2, 2, 128], F32)
        t1_ = kb.sb(st, "t1", [128, 2, 2, 128], F32)
        sq_ = kb.sb(st, "sq", [128, 2, 2, 128], F32)
        rs = kb.sb(st, "rs", [128, 2, 128], F32)
        oa = [kb.sb(st, "oa", [128, 4, 128], BF16) for _ in range(2)]
        pSc = [kb.ps(st, "pSc", [128, 4, 128], F32) for _ in range(2)]
        pO = kb.ps(st, "pO", [128, 4, 2, 128], F32)
        pD = kb.ps(st, "pD", [128, 4, 128], F32)
        pN = kb.ps(st, "pN", [128, 2, 128], F32)
        scale = 128.0 ** -0.5
        nk_ = 0
        no = 0
        for (tok0, Tn, Pn, sidx) in seqs:
            nqt = max(Tn // 128, 1)
            for qi in range(nqt):
                nq = min(128, Tn)
                tile_g = (tok0 // 128) if sidx is None else NTP
                qc0 = 0 if sidx is None else DSEQ * sidx
                q = qt[(qi + (0 if sidx is None else sidx)) % 2]
                kb.dma("sp", q[:], QA[tile_g if sidx is not None else qi], R=(QA,), W=(q,))
                keys = []
                if sidx is None:
                    for j in range(qi + 1):
                        keys.append(("p", j, 128, j == qi))
                else:
                    for j in range(NPT):
                        keys.append(("c", j, 128, False))
                    keys.append(("n", 0, DSEQ, False))
                for hp in range(2):
                    pO2 = pO[:, :, :, :].rearrange("p a b c -> p (a b c)")
                    kb.mm(pO2[:, 0:512], zb[:], q[:, 0:4, :], True, False, R=(zb, q), W=(pO,))
                    kb.mm(pO2[:, 512:1024], zb[:], q[:, 0:4, :], True, False, R=(zb, q), W=(pO,))
                    kb.mm(pD[:, :, :], zb[:], q[:, 0:4, :], True, False, R=(zb, q), W=(pD,))
                    def emit_qk(ki_):
                        nonlocal nk_
                        kk, j, nk, diag = keys[ki_]
                        k_ = kt[nk_ % 3]
                        v_ = vt[nk_ % 3]
                        nk_ += 1
                        if kk == "p":
                            kb.dma("sp", k_[:], KA[j, :, 4 * hp:4 * hp + 4, :], R=(KA,), W=(k_,))
                            kb.dma("pool", v_[:], va_out[l, j * 128:(j + 1) * 128, 512 * hp:512 * hp + 512], R=(va_out,), W=(v_,))
                        elif kk == "c":
                            kb.dma("sp", k_[:], KAc[sidx, j, :, 4 * hp:4 * hp + 4, :], R=(KAc,), W=(k_,))
                            kb.dma("pool", v_[:], cache_va[l, sidx, j * 128:(j + 1) * 128, 512 * hp:512 * hp + 512], W=(v_,))
                        else:
                            kb.dma("sp", k_[:, :, 0:nk], KA[NTP, :, 4 * hp:4 * hp + 4, qc0:qc0 + nk], R=(KA,), W=(k_,))
                            kb.dma("pool", v_[0:nk, :], va_out[l, tok0:tok0 + nk, 512 * hp:512 * hp + 512], R=(va_out,), W=(v_,))
                        ps = pSc[nk_ % 2]
                        p_ = PT[nk_ % 2]
                        for hmi in range(4):
                            kb.mm(ps[0:nk, hmi, 0:nq], k_[:, hmi, 0:nk], q[:, 4 * hp + hmi, qc0:qc0 + nq], True, True,
                                  R=(k_, q), W=(ps,))
                        kb.act(p_[0:nk, :, 0:nq], ps[0:nk, :, 0:nq], AF.Exp, R=(ps,), W=(p_,), scale=scale)
                        if diag:
                            kb.memset("pool", p_[64:128, :, 0:64], 0.0, W=(p_,), partial=True)
                        return v_, p_

                    nxt_vp = emit_qk(0)
                    for ki_, (kk, j, nk, diag) in enumerate(keys):
                        v_, p_ = nxt_vp
                        if ki_ + 1 < len(keys):
                            nxt_vp = emit_qk(ki_ + 1)
                        first = ki_ == 0
                        last = ki_ == len(keys) - 1
                        for hmi in range(4):
                            for ec in range(2):
                                c0 = (hmi // 2) * 256 + ec * 128
                                kb.mm(pO[:, hmi, ec, 0:nq], v_[0:nk, c0:c0 + 128], p_[0:nk, hmi, 0:nq], False, last,
                                      R=(v_, p_), W=(pO,))
                            kb.mm(pD[:, hmi, 0:nq], onesb[0:nk, :], p_[0:nk, hmi, 0:nq], False, last, R=(cmb, p_), W=(pD,))
                    kb.op("dve", lambda e: e.reciprocal(out=rd[:, :, 0:nq], in_=pD[:, :, 0:nq]), R=(pD,), W=(rd,))
                    pO5 = pO[:, :, :, :].rearrange("p (h m) e q -> p h m e q", m=2)
                    rd4 = rd[:, :, :].rearrange("p (h m) q -> p h m q", m=2)
                    for h2 in range(2):
                        kb.tt("dve", t0_[:, h2, :, 0:nq], pO5[:, h2, 0, :, 0:nq],
                              rd4[:, h2, 0, 0:nq].unsqueeze(1).to_broadcast([128, 2, nq]), ALU.mult, R=(pO, rd), W=(t0_,),
                              partial=(h2 > 0))
                        kb.stt("dve", t1_[:, h2, :, 0:nq], pO5[:, h2, 1, :, 0:nq], nlam[:, 0:1],
                               rd4[:, h2, 1, 0:nq].unsqueeze(1).to_broadcast([128, 2, nq]), ALU.mult, ALU.mult, R=(pO, rd, nlam),
                               W=(t1_,), partial=(h2 > 0))
                    for h2 in range(2):
                        kb.tt("dve", t0_[:, h2, :, 0:nq], t0_[:, h2, :, 0:nq], t1_[:, h2, :, 0:nq], ALU.add, R=(t0_, t1_), W=(t0_,),
                              partial=(h2 > 0))
                        kb.act(sq_[:, h2, :, 0:nq], t0_[:, h2, :, 0:nq], AF.Square, R=(t0_,), W=(sq_,), partial=(h2 > 0))
                    for h2 in range(2):
                        for ec in range(2):
                            kb.mm(pN[:, h2, 0:nq], onesf, sq_[:, h2, ec, 0:nq], ec == 0, ec == 1, R=(cm, sq_), W=(pN,))
                    kb.rsqrt(rs[:, :, 0:nq], pN[:, :, 0:nq], 1.0 / 256, EPS, R=(pN,), W=(rs,))
                    for h2 in range(2):
                        kb.tt("dve", t0_[:, h2, :, 0:nq], t0_[:, h2, :, 0:nq],
                              rs[:, h2, 0:nq].unsqueeze(1).to_broadcast([128, 2, nq]), ALU.mult, R=(t0_, rs), W=(t0_,),
                              partial=(h2 > 0))
                    o_ = oa[no % 2]
                    no += 1
                    o4 = o_[:, :, :].rearrange("p (h e) q -> p h e q", e=2)
                    for ec in range(2):
                        kb.ts("dve", o4[:, :, ec, 0:nq], t0_[:, :, ec, 0:nq], SUB[:, ec:ec + 1], ALU.mult, R=(t0_, SUB), W=(o_,),
                              partial=(ec > 0))
                    qtok = tok0 + qi * 128
                    kb.dma("sp", BR[4 * hp:4 * hp + 4, :, qtok:qtok + nq].rearrange("c p t -> p c t"), o_[:, :, 0:nq], R=(o_,),
                           W=(BR,), partial=True)
        kb.barrier()
        st.close()

    def phase2b(l):
        st = ExitStack()
        LMAX = max(SEQ, PAST + DSEQ)
        LT_ = (LMAX + 127) // 128
        KIT = kb.sb(st, "kit", [128, LT_ * 128], BF16)
        KBT = kb.sb(st, "kbt", [128, 2, LT_ * 128], BF16)
        VB = kb.sb(st, "vb", [128, LT_, 256], BF16)
        qi_ = [kb.sb(st, "qi", [128, 4, 128], BF16) for _ in range(2)]
        qb_ = [kb.sb(st, "qb", [128, 8, 128], BF16) for _ in range(2)]
        score = kb.sb(st, "score", [128, LT_ * 128], F32)
        work = kb.sb(st, "work", [128, LT_ * 128], F32)
        mask = kb.sb(st, "mask", [128, LT_ * 128], BF16)
        MT = kb.sb(st, "mt", [128, LT_, 128], BF16)
        rr = [kb.sb(st, "rr", [128, 512], F32) for _ in range(2)]
        m8 = kb.sb(st, "m8", [128, 8], F32)
        thr = kb.sb(st, "thr", [128, 1], F32)
        sg0 = [kb.sb(st, "sg0", [128, 8], F32) for _ in range(2)]
        E_ = [kb.sb(st, "E", [128, 4, 128], BF16) for _ in range(2)]
        P_ = [kb.sb(st, "P", [128, 4, 128], BF16) for _ in range(2)]
        rdn = kb.sb(st, "rdn", [128, 4, 128], F32)
        ob_ = [kb.sb(st, "obo", [128, 4, 128], BF16) for _ in range(2)]
        pI = [kb.ps(st, "pI", [128, 512], F32) for _ in range(2)]
        pM = kb.ps(st, "pM", [128, 8, 128], BF16)
        pSb = [kb.ps(st, "pSb", [128, 4, 128], F32) for _ in range(2)]
        pOb = kb.ps(st, "pOb", [128, 4, 128], F32)
        pDb = kb.ps(st, "pDb", [128, 4, 128], F32)
        scale = 128.0 ** -0.5
        ni = 0
        nb_ = 0
        for (tok0, Tn, Pn, sidx) in seqs:
            L = Pn + Tn
            nkt = (L + 127) // 128
            if sidx is None:
                kb.dma("sp", KIT[:, 0:L], KI2[:, 0:L], R=(KI2,), W=(KIT,))
                kb.dma("sp", KBT[:, :, 0:L], KBs[:, :, 0:L].rearrange("n p t -> p n t"), R=(KBs,), W=(KBT,))
                kb.dma("pool", VB[:, 0:nkt, :], vb_out[l, 0:L, :].rearrange("(j p) c -> p j c", p=128), R=(vb_out,), W=(VB,))
            else:
                if Pn:
                    kb.dma("sp", KIT[:, 0:Pn], KIc[sidx, :, 0:Pn], R=(KIc,), W=(KIT,))
                    kb.dma("sp", KBT[:, :, 0:Pn], KBc[sidx, :, :, 0:Pn].rearrange("n p t -> p n t"), R=(KBc,), W=(KBT,))
                    kb.dma("pool", VB[:, 0:Pn // 128, :], cache_vb[l, sidx, :, :].rearrange("(j p) c -> p j c", p=128), W=(VB,))
                kb.dma("sp", KIT[:, Pn:L], KI2[:, tok0:tok0 + Tn], R=(KI2,), W=(KIT,), partial=True)
                kb.dma("sp", KBT[:, :, Pn:L], KBs[:, :, tok0:tok0 + Tn].rearrange("n p t -> p n t"), R=(KBs,), W=(KBT,), partial=True)
                kb.dma("pool", VB[0:Tn, Pn // 128, :], vb_out[l, tok0:tok0 + Tn, :], R=(vb_out,), W=(VB,), partial=True)
            nqt = max(Tn // 128, 1)
            topk = TOPK_P if sidx is None else TOPK_S
            for qi in range(nqt):
                nq = min(128, Tn)
                tile_g = qi if sidx is None else NTP
                qc0 = 0 if sidx is None else DSEQ * sidx
                Lv = 128 * (qi + 1) if sidx is None else L
                nvt = (Lv + 127) // 128
                qI = qi_[ni % 2]
                qB = qb_[ni % 2]
                ni += 1
                kb.dma("sp", qI[:], QI[tile_g], R=(QI,), W=(qI,))
                kb.dma("sp", qB[:], QB[tile_g], R=(QB,), W=(qB,))
                sgt = sg0[ni % 2]
                kb.dma("sp", sgt[0:nq, :], SGN[qc0:qc0 + nq, tile_g, :], R=(SGN,), W=(sgt,))
                for k0 in range(0, Lv, 512):
                    kn = min(512, Lv - k0)
                    for h in range(8):
                        p = pI[(nb_) % 2]
                        r_ = rr[nb_ % 2]
                        nb_ += 1
                        pb = (h % 2) * 64
                        kb.mm(p[0:nq, 0:kn], qI[pb:pb + 64, h // 2, qc0:qc0 + nq], KIT[pb:pb + 64, k0:k0 + kn], True, True,
                              R=(qI, KIT), W=(p,))
                        kb.act(r_[0:nq, 0:kn], p[0:nq, 0:kn], AF.Relu, R=(p,), W=(r_,))
                        sg = sgt[0:nq, h:h + 1]
                        if h == 0:
                            kb.ts("dve", score[0:nq, k0:k0 + kn], r_[0:nq, 0:kn], sg, ALU.mult, R=(r_, sgt), W=(score,),
                                  partial=True)
                        else:
                            kb.stt("dve", score[0:nq, k0:k0 + kn], r_[0:nq, 0:kn], sg, score[0:nq, k0:k0 + kn], ALU.mult, ALU.add,
                                   R=(r_, sgt, score), W=(score,), partial=True)
                if sidx is None:
                    kb.memset("dve", score[0:64, Lv - 64:Lv], NEG, W=(score,), partial=True)
                nvis_min = (Lv - 64) if sidx is None else Lv
                if nvis_min > topk:
                    src = score
                    for r in range(topk // 8):
                        kb.op("dve", lambda e, src=src: e.max(out=m8[0:nq, :], in_=src[0:nq, 0:Lv]), R=(src,), W=(m8,))
                        if r < topk // 8 - 1:
                            kb.op("dve", lambda e, src=src: e.match_replace(out=work[0:nq, 0:Lv], in_to_replace=m8[0:nq, :],
                                                                            in_values=src[0:nq, 0:Lv], imm_value=NEG),
                                  R=(src, m8), W=(work,))
                            src = work
                    kb.cp("dve", thr[0:nq, :], m8[0:nq, 7:8], R=(m8,), W=(thr,))
                else:
                    kb.memset("dve", thr[0:nq, :], -1.0e29, W=(thr,))
                kb.ts("dve", mask[0:nq, 0:Lv], score[0:nq, 0:Lv], thr[0:nq, 0:1], ALU.is_ge, R=(score, thr), W=(mask,))
                for j0 in range(0, nvt, 8):
                    jn = min(8, nvt - j0)
                    for jj in range(jn):
                        j = j0 + jj
                        nk = min(128, Lv - j * 128)
                        kb.tr(pM[0:nk, jj, 0:nq], mask[0:nq, j * 128:j * 128 + nk], identb[0:nq, 0:nq], R=(mask, cmb), W=(pM,))
                    nkl = min(128, Lv - (j0 + jn - 1) * 128)
                    if nkl == 128:
                        kb.cp("act", MT[:, j0:j0 + jn, 0:nq], pM[:, 0:jn, 0:nq], R=(pM,), W=(MT,), partial=True)
                    else:
                        if jn > 1:
                            kb.cp("act", MT[:, j0:j0 + jn - 1, 0:nq], pM[:, 0:jn - 1, 0:nq], R=(pM,), W=(MT,), partial=True)
                        kb.cp("act", MT[0:nkl, j0 + jn - 1, 0:nq], pM[0:nkl, jn - 1, 0:nq], R=(pM,), W=(MT,), partial=True)
                steps = [(n, j) for n in range(2) for j in range(nvt)]
                nb0 = nb_
                nb_ += len(steps)

                def emit_qk(s_):
                    n, j = steps[s_]
                    nk = min(128, Lv - j * 128)
                    ps = pSb[(nb0 + s_) % 2]
                    e_ = E_[(nb0 + s_) % 2]
                    p_ = P_[(nb0 + s_) % 2]
                    kb.mm(ps[0:nk, :, 0:nq], KBT[:, n, j * 128:j * 128 + nk], qB[:, 4 * n:4 * n + 4, qc0:qc0 + nq], True, True,
                          R=(KBT, qB), W=(ps,))
                    kb.act(e_[0:nk, :, 0:nq], ps[0:nk, :, 0:nq], AF.Exp, R=(ps,), W=(e_,), scale=scale)
                    kb.tt(MASK_ENG, p_[0:nk, :, 0:nq], e_[0:nk, :, 0:nq],
                          MT[0:nk, j, 0:nq].unsqueeze(1).to_broadcast([nk, 4, nq]), ALU.mult, R=(e_, MT), W=(p_,))

                emit_qk(0)
                for s_, (n, j) in enumerate(steps):
                    if s_ + 1 < len(steps):
                        emit_qk(s_ + 1)
                    nk = min(128, Lv - j * 128)
                    p_ = P_[(nb0 + s_) % 2]
                    kb.mm(pOb[:, :, 0:nq], VB[0:nk, j, n * 128:(n + 1) * 128], p_[0:nk, :, 0:nq], j == 0, j == nvt - 1,
                          R=(VB, p_), W=(pOb,))
                    kb.mm(pDb[:, :, 0:nq], onesb[0:nk, :], p_[0:nk, :, 0:nq], j == 0, j == nvt - 1, R=(cmb, p_), W=(pDb,))
                    if j == nvt - 1:
                        kb.op("dve", lambda e: e.reciprocal(out=rdn[:, :, 0:nq], in_=pDb[:, :, 0:nq]), R=(pDb,), W=(rdn,))
                        o_ = ob_[n % 2]
                        kb.tt("dve", o_[:, :, 0:nq], pOb[:, :, 0:nq], rdn[:, :, 0:nq], ALU.mult, R=(pOb, rdn), W=(o_,))
                        qtok = tok0 + qi * 128
                        kb.dma("sp", BR[8 + 4 * n:8 + 4 * n + 4, :, qtok:qtok + nq].rearrange("c p t -> p c t"), o_[:, :, 0:nq], R=(o_,),
                               W=(BR,), partial=True)
        kb.barrier()
        st.close()

    def phase2c(l):
        st = ExitStack()
        COUT = kb.sb(st, "cout", [128, 128], F32)
        kb.dma("sp", COUT[:], c_out_norm[l, :].partition_broadcast(128), W=(COUT,))
        S = kb.sb(st, "S", [128, 8, 128], F32)
        Sb = kb.sb(st, "Sb", [128, 8, 128], BF16)
        qkv = [kb.sb(st, "qkv", [128, 24, 512], BF16) for _ in range(2)]
        czt = [kb.sb(st, "czt", [64, 1024], BF16) for _ in range(2)]
        octs = [kb.sb(st, "octs", [128, 8, 512], BF16) for _ in range(2)]

        def f32t(nm, shape, n=1):
            return [kb.sb(st, nm, shape, F32) for _ in range(n)]
        Ug, Ib, X, Xm, Xp = (f32t(k, [64, 8, 64])[0] for k in ("Ug", "Ib", "X", "Xm", "Xp"))
        gTi, gTs, gLs = (f32t(k, [64, 8, 64])[0] for k in ("gTi", "gTs", "gLs"))
        Gc = f32t("Gc", [64, 8])[0]
        egc = f32t("egc", [64, 8])[0]
        dk = f32t("dk", [64, 8])[0]
        bge = f32t("bge", [64, 8])[0]
        eG128 = f32t("eG128", [128, 8, 64])[0]
        egl = f32t("egl", [128, 8])[0]
        Pm = f32t("Pm", [64, 8, 64], 2)
        PTm = f32t("PTm", [64, 8, 64], 2)
        Rm = f32t("Rm", [64, 8, 64])[0]
        tmp = f32t("tmp", [64, 8, 64])[0]
        QKg = kb.sb(st, "QKg", [64, 8, 64], BF16)
        bv = f32t("bv", [64, 8, 128])[0]
        bgk = f32t("bgk", [64, 8, 128])[0]
        kd = kb.sb(st, "kd", [64, 8, 128], BF16)
        Usb = f32t("Usb", [64, 8, 128])[0]
        wTb = kb.sb(st, "wTb", [128, 8, 64], BF16)
        qgT = kb.sb(st, "qgT", [128, 8, 64], BF16)
        vnb = kb.sb(st, "vnb", [64, 8, 128], BF16)
        osb = f32t("osb", [64, 8, 128])[0]
        osq = f32t("osq", [64, 8, 128])[0]
        ors = f32t("ors", [64, 8])[0]
        onb = kb.sb(st, "onb", [64, 8, 128], BF16)
        Stmp = f32t("Stmp", [128, 8, 128])[0]
        b0 = kb.ps(st, "b0", [128, 8, 64], F32)
        b1 = kb.ps(st, "b1", [128, 8, 64], F32)
        b2 = kb.ps(st, "b2", [128, 8, 64], F32)
        b3 = kb.ps(st, "b3", [128, 8, 64], F32)
        b4 = kb.ps(st, "b4", [128, 8, 64], F32)
        b5 = kb.ps(st, "b5", [128, 8, 128], BF16)
        b6 = kb.ps(st, "b6", [128, 8, 128], BF16)
        b7 = kb.ps(st, "b7", [128, 4, 128], F32)
        b3u = b3
        nch = 0
        for si, (tok0, Tn, Pn, sidx) in enumerate(seqs):
            C = 64 if sidx is None else DSEQ
            nsteps = int(math.log2(C)) - 1
            if sidx is None:
                kb.memset("pool", S[:], 0.0, W=(S,))
                kb.memset("pool", Sb[:], 0.0, W=(Sb,))
            else:
                kb.dma("sp", S[:], st_gdn[l, sidx].rearrange("h k v -> k h v"), W=(S,))
                kb.cp("act", Sb[:], S[:], R=(S,), W=(Sb,))
            nchunk = Tn // C
            for ci in range(nchunk):
                t0 = tok0 + ci * C
                chg = (t0 // 64) if sidx is None else SEQ // 64 + sidx
                if sidx is None:
                    if ci % 8 == 0:
                        qb_ = qkv[(ci // 8) % 2]
                        kb.dma("sp", qb_[:], CQKV[:, :, t0:t0 + 512].rearrange("c p t -> p c t"), R=(CQKV,), W=(qb_,))
                    co = (ci % 8) * 64
                else:
                    qb_ = qkv[nch % 2]
                    kb.dma("sp", qb_[:, :, 0:C], CQKV[:, :, t0:t0 + C].rearrange("c p t -> p c t"), R=(CQKV,), W=(qb_,))
                    co = 0
                nch += 1
                qT = qb_[:, 0:8, co:co + C]
                kT = qb_[:, 8:16, co:co + C]
                vT = qb_[:, 16:24, co:co + C]
                cz_ = czt[nch % 2]
                kb.dma("sp", cz_[0:C, :], CZs[chg, 0:C, :], R=(CZs,), W=(cz_,))
                g = GBR[0:C, chg, 0:8]
                beta = GBR[0:C, chg, 8:16]

                def bch(ap2, n=C):
                    return ap2.unsqueeze(2).to_broadcast([C, 8, n])

                def bcm(ap2):
                    return ap2.unsqueeze(1).to_broadcast([C, 8, C])
                kb.tt("dve", Ug[0:C, :, 0:C], bcm(UTi[0:C, 0:C]), bch(g), ALU.mult, R=(cm, GBR), W=(Ug,))
                kb.tt("dve", Ib[0:C, :, 0:C], bcm(identf[0:C, 0:C]), bch(beta), ALU.mult, R=(cm, GBR), W=(Ib,))
                kb.mm(b0[:, :, 0:C], onesf[0:C, :], Ug[0:C, :, 0:C], True, True, R=(cm, Ug), W=(b0,))
                kb.mm(b1[0:C, :, 0:C], onesf[0:C, 0:C], Ib[0:C, :, 0:C], True, True, R=(cm, Ib), W=(b1,))
                kb.mm(b2[0:C, 0, 0:8], UTi[0:C, 0:C], g, True, True, R=(cm, GBR), W=(b2,))
                kb.mm(b2[:, 1, 0:8], onesf[0:C, :], g, True, True, R=(cm, GBR), W=(b2,))
                kb.cp("act", Gc[0:C, :], b2[0:C, 0, 0:8], R=(b2,), W=(Gc,))
                kb.tt("dve", X[0:C, :, 0:C], b0[0:C, :, 0:C], bch(Gc[0:C, :]), ALU.subtract, R=(b0, Gc), W=(X,))
                kb.ts("dve", Xm[0:C, :, 0:C], X[0:C, :, 0:C], 0.0, ALU.min, R=(X,), W=(Xm,))
                kb.ts("pool", Xp[0:C, :, 0:C], X[0:C, :, 0:C], 0.0, ALU.max, R=(X,), W=(Xp,))
                kb.act(Xm[0:C, :, 0:C], Xm[0:C, :, 0:C], AF.Exp, R=(Xm,), W=(Xm,))
                kb.act(Xp[0:C, :, 0:C], Xp[0:C, :, 0:C], AF.Exp, R=(Xp,), W=(Xp,), scale=-1.0)
                kb.tt("dve", gTi[0:C, :, 0:C], Xm[0:C, :, 0:C], bcm(UTi[0:C, 0:C]), ALU.mult, R=(Xm, cm), W=(gTi,))
                kb.tt("dve", gTs[0:C, :, 0:C], Xm[0:C, :, 0:C], bcm(UTs[0:C, 0:C]), ALU.mult, R=(Xm, cm), W=(gTs,))
                kb.tt("dve", gLs[0:C, :, 0:C], Xp[0:C, :, 0:C], bcm(LTs[0:C, 0:C]), ALU.mult, R=(Xp, cm), W=(gLs,))
                kb.act(eG128[:, :, 0:C], b0[:, :, 0:C], AF.Exp, R=(b0,), W=(eG128,))
                kb.act(egc[0:C, :], Gc[0:C, :], AF.Exp, R=(Gc,), W=(egc,))
                kb.act(egl[:, :], b2[:, 1, 0:8], AF.Exp, R=(b2,), W=(egl,))
                kb.tt("dve", dk[0:C, :], b2[0:C, 1, 0:8], Gc[0:C, :], ALU.subtract, R=(b2, Gc), W=(dk,))
                kb.act(dk[0:C, :], dk[0:C, :], AF.Exp, R=(dk,), W=(dk,))
                kb.tt("dve", bge[0:C, :], beta, egc[0:C, :], ALU.mult, R=(GBR, egc), W=(bge,))
                for h in range(8):
                    kb.mm(b3[0:C, h, 0:C], kT[:, h, :], kT[:, h, :], True, True, R=(qb_,), W=(b3,))
                for h in range(8):
                    kb.mm(b4[0:C, h, 0:C], kT[:, h, :], qT[:, h, :], True, True, R=(qb_,), W=(b4,))
                for h in range(8):
                    kb.tr(b5[0:C, h, :], kT[:, h, :], identb, R=(qb_, cmb), W=(b5,))
                for h in range(8):
                    kb.tr(b6[0:C, h, :], vT[:, h, :], identb, R=(qb_, cmb), W=(b6,))
                P0, P0T = Pm[0], PTm[0]
                kb.stt("dve", tmp[0:C, :, 0:C], b3[0:C, :, 0:C], -1.0, gTs[0:C, :, 0:C], ALU.mult, ALU.mult, R=(b3, gTs), W=(tmp,))
                kb.tt("dve", P0[0:C, :, 0:C], tmp[0:C, :, 0:C], b1[0:C, :, 0:C], ALU.mult, R=(tmp, b1), W=(P0,))
                kb.stt("dve", tmp[0:C, :, 0:C], b3[0:C, :, 0:C], -1.0, gLs[0:C, :, 0:C], ALU.mult, ALU.mult, R=(b3, gLs), W=(tmp,))
                kb.tt("dve", P0T[0:C, :, 0:C], tmp[0:C, :, 0:C], bch(beta), ALU.mult, R=(tmp, GBR), W=(P0T,))
                kb.tt("dve", QKg[0:C, :, 0:C], b4[0:C, :, 0:C], gTi[0:C, :, 0:C], ALU.mult, R=(b4, gTi), W=(QKg,))
                kb.tt("dve", bv[0:C, :, :], b6[0:C, :, :], bch(beta, 128), ALU.mult, R=(b6, GBR), W=(bv,))
                kb.tt("dve", bgk[0:C, :, :], b5[0:C, :, :], bch(bge[0:C, :], 128), ALU.mult, R=(b5, bge), W=(bgk,))
                kb.tt("dve", kd[0:C, :, :], b5[0:C, :, :], bch(dk[0:C, :], 128), ALU.mult, R=(b5, dk), W=(kd,))
                kb.tt("dve", qgT[:, :, 0:C], qT, eG128[:, :, 0:C], ALU.mult, R=(qb_, eG128), W=(qgT,))
                kb.tt("dve", Rm[0:C, :, 0:C], P0[0:C, :, 0:C], bcm(identf[0:C, 0:C]), ALU.add, R=(P0, cm), W=(Rm,))
                Pc, PTc = P0, P0T
                for sstep in range(nsteps):
                    Pn_, PTn_ = Pm[(sstep + 1) % 2], PTm[(sstep + 1) % 2]
                    lastst = sstep == nsteps - 1
                    for h in range(8):
                        kb.mm(b1[0:C, h, 0:C], Pc[0:C, h, 0:C], PTc[0:C, h, 0:C], True, True, R=(Pc, PTc), W=(b1,))
                    kb.cp("act", PTn_[0:C, :, 0:C], b1[0:C, :, 0:C], R=(b1,), W=(PTn_,))
                    if not lastst:
                        for h in range(8):
                            kb.mm(b0[0:C, h, 0:C], PTc[0:C, h, 0:C], Pc[0:C, h, 0:C], True, True, R=(Pc, PTc), W=(b0,))
                        kb.cp("dve", Pn_[0:C, :, 0:C], b0[0:C, :, 0:C], R=(b0,), W=(Pn_,))
                    for h in range(8):
                        kb.mm(b2[0:C, h, 0:C], PTn_[0:C, h, 0:C], Rm[0:C, h, 0:C], True, True, R=(PTn_, Rm), W=(b2,))
                    kb.tt("dve", Rm[0:C, :, 0:C], Rm[0:C, :, 0:C], b2[0:C, :, 0:C], ALU.add, R=(Rm, b2), W=(Rm,))
                    Pc, PTc = Pn_, PTn_
                for half in range(2):
                    for hh in range(4):
                        h = half * 4 + hh
                        kb.mm(b3u[0:C, 2 * hh:2 * hh + 2, :].rearrange("p a b -> p (a b)"), Rm[0:C, h, 0:C], bv[0:C, h, :], True, True,
                              R=(Rm, bv), W=(b3u,))
                    kb.cp("act", Usb[0:C, half * 4:half * 4 + 4, :].rearrange("p a b -> p (a b)"),
                          b3u[0:C, :, :].rearrange("p a b -> p (a b)"), R=(b3u,), W=(Usb,), partial=(half > 0))
                for h in range(8):
                    kb.mm(b4[:, h, 0:C], bgk[0:C, h, :], Rm[0:C, h, 0:C], True, True, R=(bgk, Rm), W=(b4,))
                kb.cp("act", wTb[:, :, 0:C], b4[:, :, 0:C], R=(b4,), W=(wTb,))
                for half in range(2):
                    hs = slice(half * 4, half * 4 + 4)
                    for hh in range(4):
                        h = half * 4 + hh
                        kb.mm(b7[0:C, hh, :], wTb[:, h, 0:C], Sb[:, h, :], True, True, R=(wTb, Sb), W=(b7,))
                    kb.tt("dve", vnb[0:C, hs, :], Usb[0:C, hs, :], b7[0:C, :, :], ALU.subtract, R=(Usb, b7), W=(vnb,), partial=(half > 0))
                    for hh in range(4):
                        h = half * 4 + hh
                        kb.mm(b7[0:C, hh, :], qgT[:, h, 0:C], Sb[:, h, :], True, False, R=(qgT, Sb), W=(b7,))
                        kb.mm(b7[0:C, hh, :], QKg[0:C, h, 0:C], vnb[0:C, h, :], False, True, R=(QKg, vnb), W=(b7,))
                    kb.cp("act", osb[0:C, hs, :], b7[0:C, :, :], R=(b7,), W=(osb,), partial=(half > 0))
                    for hh in range(4):
                        h = half * 4 + hh
                        kb.mm(b7[:, hh, :], kd[0:C, h, :], vnb[0:C, h, :], True, True, R=(kd, vnb), W=(b7,))
                    kb.tt("dve", Stmp[:, hs, :], S[:, hs, :], egl[:, hs].unsqueeze(2).to_broadcast([128, 4, 128]), ALU.mult,
                          R=(S, egl), W=(Stmp,), partial=(half > 0))
                    kb.tt("dve", S[:, hs, :], Stmp[:, hs, :], b7[:, :, :], ALU.add, R=(Stmp, b7), W=(S,), partial=True)
                    kb.cp("act", Sb[:, hs, :], S[:, hs, :], R=(S,), W=(Sb,), partial=True)
                kb.act(osq[0:C, :, :], osb[0:C, :, :], AF.Square, R=(osb,), W=(osq,))
                kb.op("dve", lambda e: e.tensor_reduce(out=ors[0:C, :], in_=osq[0:C, :, :], axis=AX.X, op=ALU.add), R=(osq,), W=(ors,))
                kb.rsqrt(ors[0:C, :], ors[0:C, :], 1.0 / 128, EPS, R=(ors,), W=(ors,))
                kb.tt("dve", osb[0:C, :, :], osb[0:C, :, :], bch(ors[0:C, :], 128), ALU.mult, R=(osb, ors), W=(osb,))
                kb.tt("dve", osb[0:C, :, :], osb[0:C, :, :], COUT[0:C, :].unsqueeze(1).to_broadcast([C, 8, 128]), ALU.mult,
                      R=(osb, COUT), W=(osb,))
                kb.tt("dve", onb[0:C, :, :], osb[0:C, :, :], cz_[0:C, :].rearrange("p (h d) -> p h d", d=128), ALU.mult,
                      R=(osb, cz_), W=(onb,))
                for h in range(8):
                    kb.tr(b5[:, h, 0:C], onb[0:C, h, :], identb[0:C, 0:C], R=(onb, cmb), W=(b5,))
                if sidx is None:
                    oc_ = octs[(ci // 8) % 2]
                    kb.cp("act", oc_[:, :, co:co + C], b5[:, :, 0:C], R=(b5,), W=(oc_,), partial=(ci % 8 > 0))
                    if ci % 8 == 7:
                        tb = t0 + C - 512
                        kb.dma("sp", BR[16:24, :, tb:tb + 512].rearrange("c p t -> p c t"), oc_[:], R=(oc_,), W=(BR,), partial=True)
                else:
                    oc_ = octs[nch % 2]
                    kb.cp("act", oc_[:, :, 0:C], b5[:, :, 0:C], R=(b5,), W=(oc_,))
                    kb.dma("sp", BR[16:24, :, t0:t0 + C].rearrange("c p t -> p c t"), oc_[:, :, 0:C], R=(oc_,), W=(BR,), partial=True)
            kb.dma("sp", gdn_out[l, si].rearrange("h k v -> k h v"), S[:], R=(S,), W=(gdn_out,), partial=True)
        kb.barrier()
        st.close()

    def phase3(l):
        st = ExitStack()
        last_layer = l == DEPTH - 1
        xT = kb.sb(st, "xT3", [128, KC, 512], F32)
        gt_ = [kb.sb(st, "gt3", [128, 4, 512], BF16) for _ in range(2)]
        mixed = kb.sb(st, "mixed", [128, KC, 512], BF16)
        h2T = mixed
        sqb = [kb.sb(st, "sqb3", [128, 512], F32) for _ in range(2)]
        rstd = kb.sb(st, "rstd3", [128, 512], F32)
        tmpx = [kb.sb(st, "tmpx3", [128, 512], F32) for _ in range(2)]
        actT = kb.sb(st, "actT", [128, max(FC, 24), 512], BF16)
        br = actT
        NW = 11 if FC % 11 == 0 else 4
        wk = [kb.sb(st, "wk", [128, max(KC, 8, NW), 512], BF16) for _ in range(3)]
        FW = kb.sb(st, "fw", [128, 2 * FC, 3], F32)
        for tap in range(3):
            kb.dma("sp", FW[:, :, tap], ffn_conv[l, tap, :].rearrange("(k p) -> p k", p=128), W=(FW,), partial=True, slow=True)
        HF = kb.sb(st, "hf", [128, 2 * FC, 2], F32)
        kb.memset("pool", HF[:], 0.0, W=(HF,))
        SHF = kb.sb(st, "shf", [128, NSS, 2 * FC, 2], F32)
        for s in range(NSS):
            for j in range(2 * FC):
                kb.dma("sp", SHF[:, s, j, :], st_fconv[l, s, :, j * 128:(j + 1) * 128].rearrange("t p -> p t"), W=(SHF,),
                       partial=True, slow=True)
        eg = [kb.sb(st, "eg", [128, 514], F32) for _ in range(2)]
        eu = [kb.sb(st, "eu", [128, 514], F32) for _ in range(2)]
        cg = [kb.sb(st, "cg", [128, 512], F32) for _ in range(2)]
        cu = [kb.sb(st, "cu", [128, 512], F32) for _ in range(2)]
        yst = [kb.sb(st, "yst", [128, D], F32) for _ in range(1)] if last_layer else []
        MACC = [cg[0], cg[1], cu[0], cu[1]]
        mtmp = [eg[0], eu[0]]
        pB = [kb.ps(st, "pB3", [128, 512], F32) for _ in range(2)]
        pY = [kb.ps(st, "pY3", [128, 512], F32) for _ in range(2)]
        pS = kb.ps(st, "pS3", [128, 512], F32)
        pU = [kb.ps(st, "pU3", [128, 512], F32) for _ in range(3)]
        cn = dict(b=0, y=0, u=0, t=0)

        def nxt(k, lst):
            cn[k] += 1
            return lst[cn[k] % len(lst)]

        for gi, (tok0, ntok, segs) in enumerate(groups):
            ntile = ntok // 128
            kb.dma("sp", xT[:, :, 0:ntok], XT[:, :, tok0:tok0 + ntok].rearrange("k p t -> p k t"), R=(XT,), W=(xT,))
            kb.dma("sp", br[:, 0:24, 0:ntok], BR[:, :, tok0:tok0 + ntok].rearrange("c p t -> p c t"), R=(BR,), W=(br,))
            nbw = D // 512
            loads = []
            for cb in range(nbw):
                for n in range(3):
                    loads.append(lambda b, n=n, cb=cb: (b[:, 0:8, :], w_branch[l, n, :, cb * 512:(cb + 1) * 512].rearrange("(k p) c -> p k c", p=128)))
            if gi == 0:
                sc3 = [T(WS3[l][k]) for k in range(NB3)]
            md = ("first" if gi == 0 else "scr") if USE_SCR else "cast"
            o3 = 0
            ws = WStream(kb, wk, loads, sc=sc3[o3:o3 + len(loads)], mode=md)
            o3 += len(loads)
            for cb in range(nbw):
                for n in range(3):
                    w = ws.get(cb * 3 + n)
                    g4 = nxt("b", gt_)
                    c0 = n * KC + cb * 4
                    kb.dma("sp", g4[:, :, 0:ntok], GT[c0:c0 + 4, :, tok0:tok0 + ntok].rearrange("c p t -> p c t"), R=(GT,), W=(g4,))
                    for q in range(4):
                        dc = cb * 4 + q
                        p = nxt("b", pB)
                        for kc in range(8):
                            kb.mm(p[:, 0:ntok], w[:, kc, q * 128:(q + 1) * 128], br[:, 8 * n + kc, 0:ntok], kc == 0, kc == 7,
                                  R=(w, br), W=(p,))
                        acc = MACC[q]
                        if n == 0:
                            kb.tt("dve", acc[:, 0:ntok], p[:, 0:ntok], g4[:, q, 0:ntok], ALU.mult, R=(p, g4), W=(acc,))
                        else:
                            mt = nxt("t", mtmp)
                            kb.tt("dve", mt[:, 0:ntok], p[:, 0:ntok], g4[:, q, 0:ntok], ALU.mult, R=(p, g4), W=(mt,))
                            if n == 1:
                                kb.tt("pool", acc[:, 0:ntok], acc[:, 0:ntok], mt[:, 0:ntok], ALU.add, R=(acc, mt), W=(acc,))
                            else:
                                kb.tt("pool", mixed[:, dc, 0:ntok], acc[:, 0:ntok], mt[:, 0:ntok], ALU.add, R=(acc, mt), W=(mixed,),
                                      partial=True)
            loads = [(lambda b, cb=cb: (b[:, 0:KC, :], w_out[l, :, cb * 512:(cb + 1) * 512].rearrange("(k p) c -> p k c", p=128)))
                     for cb in range(nbw)]
            ws = WStream(kb, wk, loads, sc=sc3[o3:o3 + len(loads)], mode=md)
            o3 += len(loads)
            for cb in range(nbw):
                w = ws.get(cb)
                for q in range(4):
                    dc = cb * 4 + q
                    p = nxt("y", pY)
                    for kc in range(KC):
                        kb.mm(p[:, 0:ntok], w[:, kc, q * 128:(q + 1) * 128], mixed[:, kc, 0:ntok], kc == 0, kc == KC - 1,
                              R=(w, mixed), W=(p,))
                    for (c0, n_, sq_) in segs:
                        kb.stt("dve", xT[:, dc, c0:c0 + n_], p[:, c0:c0 + n_], MOD[l][:, 2 * KC + dc, sq_:sq_ + 1], xT[:, dc, c0:c0 + n_],
                               ALU.mult, ALU.add, R=(p, MOD[l], xT), W=(xT,), partial=True)
            for kc in range(KC):
                sq = sqb[kc % 2]
                kb.act(sq[:, 0:ntok], xT[:, kc, 0:ntok], AF.Square, R=(xT,), W=(sq,))
                kb.mm(pS[:, 0:ntok], onesf, sq[:, 0:ntok], kc == 0, kc == KC - 1, R=(cm, sq), W=(pS,))
            kb.rsqrt(rstd[:, 0:ntok], pS[:, 0:ntok], 1.0 / D, EPS, R=(pS,), W=(rstd,))
            for kc in range(KC):
                tx = tmpx[kc % 2]
                kb.tt("dve", tx[:, 0:ntok], xT[:, kc, 0:ntok], rstd[:, 0:ntok], ALU.mult, R=(xT, rstd), W=(tx,))
                for (c0, n_, sq_) in segs:
                    kb.ts("pool", h2T[:, kc, c0:c0 + n_], tx[:, c0:c0 + n_], A2[l][:, kc, sq_:sq_ + 1], ALU.mult,
                          R=(tx, A2[l], MOD[l]), W=(h2T,), s2=MOD[l][:, 3 * KC + kc, sq_:sq_ + 1], op1=ALU.add, partial=True)
            nfb = DFF // 512
            loads = []
            for fb in range(nfb):
                for half in range(2):
                    c0 = half * DFF + fb * 512
                    loads.append(lambda b, c0=c0: (b[:, 0:KC, :], w_up[l, :, c0:c0 + 512].rearrange("(k p) c -> p k c", p=128)))
            ws = WStream(kb, wk, loads, sc=sc3[o3:o3 + len(loads)], mode=md)
            o3 += len(loads)
            for fb in range(nfb):
                wg = ws.get(2 * fb)
                wu = ws.get(2 * fb + 1, keep=1)
                for q in range(4):
                    fc = fb * 4 + q
                    res = []
                    for half, w in ((0, wg), (1, wu)):
                        ch = half * FC + fc
                        p = nxt("u", pU)
                        for kc in range(KC):
                            kb.mm(p[:, 0:ntok], w[:, kc, q * 128:(q + 1) * 128], h2T[:, kc, 0:ntok], kc == 0, kc == KC - 1,
                                  R=(w, h2T), W=(p,))
                        e_ = (eg if half == 0 else eu)[fc % 2]
                        c_ = (cg if half == 0 else cu)[fc % 2]
                        for (c0, n_, sq_) in segs:
                            if sq_ == 0:
                                hist, hT_ = HF[:, ch, :], HF
                            else:
                                hist, hT_ = SHF[:, sq_ - 1, ch, :], SHF
                            kb.cp("pool", e_[:, 0:2], hist, R=(hT_,), W=(e_,))
                            kb.cp("act", e_[:, 2:2 + n_], p[:, c0:c0 + n_], R=(p,), W=(e_,), partial=True)
                            kb.ts("dve", c_[:, c0:c0 + n_], e_[:, 0:n_], FW[:, ch, 0:1], ALU.mult, R=(e_, FW), W=(c_,), partial=(c0 > 0))
                            for tap in (1, 2):
                                kb.stt("dve", c_[:, c0:c0 + n_], e_[:, tap:tap + n_], FW[:, ch, tap:tap + 1], c_[:, c0:c0 + n_],
                                       ALU.mult, ALU.add, R=(e_, FW, c_), W=(c_,), partial=True)
                            kb.cp("pool", hist, e_[:, n_:n_ + 2], R=(e_,), W=(hT_,), partial=True)
                        res.append(c_)
                    kb.act(res[0][:, 0:ntok], res[0][:, 0:ntok], AF.Silu, R=(res[0],), W=(res[0],))
                    kb.tt("dve", actT[:, fc, 0:ntok], res[0][:, 0:ntok], res[1][:, 0:ntok], ALU.mult, R=(res[0], res[1]), W=(actT,),
                          partial=True)
            nparts = FC // NW
            loads = []
            for cb in range(nbw):
                for pp in range(nparts):
                    loads.append(lambda b, cb=cb, pp=pp: (b[:, 0:NW, :], w_down[l, pp * NW * 128:(pp + 1) * NW * 128,
                                                                     cb * 512:(cb + 1) * 512].rearrange("(k p) c -> p k c", p=128)))
            ws = WStream(kb, wk, loads, sc=sc3[o3:o3 + len(loads)], mode=md)
            o3 += len(loads)
            assert o3 == NB3
            for cb in range(nbw):
                wl = [None] * nparts
                ps4 = []
                for q in range(4):
                    ps4.append(nxt("y", pY) if q < 2 else nxt("b", pB))
                for pp in range(nparts):
                    w = ws.get(cb * nparts + pp)
                    for q in range(4):
                        for k in range(NW):
                            fc = pp * NW + k
                            kb.mm(ps4[q][:, 0:ntok], w[:, k, q * 128:(q + 1) * 128], actT[:, fc, 0:ntok], fc == 0, fc == FC - 1,
                                  R=(w, actT), W=(ps4[q],))
                for q in range(4):
                    dc = cb * 4 + q
                    for (c0, n_, sq_) in segs:
                        kb.stt("dve", xT[:, dc, c0:c0 + n_], ps4[q][:, c0:c0 + n_], MOD[l][:, 5 * KC + dc, sq_:sq_ + 1],
                               xT[:, dc, c0:c0 + n_], ALU.mult, ALU.add, R=(ps4[q], MOD[l], xT), W=(xT,), partial=True)
            if not last_layer:
                kb.dma("sp", XT[:, :, tok0:tok0 + ntok].rearrange("k p t -> p k t"), xT[:, :, 0:ntok], R=(xT,), W=(XT,), partial=True)
            else:
                for ti in range(ntile):
                    ys = nxt("t", yst)
                    for k4 in range(KC // 4):
                        p = nxt("u", pU)
                        p3 = p[:, :].rearrange("p (a b) -> p a b", b=128)
                        for j in range(4):
                            kc = k4 * 4 + j
                            kb.tr(p3[:, j, :], xT[:, kc, ti * 128:(ti + 1) * 128], identf, R=(xT, cm), W=(p,))
                        kb.cp("act" if k4 % 2 else "dve", ys[:, k4 * 512:(k4 + 1) * 512], p[:, :], R=(p,), W=(ys,), partial=(k4 > 0))
                    r0 = tok0 + ti * 128
                    kb.dma("sp", y_all[r0:r0 + 128, :], ys[:], R=(ys,), W=(y_all,), partial=True)
        for sq_ in range(NSEQ):
            for ch in range(2 * FC):
                src = HF[:, ch, :] if sq_ == 0 else SHF[:, sq_ - 1, ch, :]
                kb.dma("sp", fconv_out[l, sq_, :, ch * 128:(ch + 1) * 128].rearrange("t p -> p t"), src,
                       R=(HF if sq_ == 0 else SHF,), W=(fconv_out,), partial=True, slow=True)
        kb.barrier()
        st.close()


    stop_after = cfg.get("stop_after")
    phase0()
    for l in range(DEPTH):
        phase1(l)
        if stop_after == "p1":
            break
        phase2_cache(l)
        phase2a(l)
        if stop_after == "p2a":
            break
        phase2b(l)
        if stop_after == "p2b":
            break
        phase2c(l)
        if stop_after == "p2c":
            break
        phase3(l)
    kb.barrier()
    top.close()
    es.close()
    return nc, kb


def host_consts(cfg):
    SEQ, PAST, DSEQ, NSS = cfg["SEQ"], cfg["PAST"], cfg["DSEQ"], cfg["NSS"]
    pos = np.concatenate([np.arange(SEQ)] + [PAST + np.arange(DSEQ)] * NSS).astype(np.float32)

    def table(half):
        inv = np.power(np.float32(10000.0), -np.arange(half, dtype=np.float32) / np.float32(half)).astype(np.float32)
        ang = (pos[:, None] * inv[None, :]).astype(np.float32)
        return np.concatenate([np.cos(ang.astype(np.float64)), np.sin(ang.astype(np.float64))], axis=1).astype(np.float32)

    i = np.arange(128)
    cm = np.zeros((128, 640), np.float32)
    cm[:, 0:128] = np.eye(128)
    cm[:, 128:256] = 1.0
    cm[:, 256:384] = (i[None, :] >= i[:, None])
    cm[:, 384:512] = (i[None, :] > i[:, None])
    cm[:, 512:640] = (i[:, None] > i[None, :])
    return dict(rope64=table(64), rope32=table(32), cmat=cm)


_WNAMES = ("w_ada", "b_ada", "norm_mix", "w_in", "a_q_norm", "a_k_norm", "a_lambda", "a_subln", "b_q_norm", "b_k_norm",
           "c_conv", "c_a_log", "c_dt_bias", "c_out_norm", "w_branch", "w_out", "norm_ffn", "w_up", "ffn_conv", "w_down")


def host_inmaps(cfg, inp, ncores):
    NSS, DEPTH, PAST = cfg["NSS"], cfg["DEPTH"], cfg["PAST"]
    f = lambda a: np.ascontiguousarray(np.asarray(a, dtype=np.float32))
    cst = host_consts(cfg)
    shared = {k: f(inp[k]) for k in _WNAMES}
    shared.update(cst)
    nb = inp["x_prompt"].shape[0]
    maps = []
    for c in range(ncores):
        b = c % nb
        s0 = NSS * c
        m = dict(shared)
        m["x_all"] = f(np.concatenate([inp["x_prompt"][b], np.asarray(inp["x_sample"][s0:s0 + NSS]).reshape(-1, cfg["D"])], 0))
        m["c_all"] = f(np.concatenate([inp["c_prompt"][b:b + 1], inp["c_sample"][s0:s0 + NSS]], 0))
        m["cache_ka"] = f(np.asarray(inp["cache_diff_k"][:, s0:s0 + NSS]).reshape(DEPTH, NSS, PAST, 1024))
        m["cache_va"] = f(np.asarray(inp["cache_diff_v"][:, s0:s0 + NSS]).reshape(DEPTH, NSS, PAST, 1024))
        m["cache_kb"] = f(np.asarray(inp["cache_dsa_k"][:, s0:s0 + NSS]).reshape(DEPTH, NSS, PAST, 256))
        m["cache_vb"] = f(np.asarray(inp["cache_dsa_v"][:, s0:s0 + NSS]).reshape(DEPTH, NSS, PAST, 256))
        m["cache_ki"] = f(np.asarray(inp["cache_dsa_kidx"][:, s0:s0 + NSS]))
        m["st_cconv"] = f(np.asarray(inp["state_gdn_conv"][:, s0:s0 + NSS]))
        m["st_gdn"] = f(np.asarray(inp["state_gdn"][:, s0:s0 + NSS]))
        m["st_fconv"] = f(np.asarray(inp["state_ffn_conv"][:, s0:s0 + NSS]))
        maps.append(m)
    return maps


def host_gather(cfg, res, ncores, nb):
    SEQ, NSS, DSEQ, DEPTH, D, DFF = cfg["SEQ"], cfg["NSS"], cfg["DSEQ"], cfg["DEPTH"], cfg["D"], cfg["DFF"]
    R = [r for r in res]

    def P(name, fn):
        return np.stack([fn(R[b][name]) for b in range(nb)], axis=0)

    def S(name, fn):
        return np.concatenate([fn(R[c][name]) for c in range(ncores)], axis=0)

    y_p = P("y_all", lambda a: a[:SEQ])
    y_s = S("y_all", lambda a: a[SEQ:].reshape(NSS, DSEQ, D))
    outs = [y_p, y_s]

    def tok(name, shp_tail):
        p = np.stack([R[b][name][:, :SEQ] for b in range(nb)], axis=1).reshape((DEPTH, nb, SEQ) + shp_tail)
        s = np.concatenate([R[c][name][:, SEQ:].reshape((DEPTH, NSS, DSEQ) + shp_tail) for c in range(ncores)], axis=1)
        return p, s

    def seqo(name):
        p = np.stack([R[b][name][:, 0] for b in range(nb)], axis=1)
        s = np.concatenate([R[c][name][:, 1:] for c in range(ncores)], axis=1)
        return p, s

    ka = tok("ka_out", (4, 2, 128))
    va = tok("va_out", (4, 256))
    kb_ = tok("kb_out", (2, 128))
    vb = tok("vb_out", (2, 128))
    ki = tok("ki_out", (64,))
    cc = seqo("cconv_out")
    gs = seqo("gdn_out")
    fc = seqo("fconv_out")
    allp = [ka, va, kb_, vb, ki, cc, gs, fc]
    outs += [a[0] for a in allp] + [a[1] for a in allp]
    return tuple(np.ascontiguousarray(o, dtype=np.float32) for o in outs)


_CACHE = {}


def run_cfg(cfg, inputs, ncores=8, trace=False):
    key = tuple(sorted((k, str(v)) for k, v in cfg.items()))
    if key not in _CACHE:
        _CACHE[key] = build(cfg)[0]
    nc = _CACHE[key]
    maps = host_inmaps(cfg, inputs, ncores)
    res = run_bass_kernel_spmd(nc, maps, core_ids=list(range(ncores)), trace=trace)
    return host_gather(cfg, res.results, ncores, inputs["x_prompt"].shape[0]), res


def kernel(**inputs):
    outs, _ = run_cfg(CFG_FULL, inputs, 8)
    return outs
```
